# Optimizing a Trainium2 kernel written in Bass

```python
import math
import jax
import jax.numpy as jnp
from jax import lax
import numpy as np

D_MODEL = 1024
BATCH = 8
SEQ = 4096
DEPTH = 1

HEAD_DIM = 64
SWA_Q_HEADS = 8
SWA_KV_HEADS = 2
SWA_GROUP = SWA_Q_HEADS // SWA_KV_HEADS
SB_HEADS = 8
WINDOW = 128
BLOCK = 128
ROPE_THETA = 10000.0
D_FF = 2816
CONV_WIDTH = 3
N_BRANCHES = 2
LN_EPS = 1e-5
DEEPNORM_ALPHA = (2.0 * DEPTH) ** 0.25
DEEPNORM_BETA = (8.0 * DEPTH) ** -0.25

SWA_Q_WIDTH = SWA_Q_HEADS * HEAD_DIM
SWA_KV_WIDTH = SWA_KV_HEADS * HEAD_DIM
SB_WIDTH = SB_HEADS * HEAD_DIM
GATE_WIDTH = N_BRANCHES * D_MODEL
IN_WIDTHS = [SWA_Q_WIDTH, SWA_KV_WIDTH, SWA_KV_WIDTH, SB_WIDTH, SB_WIDTH, SB_WIDTH, GATE_WIDTH]
IN_SPLITS = [int(v) for v in np.cumsum(IN_WIDTHS)[:-1]]
IN_TOTAL = int(sum(IN_WIDTHS))

kernel_name = 'hybrid_swa_sink_stickbreaking_convffn_deepnorm'


def layer_norm(x, g, b):
    xf = x.astype(jnp.float32)
    mu = jnp.mean(xf, axis=-1, keepdims=True)
    xc = xf - mu
    var = jnp.mean(xc * xc, axis=-1, keepdims=True)
    y = xc * lax.rsqrt(var + LN_EPS) * g.astype(jnp.float32) + b.astype(jnp.float32)
    return y.astype(x.dtype)


def rotary_tables(positions):
    inv_freq = 1.0 / (ROPE_THETA ** (jnp.arange(0, HEAD_DIM, 2, dtype=jnp.float32) / HEAD_DIM))
    ang = positions.astype(jnp.float32)[..., None] * inv_freq
    return jnp.cos(ang)[:, :, None, :], jnp.sin(ang)[:, :, None, :]


def apply_rope(t, cos, sin):
    tf = t.astype(jnp.float32)
    t1, t2 = jnp.split(tf, 2, axis=-1)
    out = jnp.concatenate([t1 * cos - t2 * sin, t2 * cos + t1 * sin], axis=-1)
    return out.astype(t.dtype)


def sliding_window_sink_attention(q, k, v, sinks):
    B, T, _, _ = q.shape
    n = T // BLOCK
    qb = q.reshape(B, n, BLOCK, SWA_KV_HEADS, SWA_GROUP, HEAD_DIM)
    pad = ((0, 0), (BLOCK, 0), (0, 0), (0, 0))
    kb = jnp.pad(k, pad).reshape(B, n + 1, BLOCK, SWA_KV_HEADS, HEAD_DIM)
    vb = jnp.pad(v, pad).reshape(B, n + 1, BLOCK, SWA_KV_HEADS, HEAD_DIM)
    kwin = jnp.concatenate([kb[:, :-1], kb[:, 1:]], axis=2)
    vwin = jnp.concatenate([vb[:, :-1], vb[:, 1:]], axis=2)
    scale = HEAD_DIM ** -0.5
    s = jnp.einsum('bnqhgd,bnshd->bnhgqs', qb, kwin).astype(jnp.float32) * scale
    blk = jnp.arange(n)[:, None, None]
    qloc = jnp.arange(BLOCK)[None, :, None] + BLOCK
    kloc = jnp.arange(2 * BLOCK)[None, None, :]
    rel = qloc - kloc
    kglob = blk * BLOCK + kloc - BLOCK
    mask = (rel >= 0) & (rel < WINDOW) & (kglob >= 0)
    s = jnp.where(mask[None, :, None, None], s, -jnp.inf)
    sink = sinks.astype(jnp.float32).reshape(SWA_KV_HEADS, SWA_GROUP)[None, None, :, :, None, None]
    m = jnp.maximum(jnp.max(s, axis=-1, keepdims=True), sink)
    p = jnp.exp(s - m)
    denom = jnp.sum(p, axis=-1, keepdims=True) + jnp.exp(sink - m)
    probs = (p / denom).astype(v.dtype)
    out = jnp.einsum('bnhgqs,bnshd->bnqhgd', probs, vwin)
    return out.reshape(B, T, SWA_Q_WIDTH)


def stick_breaking_attention(q, k, v):
    B, T, H, D = q.shape
    n = T // BLOCK
    qb = q.reshape(B, n, BLOCK, H, D).transpose(1, 0, 2, 3, 4)
    kpos = jnp.arange(T)
    scale = D ** -0.5

    def block(args):
        qi, i = args
        z = jnp.einsum('bqhd,bshd->bhqs', qi, k).astype(jnp.float32) * scale
        qpos = i * BLOCK + jnp.arange(BLOCK)
        mask = kpos[None, :] < qpos[:, None]
        log_beta = jax.nn.log_sigmoid(z)
        log_one_minus = jnp.where(mask, jax.nn.log_sigmoid(-z), 0.0)
        after = lax.cumsum(log_one_minus, axis=3, reverse=True) - log_one_minus
        a = jnp.where(mask, jnp.exp(log_beta + after), 0.0).astype(v.dtype)
        return jnp.einsum('bhqs,bshd->bqhd', a, v)

    out = lax.map(block, (qb, jnp.arange(n)))
    return out.transpose(1, 0, 2, 3, 4).reshape(B, T, H * D)


def causal_depthwise_conv(u, w, b):
    K, C = w.shape
    y = lax.conv_general_dilated(
        u, w[:, None, :].astype(u.dtype), window_strides=(1,), padding=[(K - 1, 0)],
        dimension_numbers=('NWC', 'WIO', 'NWC'), feature_group_count=C)
    return y + b


def conv_ffn(x, w_up, conv_w, conv_b, w_down):
    a = causal_depthwise_conv(x @ w_up, conv_w, conv_b)
    gate, up = jnp.split(a, 2, axis=-1)
    return (jax.nn.silu(gate) * up) @ w_down


def setup_inputs(seed: int = 0) -> dict:
    key = jax.random.key(seed)
    ks = jax.random.split(key, 20)
    f32 = jnp.float32

    def dense(k, fan_in, fan_out, scale=1.0):
        return jax.random.normal(k, (DEPTH, fan_in, fan_out), f32) * (fan_in ** -0.5) * scale

    x = jax.random.normal(ks[0], (BATCH, SEQ, D_MODEL), f32)
    offset = jax.random.randint(ks[1], (BATCH, 1), 0, 1024, dtype=jnp.int32)
    positions = (offset + jnp.arange(SEQ, dtype=jnp.int32)[None, :]).astype(jnp.int32)
    w_in = jnp.concatenate([
        dense(ks[2], D_MODEL, SWA_Q_WIDTH),
        dense(ks[3], D_MODEL, SWA_KV_WIDTH),
        dense(ks[4], D_MODEL, SWA_KV_WIDTH, DEEPNORM_BETA),
        dense(ks[5], D_MODEL, SB_WIDTH),
        dense(ks[6], D_MODEL, SB_WIDTH),
        dense(ks[7], D_MODEL, SB_WIDTH, DEEPNORM_BETA),
        dense(ks[8], D_MODEL, GATE_WIDTH),
    ], axis=-1)
    b_gate = 0.02 * jax.random.normal(ks[9], (DEPTH, GATE_WIDTH), f32)
    sinks = 0.5 * jax.random.normal(ks[10], (DEPTH, SWA_Q_HEADS), f32)
    w_branch_a = dense(ks[11], SWA_Q_WIDTH, D_MODEL, DEEPNORM_BETA)
    w_branch_b = dense(ks[12], SB_WIDTH, D_MODEL, DEEPNORM_BETA)
    w_out = dense(ks[13], D_MODEL, D_MODEL, DEEPNORM_BETA)
    ln1_g = 1.0 + 0.02 * jax.random.normal(ks[14], (DEPTH, D_MODEL), f32)
    ln1_b = 0.02 * jax.random.normal(ks[15], (DEPTH, D_MODEL), f32)
    w_up = dense(ks[16], D_MODEL, 2 * D_FF, DEEPNORM_BETA)
    kc1, kc2 = jax.random.split(ks[17])
    conv_w = jax.random.normal(kc1, (DEPTH, CONV_WIDTH, 2 * D_FF), f32) * (CONV_WIDTH ** -0.5)
    conv_b = 0.02 * jax.random.normal(kc2, (DEPTH, 2 * D_FF), f32)
    w_down = dense(ks[18], D_FF, D_MODEL, DEEPNORM_BETA)
    kl1, kl2 = jax.random.split(ks[19])
    ln2_g = 1.0 + 0.02 * jax.random.normal(kl1, (DEPTH, D_MODEL), f32)
    ln2_b = 0.02 * jax.random.normal(kl2, (DEPTH, D_MODEL), f32)
    return {'x': x, 'positions': positions, 'w_in': w_in, 'b_gate': b_gate, 'sinks': sinks,
            'w_branch_a': w_branch_a, 'w_branch_b': w_branch_b, 'w_out': w_out,
            'ln1_g': ln1_g, 'ln1_b': ln1_b, 'w_up': w_up, 'conv_w': conv_w, 'conv_b': conv_b,
            'w_down': w_down, 'ln2_g': ln2_g, 'ln2_b': ln2_b}


def reference(x, positions, w_in, b_gate, sinks, w_branch_a, w_branch_b, w_out,
              ln1_g, ln1_b, w_up, conv_w, conv_b, w_down, ln2_g, ln2_b):
    B, T, _ = x.shape
    cos, sin = rotary_tables(positions)
    for l in range(DEPTH):
        proj = x @ w_in[l]
        qa, ka, va, qb, kb, vb, gl = jnp.split(proj, IN_SPLITS, axis=-1)
        qa = apply_rope(qa.reshape(B, T, SWA_Q_HEADS, HEAD_DIM), cos, sin)
        ka = apply_rope(ka.reshape(B, T, SWA_KV_HEADS, HEAD_DIM), cos, sin)
        va = va.reshape(B, T, SWA_KV_HEADS, HEAD_DIM)
        ya = sliding_window_sink_attention(qa, ka, va, sinks[l])
        yb = stick_breaking_attention(
            qb.reshape(B, T, SB_HEADS, HEAD_DIM),
            kb.reshape(B, T, SB_HEADS, HEAD_DIM),
            vb.reshape(B, T, SB_HEADS, HEAD_DIM))
        gates = jax.nn.sigmoid(gl + b_gate[l]).reshape(B, T, N_BRANCHES, D_MODEL)
        h = gates[:, :, 0, :] * (ya @ w_branch_a[l]) + gates[:, :, 1, :] * (yb @ w_branch_b[l])
        x = layer_norm(DEEPNORM_ALPHA * x + h @ w_out[l], ln1_g[l], ln1_b[l])
        f = conv_ffn(x, w_up[l], conv_w[l], conv_b[l], w_down[l])
        x = layer_norm(DEEPNORM_ALPHA * x + f, ln2_g[l], ln2_b[l])
    return x
```

```python
import numpy as np
import concourse.bass as bass
import concourse.mybir as mybir
from concourse.bass_utils import run_bass_kernel_spmd

F32 = mybir.dt.float32
BF16 = mybir.dt.bfloat16
I32 = mybir.dt.int32
AF = mybir.ActivationFunctionType
ALU = mybir.AluOpType

D = 1024
DFF = 2816
NJ = DFF // 128
HD = 64
ALPHA = float(2.0 ** 0.25)
EPS = 1e-5
PI = float(np.pi)
INV2PI = float(np.float32(1.0 / (2.0 * np.pi)))
MAGIC = 12582912.0
CW1 = 6.28125
CW2 = float(2.0 * np.pi - 6.28125)
SBUF_BASE = 16384

C_ID, C_ROT, C_TRI, C_ONE, C_MSB, C_MSWA, C_O64, NCB = 0, 128, 256, 384, 512, 768, 1280, 1344


class Buf:
    __slots__ = ("name", "w", "r", "psum")

    def __init__(self, name, psum=False):
        self.name = name
        self.w = []
        self.r = []
        self.psum = psum


class Prog:
    ENGS = ("pe", "act", "dve", "pool", "sp")

    def __init__(self):
        self.ops = []
        self.bar_start = 0

    def op(self, eng, fn, reads=(), writes=(), dma=None, multi=False):
        i = len(self.ops)
        deps = set()
        for b in reads:
            deps.update(b.w)
            if b.psum:
                deps.update(r for r in b.r if self.ops[r]["eng"] != eng)
        for b in writes:
            if not multi:
                deps.update(b.w)
            deps.update(b.r)
        for b in reads:
            b.r.append(i)
        for b in writes:
            if multi and not b.r:
                b.w = b.w + [i]
            else:
                b.w = [i]
            b.r = []
        deps.discard(i)
        last = {}
        keep = set()
        for d in deps:
            od = self.ops[d]
            if od["dma"] is not None:
                keep.add(d)
            elif last.get(od["eng"], -1) < d:
                last[od["eng"]] = d
        keep.update(last.values())
        self.ops.append(dict(eng=eng, fn=fn, deps=keep, dma=dma))
        return i

    def barrier(self):
        last = {}
        deps = set()
        for idx in range(self.bar_start, len(self.ops)):
            o = self.ops[idx]
            if o["fn"] is None:
                continue
            if o["dma"] is not None:
                deps.add(idx)
            else:
                last[o["eng"]] = idx
        deps.update(last.values())
        for e in self.ENGS:
            self.ops.append(dict(eng=e, fn=None, deps=set(deps), dma=None))
        self.bar_start = len(self.ops)

    def emit(self, nc, stack):
        ops = self.ops
        need = set()
        for o in ops:
            for d in o["deps"]:
                od = ops[d]
                if o["eng"] == "pe" and od["eng"] == "pe" and od["dma"] is None:
                    continue
                need.add(d)
        esem = {e: stack.enter_context(nc.semaphore("s_" + e)) for e in self.ENGS}
        dsem = {}
        tick = {e: 0 for e in self.ENGS}
        dcnt = {}
        sig = {}
        for i, o in enumerate(ops):
            if i not in need or o["fn"] is None:
                continue
            if o["dma"] is not None:
                k = o["dma"]
                if k not in dsem:
                    dsem[k] = stack.enter_context(nc.semaphore("d_%d" % len(dsem)))
                    dcnt[k] = 0
                dcnt[k] += 16
                sig[i] = (dsem[k], dcnt[k], 16)
            else:
                tick[o["eng"]] += 1
                sig[i] = (esem[o["eng"]], tick[o["eng"]], 1)
        per = {e: [] for e in self.ENGS}
        for i, o in enumerate(ops):
            per[o["eng"]].append(i)
        self.stats = dict(ticks=dict(tick), nsem=len(dsem) + len(esem), nops=len(ops),
                          dmax=max(dcnt.values()) if dcnt else 0)
        block = stack.enter_context(nc.Block())

        def run(eng, ename):
            waited = {}
            for i in per[ename]:
                o = ops[i]
                for d in sorted(o["deps"]):
                    if d not in sig:
                        continue
                    od = ops[d]
                    if ename == "pe" and od["eng"] == "pe" and od["dma"] is None:
                        continue
                    sem, val, _ = sig[d]
                    key = id(sem)
                    if waited.get(key, 0) >= val:
                        continue
                    eng.wait_ge(sem, val)
                    waited[key] = val
                if o["fn"] is not None:
                    ins = o["fn"](eng)
                    if i in sig:
                        ins.then_inc(sig[i][0], sig[i][2])

        @block.tensor
        def _(e):
            run(e, "pe")

        @block.scalar
        def _(e):
            run(e, "act")

        @block.vector
        def _(e):
            run(e, "dve")

        @block.gpsimd
        def _(e):
            run(e, "pool")

        @block.sync
        def _(e):
            run(e, "sp")


class Mem:
    def __init__(self, nc):
        self.nc = nc
        self.n = 0

    def at(self, off, shape, dt):
        self.n += 1
        return self.nc.alloc_sbuf_tensor_at("t%d" % self.n, list(shape), dt, offset=SBUF_BASE + off).ap()


def build(T=4096, debug=False):
    NB = T // 128
    NT = T // 512
    nc = bass.Bass("TRN2", target_bir_lowering=False)
    P = Prog()
    M = Mem(nc)

    def din(name, shape, dt=F32):
        return nc.dram_tensor(name, list(shape), dt, kind="ExternalInput").ap()

    xT_h = din("xT", [D, T])
    x_h = din("x", [T, D])
    pos_h = din("pos", [128, T], I32)
    cb_h = din("cb", [128, NCB])
    cf_h = din("cf", [128, 8])
    wA_h = din("wA", [128, 8, 768])
    wB_h = din("wB", [4, 128, 8, 384])
    wg_h = din("wg", [8, 128, 8, 256])
    wa_h = din("wa", [128, 4, 1024])
    wb_h = din("wb", [128, 4, 1024])
    wo_h = din("wo", [128, 8, 1024])
    wup_h = din("wup", [NJ, 128, 8, 256])
    wdn_h = din("wdn", [128, NJ, 1024])
    cp_h = din("cp", [128, 2 * NJ, 4])
    bg_h = din("bg", [128, 16])
    sk_h = din("sk", [128, 4])
    ln_h = din("ln", [4, 128, D])
    out_h = nc.dram_tensor("out", [T, D], F32, kind="ExternalOutput").ap()
    x1_s = nc.dram_tensor("x1s", [T, D], F32, kind="Internal").ap()
    rt_s = nc.dram_tensor("rts", [2, 128, T], F32, kind="Internal").ap()
    if debug:
        dbgA = nc.dram_tensor("dbgA", [128, 4, T], F32, kind="ExternalOutput").ap()
        dbgB = nc.dram_tensor("dbgB", [128, 4, T], F32, kind="ExternalOutput").ap()

    import contextlib
    stack = contextlib.ExitStack()
    with stack:
        ps = stack.enter_context(nc.psum_tensor([128, 8, 512], F32))
        PS = [Buf("ps%d" % b, psum=True) for b in range(8)]

        o = 0
        cb = M.at(o, [128, NCB], BF16); o += 2 * NCB
        CB = Buf("cb")
        cf = M.at(o, [128, 8], F32); o += 32
        CF = Buf("cf")
        xT = M.at(o, [128, 8, T], BF16); o += 16 * T
        XT = [Buf("xT%d" % t) for t in range(NT)]
        LOC2 = o
        yAT = M.at(o, [128, 4, T], BF16); o += 8 * T
        YA = [Buf("yA%d" % t) for t in range(NT)]
        yBT = M.at(o, [128, 4, T], BF16); o += 8 * T
        YB = [Buf("yB%d" % t) for t in range(NT)]
        LOC = o
        assert LOC % 32 == 0

        def dma(eng, out, in_, reads=(), writes=(), key=None, multi=True):
            if key is None:
                key = ("w", writes[0].name) if writes else ("r", reads[0].name)
            return P.op(eng, lambda e, out=out, in_=in_: e.dma_start(out=out, in_=in_),
                        reads=reads, writes=writes, dma=key, multi=multi)

        def ms(ap, val, writes):
            return P.op("pool", lambda e, ap=ap, val=val: e.memset(ap, val), writes=writes)

        def mm(out, lhsT, rhs, start, stop, reads, writes):
            return P.op("pe", lambda e, out=out, lhsT=lhsT, rhs=rhs, start=start, stop=stop:
                        e.matmul(out, lhsT=lhsT, rhs=rhs, start=start, stop=stop, skip_group_check=True),
                        reads=reads, writes=writes)

        def act(out, in_, func, reads, writes, bias=0.0, scale=1.0):
            return P.op("act", lambda e, out=out, in_=in_, func=func, bias=bias, scale=scale:
                        e.activation(out=out, in_=in_, func=func, bias=bias, scale=scale),
                        reads=reads, writes=writes)

        def tt(eng, out, in0, in1, op, reads, writes):
            return P.op(eng, lambda e, out=out, in0=in0, in1=in1, op=op:
                        e.tensor_tensor(out=out, in0=in0, in1=in1, op=op), reads=reads, writes=writes)

        def ts(eng, out, in0, s1, s2, op0, op1, reads, writes):
            return P.op(eng, lambda e, out=out, in0=in0, s1=s1, s2=s2, op0=op0, op1=op1:
                        e.tensor_scalar(out=out, in0=in0, scalar1=s1, scalar2=s2, op0=op0, op1=op1),
                        reads=reads, writes=writes)

        def stt(out, in0, scalar, in1, op0, op1, reads, writes):
            return P.op("dve", lambda e, out=out, in0=in0, scalar=scalar, in1=in1, op0=op0, op1=op1:
                        e.scalar_tensor_tensor(out=out, in0=in0, scalar=scalar, in1=in1, op0=op0, op1=op1),
                        reads=reads, writes=writes)

        def cp(eng, out, in_, reads, writes):
            if eng == "act":
                return act(out, in_, AF.Copy, reads, writes)
            return P.op(eng, lambda e, out=out, in_=in_: e.tensor_copy(out=out, in_=in_),
                        reads=reads, writes=writes)

        dma("pool", cb[:, :], cb_h[:, :], writes=[CB])
        dma("sp", cf[:, :], cf_h[:, :], writes=[CF])
        xv = xT_h.rearrange("(c p) t -> p c t", p=128)
        for t in range(NT):
            dma("pool", xT[:, :, t * 512:(t + 1) * 512], xv[:, :, t * 512:(t + 1) * 512], writes=[XT[t]])

        o = LOC
        QTA = M.at(o, [128, 2, T], BF16); o += 4 * T
        QA = [[Buf("qa%d_%d" % (c, t)) for t in range(NT)] for c in range(4)]
        KTA = M.at(o, [128, T], BF16); o += 2 * T
        KA = [Buf("ka%d" % t) for t in range(NT)]
        VA = M.at(o, [128, NB, 128], BF16); o += 2 * T
        VAb = [Buf("va%d" % t) for t in range(NT)]
        sk = M.at(o, [128, 4], F32); o += 32
        SK = Buf("sk")
        npi = M.at(o, [128, 1], F32); o += 32
        NPI = Buf("npi")
        A2 = o
        wA = M.at(o, [128, 8, 768], BF16); o += 8 * 768 * 2
        WA = Buf("wA")
        ang = M.at(o, [128, 512], F32); o += 2048
        tmpa = M.at(o, [128, 512], F32); o += 2048
        tmpb = M.at(o, [128, 512], F32); o += 2048
        posi = tmpb.bitcast(I32)
        TMPB = Buf("tmpb")
        cosTs = [M.at(o + i * 2048, [128, 512], F32) for i in range(2)]; o += 4096
        sinTs = [M.at(o + i * 2048, [128, 512], F32) for i in range(2)]; o += 4096
        COSs = [Buf("cos%d" % i) for i in range(2)]
        SINs = [Buf("sin%d" % i) for i in range(2)]
        ANG, TMPA = Buf("ang"), Buf("tmpa")
        POSI = TMPB
        q32 = [M.at(o + i * 2048, [128, 512], F32) for i in range(2)]; o += 4096
        qb = [M.at(o + i * 1024, [128, 512], BF16) for i in range(2)]; o += 2048
        t2 = [M.at(o + i * 2048, [128, 512], F32) for i in range(2)]; o += 4096
        Q32 = [Buf("q32_%d" % i) for i in range(2)]
        QB = [Buf("qb_%d" % i) for i in range(2)]
        T2 = [Buf("t2_%d" % i) for i in range(2)]
        assert o <= 212992, o

        for k in range(0, 8, 2):
            dma("pool", wA[:, k:k + 2, :], wA_h[:, k:k + 2, :], writes=[WA])
        dma("sp", sk[:, :], sk_h[:, :], writes=[SK])
        ms(npi[:, :], -PI, [NPI])
        act(sk[:, :], sk[:, :], AF.Exp, [SK], [SK])

        ctr = [0]

        def rope_proj(wcols, dst, dstbuf, t, cosT, sinT, COS, SIN):
            i = ctr[0] % 2
            ctr[0] += 1
            b0, b1 = 2 * i, 2 * i + 1
            for k in range(8):
                mm(ps[:, b0, :], wA[:, k, wcols], xT[:, k, t * 512:(t + 1) * 512], k == 0, k == 7,
                   [WA, XT[t]], [PS[b0]])
            act(q32[i][:, :], ps[:, b0, :], AF.Copy, [PS[b0]], [Q32[i]])
            cp("act", qb[i][:, :], ps[:, b0, :], [PS[b0]], [QB[i]])
            mm(ps[:, b1, :], cb[:, C_ROT:C_ROT + 128], qb[i][:, :], True, True, [CB, QB[i]], [PS[b1]])
            tt("dve", q32[i][:, :], q32[i][:, :], cosT[:, :], ALU.mult, [Q32[i], COS], [Q32[i]])
            tt("dve", t2[i][:, :], ps[:, b1, :], sinT[:, :], ALU.mult, [PS[b1], SIN], [T2[i]])
            tt("dve", dst, q32[i][:, :], t2[i][:, :], ALU.add, [Q32[i], T2[i]], [dstbuf])

        Pt = [M.at(o + i * 1024, [128, 2, 256], BF16) for i in range(2)]; o += 2048
        PT = [Buf("pt%d" % i) for i in range(2)]
        dn = [M.at(o + i * 512, [128, 128], F32) for i in range(2)]; o += 1024
        DN = [Buf("dn%d" % i) for i in range(2)]
        assert o <= 212992, o
        msw = cb[:, C_MSWA:C_MSWA + 512].rearrange("p (h u) -> p h u", h=2)
        o64 = cb[:, C_O64:C_O64 + 64]
        for half in range(2):
            for t in range(NT):
                sl = slice(t * 512, (t + 1) * 512)
                cosT, sinT, COS, SIN = cosTs[t % 2], sinTs[t % 2], COSs[t % 2], SINs[t % 2]
                if half == 0:
                    dma("sp", posi[:, :], pos_h[:, sl], writes=[POSI])
                    cp("dve", ang[:, :], posi[:, :], [POSI], [ANG])
                    ts("dve", ang[:, :], ang[:, :], cf[:, 0:1], None, ALU.mult, ALU.bypass, [ANG, CF], [ANG])
                    def sin_of(base_ap, BASE, dst, DST):
                        ts("dve", tmpa[:, :], base_ap, INV2PI, MAGIC, ALU.mult, ALU.add, [BASE], [TMPA])
                        ts("dve", tmpa[:, :], tmpa[:, :], MAGIC, None, ALU.subtract, ALU.bypass, [TMPA], [TMPA])
                        stt(tmpb[:, :], tmpa[:, :], -CW1, base_ap, ALU.mult, ALU.add, [TMPA, BASE], [TMPB])
                        stt(tmpb[:, :], tmpa[:, :], -CW2, tmpb[:, :], ALU.mult, ALU.add, [TMPA, TMPB], [TMPB])
                        ts("dve", tmpb[:, :], tmpb[:, :], 3.141592, -3.141592, ALU.min, ALU.max, [TMPB], [TMPB])
                        act(dst, tmpb[:, :], AF.Sin, [TMPB], [DST])
                    sin_of(ang[:, :], ANG, sinT[:, :], SIN)
                    ts("dve", ang[:, :], ang[:, :], 0.5 * PI, None, ALU.add, ALU.bypass, [ANG], [ANG])
                    sin_of(ang[:, :], ANG, cosT[:, :], COS)
                    dma("sp", rt_s[0, :, sl], sinT[:, :], reads=[SIN], key=("rts", 0, t % 2))
                    dma("sp", rt_s[1, :, sl], cosT[:, :], reads=[COS], key=("rts", 1, t % 2))
                else:
                    dma("sp", sinT[:, :], rt_s[0, :, sl], writes=[SIN], key=("rtl", 0, t % 2))
                    dma("sp", cosT[:, :], rt_s[1, :, sl], writes=[COS], key=("rtl", 1, t % 2))
                if half == 0:
                    rope_proj(slice(512, 640), KTA[:, sl], KA[t], t, cosT, sinT, COS, SIN)
                for c in (2 * half, 2 * half + 1):
                    rope_proj(slice(c * 128, (c + 1) * 128), QTA[:, c % 2, sl], QA[c][t], t, cosT, sinT, COS, SIN)
                for j in range(4 if half == 0 else 0):
                    blk = t * 4 + j
                    for k in range(8):
                        mm(ps[:, 4, j * 128:(j + 1) * 128], xT[:, k, blk * 128:(blk + 1) * 128], wA[:, k, 640:768],
                           k == 0, k == 7, [XT[t], WA], [PS[4]])
                if half == 0:
                    cp("act", VA[:, t * 4:(t + 1) * 4, :], ps[:, 4, :].rearrange("p (j d) -> p j d", j=4), [PS[4]], [VAb[t]])

            P.barrier()
            it = 0
            for c in (2 * half, 2 * half + 1):
                for kb in range(NB):
                    N = 256 if kb < NB - 1 else 128
                    i = it % 2
                    it += 1
                    b0 = 2 * i
                    t0 = kb * 128
                    tq = [QA[c][(t0) // 512]] + ([QA[c][(t0 + 128) // 512]] if N == 256 else [])
                    for h in range(2):
                        r = slice(64 * h, 64 * h + 64)
                        mm(ps[:, b0 + h, 0:N], KTA[r, t0:t0 + 128], QTA[r, c % 2, t0:t0 + N], True, True,
                           [KA[kb // 4]] + tq, [PS[b0 + h]])
                    act(Pt[i][:, :, 0:N], ps[:, b0:b0 + 2, 0:N], AF.Exp, [PS[b0], PS[b0 + 1]], [PT[i]], scale=0.125)
                    tt("dve", Pt[i][:, :, 0:N], Pt[i][:, :, 0:N], msw[:, :, 0:N], ALU.mult, [PT[i], CB], [PT[i]])
                    ob, db = 4 + kb % 2, 6 + kb % 2
                    ob1, db1 = 4 + (kb + 1) % 2, 6 + (kb + 1) % 2
                    for h in range(2):
                        r = slice(64 * h, 64 * h + 64)
                        vsl = VA[:, kb, r]
                        mm(ps[r, ob, 0:128], vsl, Pt[i][:, h, 0:128], kb == 0, True, [VAb[kb // 4], PT[i]], [PS[ob]])
                        mm(ps[r, db, 0:128], o64, Pt[i][:, h, 0:128], kb == 0, True, [CB, PT[i]], [PS[db]])
                    j = kb % 2
                    ts("dve", dn[j][:, :], ps[:, db, 0:128], sk[:, c:c + 1], None, ALU.add, ALU.bypass,
                       [PS[db], SK], [DN[j]])
                    P.op("dve", lambda e, a=dn[j][:, :]: e.reciprocal(out=a, in_=a), reads=[DN[j]], writes=[DN[j]])
                    tt("dve", yAT[:, c, t0:t0 + 128], ps[:, ob, 0:128], dn[j][:, :], ALU.mult,
                       [PS[ob], DN[j]], [YA[kb // 4]])
                    if N == 256:
                        for h in range(2):
                            r = slice(64 * h, 64 * h + 64)
                            vsl = VA[:, kb, r]
                            mm(ps[r, ob1, 0:128], vsl, Pt[i][:, h, 128:256], True, False, [VAb[kb // 4], PT[i]], [PS[ob1]])
                            mm(ps[r, db1, 0:128], o64, Pt[i][:, h, 128:256], True, False, [CB, PT[i]], [PS[db1]])
            P.barrier()

        o = LOC
        QTBs = [M.at(o + i * 2 * T, [128, T], BF16) for i in range(2)]; o += 4 * T
        KTBs = [M.at(o + i * 2 * T, [128, T], BF16) for i in range(2)]; o += 4 * T
        VBs = [M.at(o + i * 2 * T, [128, NB, 128], BF16) for i in range(2)]; o += 4 * T
        QBbs = [[Buf("qB%d_%d" % (i, t)) for t in range(NT)] for i in range(2)]
        KBbs = [[Buf("kB%d_%d" % (i, t)) for t in range(NT)] for i in range(2)]
        VBbs = [[Buf("vB%d_%d" % (i, t)) for t in range(NT)] for i in range(2)]
        wB = [M.at(o + i * 6144, [128, 8, 384], BF16) for i in range(2)]; o += 12288
        WB = [Buf("wB%d" % i) for i in range(2)]
        NE = 1
        Et = [M.at(o + i * 4096, [128, 2, 512], F32) for i in range(NE)]; o += 4096 * NE
        ET = [Buf("E%d" % i) for i in range(NE)]
        NL = 2
        Lt = [M.at(o + i * 2048, [128, 2, 512], BF16) for i in range(NL)]; o += 2048 * NL
        LT = [Buf("L%d" % i) for i in range(NL)]
        Ac = [M.at(o + i * 2048, [128, 2, 512], BF16) for i in range(2)]; o += 4096
        AC = [Buf("Ac%d" % i) for i in range(2)]
        At = [M.at(o + i * 2048, [128, 2, 512], BF16) for i in range(2)]; o += 4096
        AT = [Buf("At%d" % i) for i in range(2)]
        assert o <= 212992, o
        msb2 = cb[:, C_MSB:C_MSB + 256].rearrange("p (h u) -> p h u", h=2)
        ntri = cb[:, C_TRI:C_TRI + 128]
        none_ = cb[:, C_ONE:C_ONE + 128]
        PJ = 7

        def wB_load(hp):
            for k in range(0, 8, 4):
                dma("pool", wB[hp % 2][:, k:k + 4, :], wB_h[hp, :, k:k + 4, :], writes=[WB[hp % 2]])

        def proj_ops(hp):
            w, W, si = wB[hp % 2], WB[hp % 2], hp % 2
            ops_ = []
            for t in range(NT):
                sl = slice(t * 512, (t + 1) * 512)
                for which, dst, dbuf in ((0, QTBs[si], QBbs[si]), (1, KTBs[si], KBbs[si])):
                    for k in range(8):
                        ops_.append(lambda k=k, which=which, sl=sl, t=t: mm(
                            ps[:, PJ, :], w[:, k, which * 128:(which + 1) * 128], xT[:, k, sl], k == 0, k == 7,
                            [W, XT[t]], [PS[PJ]]))
                    ops_.append(lambda dst=dst, dbuf=dbuf, sl=sl, t=t: cp("dve", dst[:, sl], ps[:, PJ, :], [PS[PJ]], [dbuf[t]]))
                for j in range(4):
                    blk = t * 4 + j
                    for k in range(8):
                        ops_.append(lambda k=k, j=j, blk=blk, t=t: mm(
                            ps[:, PJ, j * 128:(j + 1) * 128], xT[:, k, blk * 128:(blk + 1) * 128], w[:, k, 256:384],
                            k == 0, k == 7, [XT[t], W], [PS[PJ]]))
                ops_.append(lambda t=t: cp("dve", VBs[si][:, t * 4:(t + 1) * 4, :],
                                          ps[:, PJ, :].rearrange("p (j d) -> p j d", j=4), [PS[PJ]], [VBbs[si][t]]))
            return ops_

        wB_load(0)
        for f_ in proj_ops(0):
            f_()
        for hp in range(4):
            QTB, KTB, VB = QTBs[hp % 2], KTBs[hp % 2], VBs[hp % 2]
            QBb, KBb, VBb = QBbs[hp % 2], KBbs[hp % 2], VBbs[hp % 2]
            nxt = []
            if hp < 3:
                wB_load(hp + 1)
                nxt = proj_ops(hp + 1)

            steps = []
            for qt in range(NT):
                acur = None
                kbs = list(range(4 * qt + 3, -1, -1))
                for si, kb in enumerate(kbs):
                    off = max(0, kb - 4 * qt) * 128
                    st_ = dict(qt=qt, kb=kb, off=off, N=512 - off, t0=qt * 512 + off, diag=kb >= 4 * qt,
                               first=si == 0, last=kb == 0, ob=6, acur=acur)
                    if kb != 0:
                        st_["anew"] = 0 if si == 0 else 1 - acur
                        acur = st_["anew"]
                    steps.append(st_)
            for n_, st_ in enumerate(steps):
                st_["zp"] = 2 * (n_ % 3)
                st_["e"] = n_ % NE
                st_["l"] = n_ % NL
                st_["a"] = n_ % 2

            def s_z(S):
                off, zp, kb, t0, N = S["off"], S["zp"], S["kb"], S["t0"], S["N"]
                for h in range(2):
                    r = slice(64 * h, 64 * h + 64)
                    mm(ps[:, zp + h, off:512], KTB[r, kb * 128:(kb + 1) * 128], QTB[r, t0:t0 + N], True, False,
                       [KBb[kb // 4], QBb[S["qt"]]], [PS[zp + h]])

            def s_el(S):
                off, zp, e_i, l_i = S["off"], S["zp"], S["e"], S["l"]
                Zb = [PS[zp], PS[zp + 1]]
                act(Et[e_i][:, :, off:512], ps[:, zp:zp + 2, off:512], AF.Exp, Zb, [ET[e_i]], scale=0.125)
                act(Lt[l_i][:, :, off:512], Et[e_i][:, :, off:512], AF.Ln, [ET[e_i]], [LT[l_i]], bias=1.0)
                if S["diag"]:
                    tt("dve", Lt[l_i][:, :, off:off + 128], Lt[l_i][:, :, off:off + 128], msb2, ALU.mult,
                       [LT[l_i], CB], [LT[l_i]])
                if not S["last"]:
                    anew, acur = S["anew"], S["acur"]
                    if S["first"]:
                        ms(Ac[anew][:, :, :], 0.0, [AC[anew]])
                        cp("dve", Ac[anew][:, :, off:512], Lt[l_i][:, :, off:512], [LT[l_i]], [AC[anew]])
                    else:
                        if off > 0:
                            ms(Ac[anew][:, :, :], 0.0, [AC[anew]])
                        tt("dve", Ac[anew][:, :, off:512], Ac[acur][:, :, off:512], Lt[l_i][:, :, off:512], ALU.add,
                           [AC[acur], LT[l_i]], [AC[anew]])

            def s_tri(S):
                off, zp, l_i = S["off"], S["zp"], S["l"]
                for h in range(2):
                    mm(ps[:, zp + h, off:512], ntri, Lt[l_i][:, h, off:512], False, S["first"],
                       [CB, LT[l_i]], [PS[zp + h]])
                    if not S["first"]:
                        mm(ps[:, zp + h, off:512], none_, Ac[S["acur"]][:, h, off:512], False, True,
                           [CB, AC[S["acur"]]], [PS[zp + h]])

            def s_a(S):
                off, zp, a_i = S["off"], S["zp"], S["a"]
                Zb = [PS[zp], PS[zp + 1]]
                act(At[a_i][:, :, off:512], ps[:, zp:zp + 2, off:512], AF.Exp, Zb, [AT[a_i]], scale=0.125)
                if S["diag"]:
                    tt("dve", At[a_i][:, :, off:off + 128], At[a_i][:, :, off:off + 128], msb2, ALU.mult,
                       [AT[a_i], CB], [AT[a_i]])

            def s_av(S):
                off, a_i, ob, kb = S["off"], S["a"], S["ob"], S["kb"]
                for h in range(2):
                    r = slice(64 * h, 64 * h + 64)
                    mm(ps[r, ob, off:512], VB[:, kb, r], At[a_i][:, h, off:512], S["first"], S["last"],
                       [VBb[kb // 4], AT[a_i]], [PS[ob]])
                if S["last"]:
                    qt = S["qt"]
                    cp("dve", yBT[:, hp, qt * 512:(qt + 1) * 512], ps[:, ob, :], [PS[ob]], [YB[qt]])

            ns = len(steps)
            for tau in range(ns + 2):
                rounds_left = max(1, ns - 4 - tau)
                take = -(-len(nxt) // rounds_left) if tau < ns - 4 else len(nxt)
                for f_ in nxt[:take]:
                    f_()
                nxt = nxt[take:]
                if tau < ns:
                    s_z(steps[tau])
                if 1 <= tau <= ns:
                    s_tri(steps[tau - 1])
                if 2 <= tau:
                    s_av(steps[tau - 2])
                if tau < ns:
                    s_el(steps[tau])
                if 1 <= tau <= ns:
                    s_a(steps[tau - 1])
        P.barrier()

        if debug:
            o = LOC
            dtmp = M.at(o, [128, 4, 512], F32)
            DT = Buf("dtmp")
            for t in range(NT):
                sl = slice(t * 512, (t + 1) * 512)
                cp("dve", dtmp[:, :, :], yAT[:, :, sl], [YA[t]], [DT])
                dma("sp", dbgA[:, :, sl], dtmp[:, :, :], reads=[DT], key=("dbg", 0))
                cp("dve", dtmp[:, :, :], yBT[:, :, sl], [YB[t]], [DT])
                dma("sp", dbgB[:, :, sl], dtmp[:, :, :], reads=[DT], key=("dbg", 0))
            P.barrier()

        o = LOC
        wa = M.at(o, [128, 4, 1024], BF16); o += 8192
        wb = M.at(o, [128, 4, 1024], BF16); o += 8192
        wo = M.at(o, [128, 8, 1024], BF16); o += 16384
        WAa, WBb, WO = Buf("wa"), Buf("wb"), Buf("wo")
        g1 = M.at(o, [128, D], F32); o += 4096
        b1 = M.at(o, [128, D], F32); o += 4096
        LN1 = Buf("ln1")
        bg = M.at(o, [128, 16], F32); o += 64
        BG = Buf("bg")
        epsb = M.at(o, [128, 1], F32); o += 32
        mhalf = M.at(o, [128, 1], F32); o += 32
        CC = Buf("cc")
        wgc = [M.at(o + i * 4096, [128, 8, 256], BF16) for i in range(2)]; o += 8192
        WG = [Buf("wg%d" % i) for i in range(2)]
        gt = [M.at(o, [128, 2, 512], F32)] * 2; o += 4096
        GT = [Buf("gt")] * 2
        hT = M.at(o, [128, 8, 512], BF16); o += 8192
        HT = [Buf("hT%d" % c) for c in range(8)]
        xin = [M.at(o + i * 4096, [128, D], F32) for i in range(2)]; o += 8192
        XIN = [Buf("xin%d" % i) for i in range(2)]
        x1b = [M.at(o + i * 2048, [128, D], BF16) for i in range(4)]; o += 8192
        X1B = [Buf("x1b%d" % i) for i in range(4)]
        st = [M.at(o + i * 64, [128, 2, 6], F32) for i in range(2)]; o += 128
        mv = [M.at(o + i * 32, [128, 2], F32) for i in range(2)]; o += 64
        rs = [M.at(o + i * 32, [128, 1], F32) for i in range(2)]; o += 64
        nbt = [M.at(o + i * 32, [128, 1], F32) for i in range(2)]; o += 64
        NBT = [Buf("nb%d" % i) for i in range(2)]
        STt = [Buf("st%d" % i) for i in range(2)]
        MV = [Buf("mv%d" % i) for i in range(2)]
        RS = [Buf("rs%d" % i) for i in range(2)]
        assert o <= 212992, o

        for k in range(0, 4, 2):
            dma("pool", wa[:, k:k + 2, :], wa_h[:, k:k + 2, :], writes=[WAa])
            dma("pool", wb[:, k:k + 2, :], wb_h[:, k:k + 2, :], writes=[WBb])
        for k in range(0, 8, 2):
            dma("pool", wo[:, k:k + 2, :], wo_h[:, k:k + 2, :], writes=[WO])
        dma("sp", g1[:, :], ln_h[0], writes=[LN1])
        dma("sp", b1[:, :], ln_h[1], writes=[LN1])
        dma("sp", bg[:, :], bg_h[:, :], writes=[BG])
        ms(epsb[:, :], EPS, [CC])
        ms(mhalf[:, :], -0.5, [CC])

        def layer_norm(yv, Y, gam, bet, LNB, out_ap, OUT, st_, mv_, rs_, ST_, MV_, RS_, eps_, mh_, CC_, nb_, NB_):
            for hh in range(2):
                P.op("dve", lambda e, o_=st_[:, hh, :], i_=yv[:, hh * 512:(hh + 1) * 512]: e.bn_stats(out=o_, in_=i_),
                     reads=[Y], writes=[ST_], multi=True)
            P.op("dve", lambda e, o_=mv_[:, :], i_=st_[:, :, :]: e.bn_aggr(out=o_, in_=i_), reads=[ST_], writes=[MV_])
            ts("pool", rs_[:, :], mv_[:, 1:2], eps_[:, 0:1], None, ALU.add, ALU.bypass, [MV_, CC_], [RS_])
            tt("pool", rs_[:, :], rs_[:, :], mh_[:, :], ALU.pow, [RS_, CC_], [RS_])
            stt(nb_[:, :], mv_[:, 0:1], -1.0, rs_[:, :], ALU.mult, ALU.mult, [MV_, RS_], [NB_])
            act(yv, yv, AF.Identity, [Y, RS_, NB_], [Y], bias=nb_[:, 0:1], scale=rs_[:, 0:1])
            tt("dve", yv, yv, gam, ALU.mult, [Y, LNB], [Y])
            tt("dve", out_ap, yv, bet, ALU.add, [Y, LNB], [OUT])

        def c1_R(t, j):
            rb = 4 + 2 * (j % 2)
            for hf in range(2):
                for c in range(8):
                    mm(ps[:, rb + hf, :], hT[:, c, j * 128:(j + 1) * 128], wo[:, c, hf * 512:(hf + 1) * 512],
                       c == 0, c == 7, [HT[c], WO], [PS[rb + hf]])

        def ln_multi(items, gam, bet, LNB):
            for it_ in items:
                for hh in range(2):
                    P.op("dve", lambda e, o_=it_["st"][:, hh, :], i_=it_["yv"][:, hh * 512:(hh + 1) * 512]:
                         e.bn_stats(out=o_, in_=i_), reads=[it_["Y"]], writes=[it_["ST"]], multi=True)
            for it_ in items:
                P.op("dve", lambda e, o_=it_["mv"][:, :], i_=it_["st"][:, :, :]: e.bn_aggr(out=o_, in_=i_),
                     reads=[it_["ST"]], writes=[it_["MV"]])
            for it_ in items:
                ts("pool", it_["rs"][:, :], it_["mv"][:, 1:2], epsb[:, 0:1], None, ALU.add, ALU.bypass,
                   [it_["MV"], CC], [it_["RS"]])
            for it_ in items:
                tt("pool", it_["rs"][:, :], it_["rs"][:, :], mhalf[:, :], ALU.pow, [it_["RS"], CC], [it_["RS"]])
            for it_ in items:
                stt(it_["nb"][:, :], it_["mv"][:, 0:1], -1.0, it_["rs"][:, :], ALU.mult, ALU.mult,
                    [it_["MV"], it_["RS"]], [it_["NB"]])
            for it_ in items:
                act(it_["yv"], it_["yv"], AF.Identity, [it_["Y"], it_["RS"], it_["NB"]], [it_["Y"]],
                    bias=it_["nb"][:, 0:1], scale=it_["rs"][:, 0:1])
            for it_ in items:
                tt("dve", it_["yv"], it_["yv"], gam, ALU.mult, [it_["Y"], LNB], [it_["Y"]])
            for it_ in items:
                tt("dve", it_["yv"], it_["yv"], bet, ALU.add, [it_["Y"], LNB], [it_["Y"]])

        def c1_chain2(t, js):
            items = []
            for j in js:
                i = j % 2
                blk = t * 4 + j
                dma("sp", xin[i][:, :], x_h[blk * 128:(blk + 1) * 128, :], writes=[XIN[i]], key=("xin", i))
            for j in js:
                i = j % 2
                rb = 4 + 2 * (j % 2)
                stt(xin[i][:, :], xin[i][:, :], ALPHA, ps[:, rb:rb + 2, :].rearrange("p a n -> p (a n)"),
                    ALU.mult, ALU.add, [XIN[i], PS[rb], PS[rb + 1]], [XIN[i]])
                items.append(dict(yv=xin[i][:, :], Y=XIN[i], st=st[i], mv=mv[i], rs=rs[i], nb=nbt[i],
                                  ST=STt[i], MV=MV[i], RS=RS[i], NB=NBT[i]))
            ln_multi(items, g1[:, :], b1[:, :], LN1)
            for j in js:
                i = j % 2
                blk = t * 4 + j
                dma("sp", x1_s[blk * 128:(blk + 1) * 128, :], xin[i][:, :], reads=[XIN[i]], key=("x1s", i))
                cp("act", x1b[j][:, :], xin[i][:, :], [XIN[i]], [X1B[j]])

        def c1_T(t, j):
            blk = t * 4 + j
            tb = 4 + j
            tpb = ps[:, tb, :].bitcast(BF16)
            for c in range(8):
                P.op("pe", lambda e, o_=tpb[:, c * 128:(c + 1) * 128], i_=x1b[j][:, c * 128:(c + 1) * 128],
                     id_=cb[:, C_ID:C_ID + 128]: e.transpose(out=o_, in_=i_, identity=id_),
                     reads=[X1B[j], CB], writes=[PS[tb]])
            cp("dve", xT[:, :, blk * 128:(blk + 1) * 128], tpb.rearrange("p (c n) -> p c n", c=8),
               [PS[tb]], [XT[t]])

        gi = 0
        bi = 0

        def wg_load(n):
            if n < NT * 8:
                dma("pool", wgc[n % 2][:, :, :], wg_h[n % 8, :, :, :], writes=[WG[n % 2]], key=("wg", n % 2))
        wg_load(0)
        for t in range(NT):
            sl = slice(t * 512, (t + 1) * 512)
            for c in range(8):
                wi = gi % 2
                gi2 = gi % 2
                gi += 1
                gb = 0
                for ab in range(2):
                    for k in range(8):
                        mm(ps[:, gb + ab, :], wgc[wi][:, k, ab * 128:(ab + 1) * 128], xT[:, k, sl], k == 0, k == 7,
                           [WG[wi], XT[t]], [PS[gb + ab]])
                    act(gt[gi2][:, ab, :], ps[:, gb + ab, :], AF.Sigmoid, [PS[gb + ab], BG], [GT[gi2]],
                        bias=bg[:, ab * 8 + c:ab * 8 + c + 1])
                wg_load(gi)
                for k in range(4):
                    mm(ps[:, 2, :], wa[:, k, c * 128:(c + 1) * 128], yAT[:, k, sl], k == 0, k == 3, [WAa, YA[t]], [PS[2]])
                for k in range(4):
                    mm(ps[:, 3, :], wb[:, k, c * 128:(c + 1) * 128], yBT[:, k, sl], k == 0, k == 3, [WBb, YB[t]], [PS[3]])
                tt("dve", gt[gi2][:, 0, :], gt[gi2][:, 0, :], ps[:, 2, :], ALU.mult, [GT[gi2], PS[2]], [GT[gi2]])
                tt("dve", gt[gi2][:, 1, :], gt[gi2][:, 1, :], ps[:, 3, :], ALU.mult, [GT[gi2], PS[3]], [GT[gi2]])
                if t > 0 and c % 2 == 1:
                    c1_T(t - 1, c // 2)
                tt("dve", hT[:, c, :], gt[gi2][:, 0, :], gt[gi2][:, 1, :], ALU.add, [GT[gi2]], [HT[c]])
            c1_R(t, 0)
            c1_R(t, 1)
            c1_chain2(t, (0, 1))
            c1_R(t, 2)
            c1_R(t, 3)
            c1_chain2(t, (2, 3))
        for j in range(4):
            c1_T(NT - 1, j)
        P.barrier()

        o = LOC2
        TQ = min(1024, T)
        NQt = T // TQ
        HPQ = TQ // 512
        wdn = M.at(o, [128, NJ, 1024], BF16); o += NJ * 2048
        WDN = Buf("wdn")
        hid = M.at(o, [128, NJ, TQ], BF16); o += NJ * TQ * 2
        HID = [[Buf("hid%d_%d" % (j, hh)) for hh in range(HPQ)] for j in range(NJ)]
        g2 = M.at(o, [128, D], F32); o += 4096
        b2 = M.at(o, [128, D], F32); o += 4096
        LN2 = Buf("ln2")
        cpm = M.at(o, [128, 2 * NJ, 4], F32); o += 2 * NJ * 16
        CPM = Buf("cpm")
        halo = M.at(o, [128, 2 * NJ, 2], F32); o += 2 * NJ * 8
        HALO = [Buf("halo%d" % jj) for jj in range(2 * NJ)]
        o = (o + 31) // 32 * 32
        wupc = [M.at(o + i * 4096, [128, 8, 256], BF16) for i in range(3)]; o += 12288
        WUP = [Buf("wup%d" % i) for i in range(3)]
        U = [[M.at(o + (i * 2 + g) * 2080, [128, 514], F32) for g in range(2)] for i in range(2)]; o += 4 * 2080
        UB = [[Buf("U%d_%d" % (i, g)) for g in range(2)] for i in range(2)]
        Aa = [[M.at(o + (i * 2 + g) * 2048, [128, 512], F32) for g in range(2)] for i in range(2)]; o += 4 * 2048
        AB = [[Buf("A%d_%d" % (i, g)) for g in range(2)] for i in range(2)]
        xin = [M.at(o + i * 4096, [128, D], F32) for i in range(2)]; o += 8192
        XIN = [Buf("xin2_%d" % i) for i in range(2)]
        st = [M.at(o + i * 64, [128, 2, 6], F32) for i in range(2)]; o += 128
        mv = [M.at(o + i * 32, [128, 2], F32) for i in range(2)]; o += 64
        rs = [M.at(o + i * 32, [128, 1], F32) for i in range(2)]; o += 64
        nbt = [M.at(o + i * 32, [128, 1], F32) for i in range(2)]; o += 64
        NBT = [Buf("nb2%d" % i) for i in range(2)]
        epsb = M.at(o, [128, 1], F32); o += 32
        mhalf = M.at(o, [128, 1], F32); o += 32
        STt = [Buf("st2%d" % i) for i in range(2)]
        MV = [Buf("mv2%d" % i) for i in range(2)]
        RS = [Buf("rs2%d" % i) for i in range(2)]
        CC = Buf("cc2")
        assert o <= 212992, o

        for j in range(0, NJ, 2):
            dma("pool", wdn[:, j:j + 2, :], wdn_h[:, j:j + 2, :], writes=[WDN])
        dma("sp", g2[:, :], ln_h[2], writes=[LN2])
        dma("sp", b2[:, :], ln_h[3], writes=[LN2])
        dma("sp", cpm[:, :, :], cp_h[:, :, :], writes=[CPM])
        ms(epsb[:, :], EPS, [CC])
        ms(mhalf[:, :], -0.5, [CC])
        ms(halo[:, :, :], 0.0, HALO)

        ui = 0
        wi_ = 0
        bi = 0
        wseq = [(q, j) for q in range(NQt) for j in range(NJ)]

        def wup_load(n):
            if n < len(wseq):
                dma("pool", wupc[n % 3][:, :, :], wup_h[wseq[n][1], :, :, :], writes=[WUP[n % 3]], key=("wup", n % 3))
        wup_load(0)
        wup_load(1)
        def ln_front(it_):
            for hh in range(2):
                P.op("dve", lambda e, o_=it_["st"][:, hh, :], i_=it_["yv"][:, hh * 512:(hh + 1) * 512]:
                     e.bn_stats(out=o_, in_=i_), reads=[it_["Y"]], writes=[it_["ST"]], multi=True)
            P.op("dve", lambda e, o_=it_["mv"][:, :], i_=it_["st"][:, :, :]: e.bn_aggr(out=o_, in_=i_),
                 reads=[it_["ST"]], writes=[it_["MV"]])
            ts("pool", it_["rs"][:, :], it_["mv"][:, 1:2], epsb[:, 0:1], None, ALU.add, ALU.bypass,
               [it_["MV"], CC], [it_["RS"]])
            tt("pool", it_["rs"][:, :], it_["rs"][:, :], mhalf[:, :], ALU.pow, [it_["RS"], CC], [it_["RS"]])
            stt(it_["nb"][:, :], it_["mv"][:, 0:1], -1.0, it_["rs"][:, :], ALU.mult, ALU.mult,
                [it_["MV"], it_["RS"]], [it_["NB"]])
            act(it_["yv"], it_["yv"], AF.Identity, [it_["Y"], it_["RS"], it_["NB"]], [it_["Y"]],
                bias=it_["nb"][:, 0:1], scale=it_["rs"][:, 0:1])

        def ln_back(it_, gam, bet, LNB):
            tt("dve", it_["yv"], it_["yv"], gam, ALU.mult, [it_["Y"], LNB], [it_["Y"]])
            tt("dve", it_["yv"], it_["yv"], bet, ALU.add, [it_["Y"], LNB], [it_["Y"]])

        pend = [None]

        def flush_pend():
            if pend[0] is not None:
                o_, a_, b_, rd, wr = pend[0]
                tt("pool", o_, a_, b_, ALU.mult, rd, wr)
                pend[0] = None

        def down_front(q, jb):
            blk = q * (TQ // 128) + jb
            i = blk % 2
            rb = 4 + 2 * i
            dma("sp", xin[i][:, :], x1_s[blk * 128:(blk + 1) * 128, :], writes=[XIN[i]], key=("xin", i))
            for hf in range(2):
                for j in range(NJ):
                    mm(ps[:, rb + hf, :], hid[:, j, jb * 128:(jb + 1) * 128], wdn[:, j, hf * 512:(hf + 1) * 512],
                       j == 0, j == NJ - 1, [HID[j][jb // 4], WDN], [PS[rb + hf]])
            stt(xin[i][:, :], xin[i][:, :], ALPHA, ps[:, rb:rb + 2, :].rearrange("p a n -> p (a n)"),
                ALU.mult, ALU.add, [XIN[i], PS[rb], PS[rb + 1]], [XIN[i]])
            it_ = dict(yv=xin[i][:, :], Y=XIN[i], st=st[i], mv=mv[i], rs=rs[i], nb=nbt[i],
                       ST=STt[i], MV=MV[i], RS=RS[i], NB=NBT[i], blk=blk, i=i)
            ln_front(it_)
            return it_

        def down_back(it_):
            ln_back(it_, g2[:, :], b2[:, :], LN2)
            blk, i = it_["blk"], it_["i"]
            dma("sp", out_h[blk * 128:(blk + 1) * 128, :], xin[i][:, :], reads=[XIN[i]], key=("out", i))

        for q in range(NQt):
            for j in range(NJ):
                wi = wi_ % 3
                wup_load(wi_ + 2)
                wi_ += 1
                for hh in range(HPQ):
                    tI = q * HPQ + hh
                    sl = slice(tI * 512, (tI + 1) * 512)
                    i = ui % 2
                    ui += 1
                    for g in range(2):
                        jj = g * NJ + j
                        b = 2 * i + g
                        for k in range(8):
                            mm(ps[:, b, :], wupc[wi][:, k, g * 128:(g + 1) * 128], xT[:, k, sl], k == 0, k == 7,
                               [WUP[wi], XT[tI]], [PS[b]])
                        cp("pool", U[i][g][:, 0:2], halo[:, jj, :], [HALO[jj]], [UB[i][g]])
                        P.op("act", lambda e, o_=U[i][g][:, 2:514], i_=ps[:, b, :]: e.activation(out=o_, in_=i_, func=AF.Copy),
                             reads=[PS[b]], writes=[UB[i][g]], multi=True)
                        act(Aa[i][g][:, :], ps[:, b, :], AF.Identity, [PS[b], CPM], [AB[i][g]],
                            bias=cpm[:, jj, 3:4], scale=cpm[:, jj, 2:3])
                        cp("pool", halo[:, jj, :], U[i][g][:, 512:514], [UB[i][g]], [HALO[jj]])
                        stt(Aa[i][g][:, :], U[i][g][:, 1:513], cpm[:, jj, 1:2], Aa[i][g][:, :], ALU.mult, ALU.add,
                            [UB[i][g], CPM, AB[i][g]], [AB[i][g]])
                        stt(Aa[i][g][:, :], U[i][g][:, 0:512], cpm[:, jj, 0:1], Aa[i][g][:, :], ALU.mult, ALU.add,
                            [UB[i][g], CPM, AB[i][g]], [AB[i][g]])
                    flush_pend()
                    act(Aa[i][0][:, :], Aa[i][0][:, :], AF.Silu, [AB[i][0]], [AB[i][0]])
                    pend[0] = (hid[:, j, hh * 512:(hh + 1) * 512], Aa[i][0][:, :], Aa[i][1][:, :],
                               [AB[i][0], AB[i][1]], [HID[j][hh]])
            flush_pend()
            prev = None
            for jb in range(TQ // 128):
                cur = down_front(q, jb)
                if prev is not None:
                    down_back(prev)
                prev = cur
            down_back(prev)
        P.barrier()
        P.emit(nc, stack)
    return nc


def host_prep(b, x, positions, w_in, b_gate, sinks, w_branch_a, w_branch_b, w_out,
              ln1_g, ln1_b, w_up, conv_w, conv_b, w_down, ln2_g, ln2_b, shared):
    m = dict(shared)
    m["xT"] = np.ascontiguousarray(x[b].T)
    m["x"] = np.ascontiguousarray(x[b])
    m["pos"] = np.ascontiguousarray(np.broadcast_to(positions[b][None, :].astype(np.int32), (128, positions.shape[1])))
    return m


def host_shared(w_in, b_gate, sinks, w_branch_a, w_branch_b, w_out,
                ln1_g, ln1_b, w_up, conv_w, conv_b, w_down, ln2_g, ln2_b):
    f = np.float32
    w_in = w_in[0]
    cbm = np.zeros((128, NCB), f)
    cbm[:, C_ID:C_ID + 128] = np.eye(128, dtype=f)
    rot = np.zeros((128, 128), f)
    for m_ in range(128):
        base, ml = (m_ // 64) * 64, m_ % 64
        if ml < 32:
            rot[base + ml + 32, m_] = -1.0
        else:
            rot[base + ml - 32, m_] = 1.0
    cbm[:, C_ROT:C_ROT + 128] = rot
    jj, ss = np.meshgrid(np.arange(128), np.arange(128), indexing="ij")
    cbm[:, C_TRI:C_TRI + 128] = np.where(jj >= ss, -8.0, 0.0)
    cbm[:, C_ONE:C_ONE + 128] = -8.0
    s_, t_ = np.meshgrid(np.arange(128), np.arange(128), indexing="ij")
    msb = (s_ < t_).astype(f)
    cbm[:, C_MSB:C_MSB + 128] = msb
    cbm[:, C_MSB + 128:C_MSB + 256] = msb
    mswa = np.concatenate([(s_ <= t_).astype(f), (s_ > t_).astype(f)], axis=1)
    cbm[:, C_MSWA:C_MSWA + 256] = mswa
    cbm[:, C_MSWA + 256:C_MSWA + 512] = mswa
    cbm[:, C_O64:C_O64 + 64] = 1.0
    cfm = np.zeros((128, 8), f)
    inv = (f(1.0) / np.power(f(10000.0), np.arange(0, 64, 2, dtype=f) / f(64.0))).astype(f)
    cfm[:, 0] = inv[np.arange(128) % 32]
    def kp(w):
        K, N = w.shape
        return np.ascontiguousarray(w.reshape(K // 128, 128, N).transpose(1, 0, 2))
    qa_cols = np.concatenate([np.r_[c * 64:(c + 1) * 64, (4 + c) * 64:(5 + c) * 64] for c in range(4)])
    wA = np.concatenate([w_in[:, 0:512][:, qa_cols], w_in[:, 512:640], w_in[:, 640:768]], axis=1)
    wB = np.stack([np.concatenate([w_in[:, 768 + hp * 128:768 + (hp + 1) * 128],
                                   w_in[:, 1280 + hp * 128:1280 + (hp + 1) * 128],
                                   w_in[:, 1792 + hp * 128:1792 + (hp + 1) * 128]], axis=1) for hp in range(4)])
    wgA, wgB = w_in[:, 2304:3328], w_in[:, 3328:4352]
    wg = np.stack([np.concatenate([wgA[:, c * 128:(c + 1) * 128], wgB[:, c * 128:(c + 1) * 128]], axis=1)
                   for c in range(8)])
    frow = np.array([[(c if p < 64 else 4 + c) * 64 + p % 64 for c in range(4)] for p in range(128)])
    wa = np.ascontiguousarray(w_branch_a[0][frow, :])
    wup = w_up[0]
    wupr = np.stack([np.concatenate([wup[:, j * 128:(j + 1) * 128], wup[:, DFF + j * 128:DFF + (j + 1) * 128]], axis=1)
                     for j in range(NJ)])
    cpm = np.zeros((128, 2 * NJ, 4), f)
    cw, cbias = conv_w[0], conv_b[0]
    for jj_ in range(2 * NJ):
        ch = jj_ * 128 + np.arange(128)
        cpm[:, jj_, 0:3] = cw[:, ch].T
        cpm[:, jj_, 3] = cbias[ch]
    bgm = np.zeros((128, 16), f)
    for c in range(8):
        bgm[:, c] = b_gate[0][c * 128:(c + 1) * 128]
        bgm[:, 8 + c] = b_gate[0][1024 + c * 128:1024 + (c + 1) * 128]
    skm = np.zeros((128, 4), f)
    for c in range(4):
        skm[0:64, c] = sinks[0][c]
        skm[64:128, c] = sinks[0][4 + c]
    lnm = np.stack([np.broadcast_to(v[0][None, :], (128, D)) for v in (ln1_g, ln1_b, ln2_g, ln2_b)]).astype(f)
    return dict(
        cb=cbm, cf=cfm,
        wA=kp(wA), wB=np.stack([kp(wB[hp]) for hp in range(4)]),
        wg=np.stack([kp(wg[c]) for c in range(8)]),
        wa=wa, wb=kp(w_branch_b[0]), wo=kp(w_out[0]),
        wup=np.stack([kp(wupr[j]) for j in range(NJ)]), wdn=kp(w_down[0]),
        cp=cpm, bg=bgm, sk=skm, ln=np.ascontiguousarray(lnm),
    )


_NC_CACHE = {}


def kernel(x, positions, w_in, b_gate, sinks, w_branch_a, w_branch_b, w_out,
           ln1_g, ln1_b, w_up, conv_w, conv_b, w_down, ln2_g, ln2_b):
    args = [np.asarray(a) for a in (x, positions, w_in, b_gate, sinks, w_branch_a, w_branch_b, w_out,
                                    ln1_g, ln1_b, w_up, conv_w, conv_b, w_down, ln2_g, ln2_b)]
    x, positions = args[0], args[1]
    B, T, _ = x.shape
    shared = host_shared(*args[2:])
    in_maps = [host_prep(b, x, positions, *args[2:], shared) for b in range(B)]
    nc = build(T)
    res = run_bass_kernel_spmd(nc, in_maps, core_ids=list(range(B)))
    return np.stack([np.asarray(r["out"]) for r in res.results]).astype(np.float32)
```

```python
import numpy as np
import concourse.bass as bass
import concourse.mybir as mybir
from concourse.bass_utils import run_bass_kernel_spmd

F32 = mybir.dt.float32
BF16 = mybir.dt.bfloat16
I32 = mybir.dt.int32
AF = mybir.ActivationFunctionType
ALU = mybir.AluOpType

D = 1024
DFF = 2816
NJ = DFF // 128
HD = 64
ALPHA = float(2.0 ** 0.25)
EPS = 1e-5
PI = float(np.pi)
INV2PI = float(np.float32(1.0 / (2.0 * np.pi)))
MAGIC = 12582912.0
CW1 = 6.28125
CW2 = float(2.0 * np.pi - 6.28125)
SBUF_BASE = 16384

C_ID, C_ROT, C_TRI, C_ONE, C_MSB, C_MSWA, C_O64, NCB = 0, 128, 256, 384, 512, 768, 1280, 1344


class Buf:
    __slots__ = ("name", "w", "r", "psum")

    def __init__(self, name, psum=False):
        self.name = name
        self.w = []
        self.r = []
        self.psum = psum


class Prog:
    ENGS = ("pe", "act", "dve", "pool", "sp")

    def __init__(self):
        self.ops = []
        self.bar_start = 0

    def op(self, eng, fn, reads=(), writes=(), dma=None, multi=False):
        i = len(self.ops)
        deps = set()
        for b in reads:
            deps.update(b.w)
            if b.psum:
                deps.update(r for r in b.r if self.ops[r]["eng"] != eng)
        for b in writes:
            if not multi:
                deps.update(b.w)
            deps.update(b.r)
        for b in reads:
            b.r.append(i)
        for b in writes:
            if multi and not b.r:
                b.w = b.w + [i]
            else:
                b.w = [i]
            b.r = []
        deps.discard(i)
        last = {}
        keep = set()
        for d in deps:
            od = self.ops[d]
            if od["dma"] is not None:
                keep.add(d)
            elif last.get(od["eng"], -1) < d:
                last[od["eng"]] = d
        keep.update(last.values())
        self.ops.append(dict(eng=eng, fn=fn, deps=keep, dma=dma))
        return i

    def barrier(self):
        last = {}
        deps = set()
        for idx in range(self.bar_start, len(self.ops)):
            o = self.ops[idx]
            if o["fn"] is None:
                continue
            if o["dma"] is not None:
                deps.add(idx)
            else:
                last[o["eng"]] = idx
        deps.update(last.values())
        for e in self.ENGS:
            self.ops.append(dict(eng=e, fn=None, deps=set(deps), dma=None))
        self.bar_start = len(self.ops)

    def emit(self, nc, stack):
        ops = self.ops
        need = set()
        for o in ops:
            for d in o["deps"]:
                od = ops[d]
                if o["eng"] == "pe" and od["eng"] == "pe" and od["dma"] is None:
                    continue
                need.add(d)
        esem = {e: stack.enter_context(nc.semaphore("s_" + e)) for e in self.ENGS}
        dsem = {}
        tick = {e: 0 for e in self.ENGS}
        dcnt = {}
        sig = {}
        for i, o in enumerate(ops):
            if i not in need or o["fn"] is None:
                continue
            if o["dma"] is not None:
                k = o["dma"]
                if k not in dsem:
                    dsem[k] = stack.enter_context(nc.semaphore("d_%d" % len(dsem)))
                    dcnt[k] = 0
                dcnt[k] += 16
                sig[i] = (dsem[k], dcnt[k], 16)
            else:
                tick[o["eng"]] += 1
                sig[i] = (esem[o["eng"]], tick[o["eng"]], 1)
        per = {e: [] for e in self.ENGS}
        for i, o in enumerate(ops):
            per[o["eng"]].append(i)
        self.stats = dict(ticks=dict(tick), nsem=len(dsem) + len(esem), nops=len(ops),
                          dmax=max(dcnt.values()) if dcnt else 0)
        block = stack.enter_context(nc.Block())

        def run(eng, ename):
            waited = {}
            for i in per[ename]:
                o = ops[i]
                for d in sorted(o["deps"]):
                    if d not in sig:
                        continue
                    od = ops[d]
                    if ename == "pe" and od["eng"] == "pe" and od["dma"] is None:
                        continue
                    sem, val, _ = sig[d]
                    key = id(sem)
                    if waited.get(key, 0) >= val:
                        continue
                    eng.wait_ge(sem, val)
                    waited[key] = val
                if o["fn"] is not None:
                    ins = o["fn"](eng)
                    if i in sig:
                        ins.then_inc(sig[i][0], sig[i][2])

        @block.tensor
        def _(e):
            run(e, "pe")

        @block.scalar
        def _(e):
            run(e, "act")

        @block.vector
        def _(e):
            run(e, "dve")

        @block.gpsimd
        def _(e):
            run(e, "pool")

        @block.sync
        def _(e):
            run(e, "sp")


class Mem:
    def __init__(self, nc):
        self.nc = nc
        self.n = 0

    def at(self, off, shape, dt):
        self.n += 1
        return self.nc.alloc_sbuf_tensor_at("t%d" % self.n, list(shape), dt, offset=SBUF_BASE + off).ap()


def build(T=4096, debug=False):
    NB = T // 128
    NT = T // 512
    nc = bass.Bass("TRN2", target_bir_lowering=False)
    P = Prog()
    M = Mem(nc)

    def din(name, shape, dt=F32):
        return nc.dram_tensor(name, list(shape), dt, kind="ExternalInput").ap()

    xT_h = din("xT", [D, T])
    x_h = din("x", [T, D])
    pos_h = din("pos", [128, T], I32)
    cb_h = din("cb", [128, NCB])
    cf_h = din("cf", [128, 8])
    wA_h = din("wA", [128, 8, 768])
    wB_h = din("wB", [4, 128, 8, 384])
    wg_h = din("wg", [8, 128, 8, 256])
    wa_h = din("wa", [128, 4, 1024])
    wb_h = din("wb", [128, 4, 1024])
    wo_h = din("wo", [128, 8, 1024])
    wup_h = din("wup", [NJ, 128, 8, 256])
    wdn_h = din("wdn", [128, NJ, 1024])
    cp_h = din("cp", [128, 2 * NJ, 4])
    bg_h = din("bg", [128, 16])
    sk_h = din("sk", [128, 4])
    ln_h = din("ln", [4, 128, D])
    out_h = nc.dram_tensor("out", [T, D], F32, kind="ExternalOutput").ap()
    x1_s = nc.dram_tensor("x1s", [T, D], F32, kind="Internal").ap()
    rt_s = nc.dram_tensor("rts", [2, 128, T], F32, kind="Internal").ap()
    if debug:
        dbgA = nc.dram_tensor("dbgA", [128, 4, T], F32, kind="ExternalOutput").ap()
        dbgB = nc.dram_tensor("dbgB", [128, 4, T], F32, kind="ExternalOutput").ap()

    import contextlib
    stack = contextlib.ExitStack()
    with stack:
        ps = stack.enter_context(nc.psum_tensor([128, 8, 512], F32))
        PS = [Buf("ps%d" % b, psum=True) for b in range(8)]

        o = 0
        cb = M.at(o, [128, NCB], BF16); o += 2 * NCB
        CB = Buf("cb")
        cf = M.at(o, [128, 8], F32); o += 32
        CF = Buf("cf")
        xT = M.at(o, [128, 8, T], BF16); o += 16 * T
        XT = [Buf("xT%d" % t) for t in range(NT)]
        LOC2 = o
        yAT = M.at(o, [128, 4, T], BF16); o += 8 * T
        YA = [Buf("yA%d" % t) for t in range(NT)]
        yBT = M.at(o, [128, 4, T], BF16); o += 8 * T
        YB = [Buf("yB%d" % t) for t in range(NT)]
        LOC = o
        assert LOC % 32 == 0

        def dma(eng, out, in_, reads=(), writes=(), key=None, multi=True):
            if key is None:
                key = ("w", writes[0].name) if writes else ("r", reads[0].name)
            return P.op(eng, lambda e, out=out, in_=in_: e.dma_start(out=out, in_=in_),
                        reads=reads, writes=writes, dma=key, multi=multi)

        def ms(ap, val, writes):
            return P.op("pool", lambda e, ap=ap, val=val: e.memset(ap, val), writes=writes)

        def mm(out, lhsT, rhs, start, stop, reads, writes):
            return P.op("pe", lambda e, out=out, lhsT=lhsT, rhs=rhs, start=start, stop=stop:
                        e.matmul(out, lhsT=lhsT, rhs=rhs, start=start, stop=stop, skip_group_check=True),
                        reads=reads, writes=writes)

        def act(out, in_, func, reads, writes, bias=0.0, scale=1.0):
            return P.op("act", lambda e, out=out, in_=in_, func=func, bias=bias, scale=scale:
                        e.activation(out=out, in_=in_, func=func, bias=bias, scale=scale),
                        reads=reads, writes=writes)

        def tt(eng, out, in0, in1, op, reads, writes):
            return P.op(eng, lambda e, out=out, in0=in0, in1=in1, op=op:
                        e.tensor_tensor(out=out, in0=in0, in1=in1, op=op), reads=reads, writes=writes)

        def ts(eng, out, in0, s1, s2, op0, op1, reads, writes):
            return P.op(eng, lambda e, out=out, in0=in0, s1=s1, s2=s2, op0=op0, op1=op1:
                        e.tensor_scalar(out=out, in0=in0, scalar1=s1, scalar2=s2, op0=op0, op1=op1),
                        reads=reads, writes=writes)

        def stt(out, in0, scalar, in1, op0, op1, reads, writes):
            return P.op("dve", lambda e, out=out, in0=in0, scalar=scalar, in1=in1, op0=op0, op1=op1:
                        e.scalar_tensor_tensor(out=out, in0=in0, scalar=scalar, in1=in1, op0=op0, op1=op1),
                        reads=reads, writes=writes)

        def cp(eng, out, in_, reads, writes):
            if eng == "act":
                return act(out, in_, AF.Copy, reads, writes)
            return P.op(eng, lambda e, out=out, in_=in_: e.tensor_copy(out=out, in_=in_),
                        reads=reads, writes=writes)

        dma("pool", cb[:, :], cb_h[:, :], writes=[CB])
        dma("sp", cf[:, :], cf_h[:, :], writes=[CF])
        xv = xT_h.rearrange("(c p) t -> p c t", p=128)

        def load_xT(t):
            dma("pool", xT[:, :, t * 512:(t + 1) * 512], xv[:, :, t * 512:(t + 1) * 512], writes=[XT[t]])
        load_xT(0)

        o = LOC
        QTA = M.at(o, [128, 2, T], BF16); o += 4 * T
        QA = [[Buf("qa%d_%d" % (c, t)) for t in range(NT)] for c in range(4)]
        KTA = M.at(o, [128, T], BF16); o += 2 * T
        KA = [Buf("ka%d" % t) for t in range(NT)]
        VA = M.at(o, [128, NB, 128], BF16); o += 2 * T
        VAb = [Buf("va%d" % t) for t in range(NT)]
        sk = M.at(o, [128, 4], F32); o += 32
        SK = Buf("sk")
        npi = M.at(o, [128, 1], F32); o += 32
        NPI = Buf("npi")
        A2 = o
        wA = M.at(o, [128, 8, 768], BF16); o += 8 * 768 * 2
        WA = Buf("wA")
        ang = M.at(o, [128, 512], F32); o += 2048
        tmpa = M.at(o, [128, 512], F32); o += 2048
        tmpb = M.at(o, [128, 512], F32); o += 2048
        posi = tmpb.bitcast(I32)
        TMPB = Buf("tmpb")
        cosTs = [M.at(o + i * 2048, [128, 512], F32) for i in range(2)]; o += 4096
        sinTs = [M.at(o + i * 2048, [128, 512], F32) for i in range(2)]; o += 4096
        COSs = [Buf("cos%d" % i) for i in range(2)]
        SINs = [Buf("sin%d" % i) for i in range(2)]
        ANG, TMPA = Buf("ang"), Buf("tmpa")
        POSI = TMPB
        q32 = [M.at(o + i * 2048, [128, 512], F32) for i in range(2)]; o += 4096
        qb = [M.at(o + i * 1024, [128, 512], BF16) for i in range(2)]; o += 2048
        t2 = [M.at(o + i * 2048, [128, 512], F32) for i in range(2)]; o += 4096
        Q32 = [Buf("q32_%d" % i) for i in range(2)]
        QB = [Buf("qb_%d" % i) for i in range(2)]
        T2 = [Buf("t2_%d" % i) for i in range(2)]
        assert o <= 212992, o

        for k in range(0, 8, 2):
            dma("pool", wA[:, k:k + 2, :], wA_h[:, k:k + 2, :], writes=[WA])
        for t in range(1, NT):
            load_xT(t)
        dma("sp", sk[:, :], sk_h[:, :], writes=[SK])
        ms(npi[:, :], -PI, [NPI])
        act(sk[:, :], sk[:, :], AF.Exp, [SK], [SK])

        ctr = [0]

        def rope_proj(wcols, dst, dstbuf, t, cosT, sinT, COS, SIN):
            i = ctr[0] % 2
            ctr[0] += 1
            b0, b1 = 2 * i, 2 * i + 1
            for k in range(8):
                mm(ps[:, b0, :], wA[:, k, wcols], xT[:, k, t * 512:(t + 1) * 512], k == 0, k == 7,
                   [WA, XT[t]], [PS[b0]])
            act(q32[i][:, :], ps[:, b0, :], AF.Copy, [PS[b0]], [Q32[i]])
            cp("act", qb[i][:, :], ps[:, b0, :], [PS[b0]], [QB[i]])
            mm(ps[:, b1, :], cb[:, C_ROT:C_ROT + 128], qb[i][:, :], True, True, [CB, QB[i]], [PS[b1]])
            tt("dve", q32[i][:, :], q32[i][:, :], cosT[:, :], ALU.mult, [Q32[i], COS], [Q32[i]])
            tt("dve", t2[i][:, :], ps[:, b1, :], sinT[:, :], ALU.mult, [PS[b1], SIN], [T2[i]])
            tt("dve", dst, q32[i][:, :], t2[i][:, :], ALU.add, [Q32[i], T2[i]], [dstbuf])

        Pt = [M.at(o + i * 1024, [128, 2, 256], BF16) for i in range(2)]; o += 2048
        PT = [Buf("pt%d" % i) for i in range(2)]
        dn = [M.at(o + i * 512, [128, 128], F32) for i in range(2)]; o += 1024
        DN = [Buf("dn%d" % i) for i in range(2)]
        assert o <= 212992, o
        msw = cb[:, C_MSWA:C_MSWA + 512].rearrange("p (h u) -> p h u", h=2)
        o64 = cb[:, C_O64:C_O64 + 64]
        for half in range(2):
            for t in range(NT):
                sl = slice(t * 512, (t + 1) * 512)
                cosT, sinT, COS, SIN = cosTs[t % 2], sinTs[t % 2], COSs[t % 2], SINs[t % 2]
                if half == 0:
                    dma("sp", posi[:, :], pos_h[:, sl], writes=[POSI])
                    cp("dve", ang[:, :], posi[:, :], [POSI], [ANG])
                    ts("dve", ang[:, :], ang[:, :], cf[:, 0:1], None, ALU.mult, ALU.bypass, [ANG, CF], [ANG])
                    def sin_of(base_ap, BASE, dst, DST):
                        ts("dve", tmpa[:, :], base_ap, INV2PI, MAGIC, ALU.mult, ALU.add, [BASE], [TMPA])
                        ts("dve", tmpa[:, :], tmpa[:, :], MAGIC, None, ALU.subtract, ALU.bypass, [TMPA], [TMPA])
                        stt(tmpb[:, :], tmpa[:, :], -CW1, base_ap, ALU.mult, ALU.add, [TMPA, BASE], [TMPB])
                        stt(tmpb[:, :], tmpa[:, :], -CW2, tmpb[:, :], ALU.mult, ALU.add, [TMPA, TMPB], [TMPB])
                        ts("dve", tmpb[:, :], tmpb[:, :], 3.141592, -3.141592, ALU.min, ALU.max, [TMPB], [TMPB])
                        act(dst, tmpb[:, :], AF.Sin, [TMPB], [DST])
                    sin_of(ang[:, :], ANG, sinT[:, :], SIN)
                    ts("dve", ang[:, :], ang[:, :], 0.5 * PI, None, ALU.add, ALU.bypass, [ANG], [ANG])
                    sin_of(ang[:, :], ANG, cosT[:, :], COS)
                    dma("sp", rt_s[0, :, sl], sinT[:, :], reads=[SIN], key=("rts", 0, t % 2))
                    dma("sp", rt_s[1, :, sl], cosT[:, :], reads=[COS], key=("rts", 1, t % 2))
                else:
                    dma("sp", sinT[:, :], rt_s[0, :, sl], writes=[SIN], key=("rtl", 0, t % 2))
                    dma("sp", cosT[:, :], rt_s[1, :, sl], writes=[COS], key=("rtl", 1, t % 2))
                if half == 0:
                    rope_proj(slice(512, 640), KTA[:, sl], KA[t], t, cosT, sinT, COS, SIN)
                for c in (2 * half, 2 * half + 1):
                    rope_proj(slice(c * 128, (c + 1) * 128), QTA[:, c % 2, sl], QA[c][t], t, cosT, sinT, COS, SIN)
                for j in range(4 if half == 0 else 0):
                    blk = t * 4 + j
                    for k in range(8):
                        mm(ps[:, 4, j * 128:(j + 1) * 128], xT[:, k, blk * 128:(blk + 1) * 128], wA[:, k, 640:768],
                           k == 0, k == 7, [XT[t], WA], [PS[4]])
                if half == 0:
                    cp("act", VA[:, t * 4:(t + 1) * 4, :], ps[:, 4, :].rearrange("p (j d) -> p j d", j=4), [PS[4]], [VAb[t]])

            P.barrier()
            it = 0
            for c in (2 * half, 2 * half + 1):
                for kb in range(NB):
                    N = 256 if kb < NB - 1 else 128
                    i = it % 2
                    it += 1
                    b0 = 2 * i
                    t0 = kb * 128
                    tq = [QA[c][(t0) // 512]] + ([QA[c][(t0 + 128) // 512]] if N == 256 else [])
                    for h in range(2):
                        r = slice(64 * h, 64 * h + 64)
                        mm(ps[:, b0 + h, 0:N], KTA[r, t0:t0 + 128], QTA[r, c % 2, t0:t0 + N], True, True,
                           [KA[kb // 4]] + tq, [PS[b0 + h]])
                    act(Pt[i][:, :, 0:N], ps[:, b0:b0 + 2, 0:N], AF.Exp, [PS[b0], PS[b0 + 1]], [PT[i]], scale=0.125)
                    tt("dve", Pt[i][:, :, 0:N], Pt[i][:, :, 0:N], msw[:, :, 0:N], ALU.mult, [PT[i], CB], [PT[i]])
                    ob, db = 4 + kb % 2, 6 + kb % 2
                    ob1, db1 = 4 + (kb + 1) % 2, 6 + (kb + 1) % 2
                    for h in range(2):
                        r = slice(64 * h, 64 * h + 64)
                        vsl = VA[:, kb, r]
                        mm(ps[r, ob, 0:128], vsl, Pt[i][:, h, 0:128], kb == 0, True, [VAb[kb // 4], PT[i]], [PS[ob]])
                        mm(ps[r, db, 0:128], o64, Pt[i][:, h, 0:128], kb == 0, True, [CB, PT[i]], [PS[db]])
                    j = kb % 2
                    act(dn[j][:, :], ps[:, db, 0:128], AF.Ln, [PS[db], SK], [DN[j]], bias=sk[:, c:c + 1])
                    act(dn[j][:, :], dn[j][:, :], AF.Exp, [DN[j]], [DN[j]], scale=-1.0)
                    tt("dve", yAT[:, c, t0:t0 + 128], ps[:, ob, 0:128], dn[j][:, :], ALU.mult,
                       [PS[ob], DN[j]], [YA[kb // 4]])
                    if N == 256:
                        for h in range(2):
                            r = slice(64 * h, 64 * h + 64)
                            vsl = VA[:, kb, r]
                            mm(ps[r, ob1, 0:128], vsl, Pt[i][:, h, 128:256], True, False, [VAb[kb // 4], PT[i]], [PS[ob1]])
                            mm(ps[r, db1, 0:128], o64, Pt[i][:, h, 128:256], True, False, [CB, PT[i]], [PS[db1]])
            P.barrier()

        o = LOC
        QTBs = [M.at(o + i * 2 * T, [128, T], BF16) for i in range(2)]; o += 4 * T
        KTBs = [M.at(o + i * 2 * T, [128, T], BF16) for i in range(2)]; o += 4 * T
        VBs = [M.at(o + i * 2 * T, [128, NB, 128], BF16) for i in range(2)]; o += 4 * T
        QBbs = [[Buf("qB%d_%d" % (i, t)) for t in range(NT)] for i in range(2)]
        KBbs = [[Buf("kB%d_%d" % (i, t)) for t in range(NT)] for i in range(2)]
        VBbs = [[Buf("vB%d_%d" % (i, t)) for t in range(NT)] for i in range(2)]
        wB = [M.at(o + i * 6144, [128, 8, 384], BF16) for i in range(2)]; o += 12288
        WB = [Buf("wB%d" % i) for i in range(2)]
        NE = 1
        Et = [M.at(o + i * 4096, [128, 2, 512], F32) for i in range(NE)]; o += 4096 * NE
        ET = [Buf("E%d" % i) for i in range(NE)]
        NL = 2
        Lt = [M.at(o + i * 2048, [128, 2, 512], BF16) for i in range(NL)]; o += 2048 * NL
        LT = [Buf("L%d" % i) for i in range(NL)]
        Ac = [M.at(o + i * 2048, [128, 2, 512], BF16) for i in range(2)]; o += 4096
        AC = [Buf("Ac%d" % i) for i in range(2)]
        At = [M.at(o + i * 2048, [128, 2, 512], BF16) for i in range(2)]; o += 4096
        AT = [Buf("At%d" % i) for i in range(2)]
        assert o <= 212992, o
        msb2 = cb[:, C_MSB:C_MSB + 256].rearrange("p (h u) -> p h u", h=2)
        ntri = cb[:, C_TRI:C_TRI + 128]
        none_ = cb[:, C_ONE:C_ONE + 128]
        PJ = 7

        def wB_load(hp):
            for k in range(0, 8, 4):
                dma("pool", wB[hp % 2][:, k:k + 4, :], wB_h[hp, :, k:k + 4, :], writes=[WB[hp % 2]])

        def proj_ops(hp):
            w, W, si = wB[hp % 2], WB[hp % 2], hp % 2
            ops_ = []
            for t in range(NT):
                sl = slice(t * 512, (t + 1) * 512)
                for which, dst, dbuf in ((0, QTBs[si], QBbs[si]), (1, KTBs[si], KBbs[si])):
                    for k in range(8):
                        ops_.append(lambda k=k, which=which, sl=sl, t=t: mm(
                            ps[:, PJ, :], w[:, k, which * 128:(which + 1) * 128], xT[:, k, sl], k == 0, k == 7,
                            [W, XT[t]], [PS[PJ]]))
                    ops_.append(lambda dst=dst, dbuf=dbuf, sl=sl, t=t: cp("dve", dst[:, sl], ps[:, PJ, :], [PS[PJ]], [dbuf[t]]))
                for j in range(4):
                    blk = t * 4 + j
                    for k in range(8):
                        ops_.append(lambda k=k, j=j, blk=blk, t=t: mm(
                            ps[:, PJ, j * 128:(j + 1) * 128], xT[:, k, blk * 128:(blk + 1) * 128], w[:, k, 256:384],
                            k == 0, k == 7, [XT[t], W], [PS[PJ]]))
                ops_.append(lambda t=t: cp("dve", VBs[si][:, t * 4:(t + 1) * 4, :],
                                          ps[:, PJ, :].rearrange("p (j d) -> p j d", j=4), [PS[PJ]], [VBbs[si][t]]))
            return ops_

        wB_load(0)
        for f_ in proj_ops(0):
            f_()
        for hp in range(4):
            QTB, KTB, VB = QTBs[hp % 2], KTBs[hp % 2], VBs[hp % 2]
            QBb, KBb, VBb = QBbs[hp % 2], KBbs[hp % 2], VBbs[hp % 2]
            nxt = []
            if hp < 3:
                wB_load(hp + 1)
                nxt = proj_ops(hp + 1)

            steps = []
            for qt in range(NT):
                acur = None
                kbs = list(range(4 * qt + 3, -1, -1))
                for si, kb in enumerate(kbs):
                    off = max(0, kb - 4 * qt) * 128
                    st_ = dict(qt=qt, kb=kb, off=off, N=512 - off, t0=qt * 512 + off, diag=kb >= 4 * qt,
                               first=si == 0, last=kb == 0, ob=6, acur=acur)
                    if kb != 0:
                        st_["anew"] = 0 if si == 0 else 1 - acur
                        acur = st_["anew"]
                    steps.append(st_)
            for n_, st_ in enumerate(steps):
                st_["zp"] = 2 * (n_ % 3)
                st_["e"] = n_ % NE
                st_["l"] = n_ % NL
                st_["a"] = n_ % 2

            def s_z(S):
                off, zp, kb, t0, N = S["off"], S["zp"], S["kb"], S["t0"], S["N"]
                for h in range(2):
                    r = slice(64 * h, 64 * h + 64)
                    mm(ps[:, zp + h, off:512], KTB[r, kb * 128:(kb + 1) * 128], QTB[r, t0:t0 + N], True, False,
                       [KBb[kb // 4], QBb[S["qt"]]], [PS[zp + h]])

            def s_el(S):
                off, zp, e_i, l_i = S["off"], S["zp"], S["e"], S["l"]
                Zb = [PS[zp], PS[zp + 1]]
                act(Et[e_i][:, :, off:512], ps[:, zp:zp + 2, off:512], AF.Exp, Zb, [ET[e_i]], scale=0.125)
                act(Lt[l_i][:, :, off:512], Et[e_i][:, :, off:512], AF.Ln, [ET[e_i]], [LT[l_i]], bias=1.0)
                if S["diag"]:
                    tt("dve", Lt[l_i][:, :, off:off + 128], Lt[l_i][:, :, off:off + 128], msb2, ALU.mult,
                       [LT[l_i], CB], [LT[l_i]])
                if not S["last"]:
                    anew, acur = S["anew"], S["acur"]
                    if S["first"]:
                        ms(Ac[anew][:, :, :], 0.0, [AC[anew]])
                        cp("dve", Ac[anew][:, :, off:512], Lt[l_i][:, :, off:512], [LT[l_i]], [AC[anew]])
                    else:
                        if off > 0:
                            ms(Ac[anew][:, :, :], 0.0, [AC[anew]])
                        tt("dve", Ac[anew][:, :, off:512], Ac[acur][:, :, off:512], Lt[l_i][:, :, off:512], ALU.add,
                           [AC[acur], LT[l_i]], [AC[anew]])

            def s_tri(S):
                off, zp, l_i = S["off"], S["zp"], S["l"]
                for h in range(2):
                    mm(ps[:, zp + h, off:512], ntri, Lt[l_i][:, h, off:512], False, S["first"],
                       [CB, LT[l_i]], [PS[zp + h]])
                    if not S["first"]:
                        mm(ps[:, zp + h, off:512], none_, Ac[S["acur"]][:, h, off:512], False, True,
                           [CB, AC[S["acur"]]], [PS[zp + h]])

            def s_a(S):
                off, zp, a_i = S["off"], S["zp"], S["a"]
                Zb = [PS[zp], PS[zp + 1]]
                act(At[a_i][:, :, off:512], ps[:, zp:zp + 2, off:512], AF.Exp, Zb, [AT[a_i]], scale=0.125)
                if S["diag"]:
                    tt("dve", At[a_i][:, :, off:off + 128], At[a_i][:, :, off:off + 128], msb2, ALU.mult,
                       [AT[a_i], CB], [AT[a_i]])

            def s_av(S):
                off, a_i, ob, kb = S["off"], S["a"], S["ob"], S["kb"]
                for h in range(2):
                    r = slice(64 * h, 64 * h + 64)
                    mm(ps[r, ob, off:512], VB[:, kb, r], At[a_i][:, h, off:512], S["first"], S["last"],
                       [VBb[kb // 4], AT[a_i]], [PS[ob]])
                if S["last"]:
                    qt = S["qt"]
                    cp("dve", yBT[:, hp, qt * 512:(qt + 1) * 512], ps[:, ob, :], [PS[ob]], [YB[qt]])

            ns = len(steps)
            for tau in range(ns + 2):
                rounds_left = max(1, ns - 4 - tau)
                take = -(-len(nxt) // rounds_left) if tau < ns - 4 else len(nxt)
                for f_ in nxt[:take]:
                    f_()
                nxt = nxt[take:]
                if tau < ns:
                    s_z(steps[tau])
                if 1 <= tau <= ns:
                    s_tri(steps[tau - 1])
                if 2 <= tau:
                    s_av(steps[tau - 2])
                if tau < ns:
                    s_el(steps[tau])
                if 1 <= tau <= ns:
                    s_a(steps[tau - 1])
        P.barrier()

        if debug:
            o = LOC
            dtmp = M.at(o, [128, 4, 512], F32)
            DT = Buf("dtmp")
            for t in range(NT):
                sl = slice(t * 512, (t + 1) * 512)
                cp("dve", dtmp[:, :, :], yAT[:, :, sl], [YA[t]], [DT])
                dma("sp", dbgA[:, :, sl], dtmp[:, :, :], reads=[DT], key=("dbg", 0))
                cp("dve", dtmp[:, :, :], yBT[:, :, sl], [YB[t]], [DT])
                dma("sp", dbgB[:, :, sl], dtmp[:, :, :], reads=[DT], key=("dbg", 0))
            P.barrier()

        o = LOC
        wa = M.at(o, [128, 4, 1024], BF16); o += 8192
        wb = M.at(o, [128, 4, 1024], BF16); o += 8192
        wo = M.at(o, [128, 8, 1024], BF16); o += 16384
        WAa, WBb, WO = Buf("wa"), Buf("wb"), Buf("wo")
        g1 = M.at(o, [128, D], F32); o += 4096
        b1 = M.at(o, [128, D], F32); o += 4096
        LN1 = Buf("ln1")
        bg = M.at(o, [128, 16], F32); o += 64
        BG = Buf("bg")
        epsb = M.at(o, [128, 1], F32); o += 32
        mhalf = M.at(o, [128, 1], F32); o += 32
        CC = Buf("cc")
        wgc = [M.at(o + i * 4096, [128, 8, 256], BF16) for i in range(2)]; o += 8192
        WG = [Buf("wg%d" % i) for i in range(2)]
        gt = [M.at(o, [128, 2, 512], F32)] * 2; o += 4096
        GT = [Buf("gt")] * 2
        hT = M.at(o, [128, 8, 512], BF16); o += 8192
        HT = [Buf("hT%d" % c) for c in range(8)]
        xin = [M.at(o + i * 4096, [128, D], F32) for i in range(2)]; o += 8192
        XIN = [Buf("xin%d" % i) for i in range(2)]
        x1b = [M.at(o + i * 2048, [128, D], BF16) for i in range(4)]; o += 8192
        X1B = [Buf("x1b%d" % i) for i in range(4)]
        st = [M.at(o + i * 64, [128, 2, 6], F32) for i in range(2)]; o += 128
        mv = [M.at(o + i * 32, [128, 2], F32) for i in range(2)]; o += 64
        rs = [M.at(o + i * 32, [128, 1], F32) for i in range(2)]; o += 64
        nbt = [M.at(o + i * 32, [128, 1], F32) for i in range(2)]; o += 64
        NBT = [Buf("nb%d" % i) for i in range(2)]
        STt = [Buf("st%d" % i) for i in range(2)]
        MV = [Buf("mv%d" % i) for i in range(2)]
        RS = [Buf("rs%d" % i) for i in range(2)]
        assert o <= 212992, o

        for k in range(0, 4, 2):
            dma("pool", wa[:, k:k + 2, :], wa_h[:, k:k + 2, :], writes=[WAa])
            dma("pool", wb[:, k:k + 2, :], wb_h[:, k:k + 2, :], writes=[WBb])
        for k in range(0, 8, 2):
            dma("pool", wo[:, k:k + 2, :], wo_h[:, k:k + 2, :], writes=[WO])
        dma("sp", g1[:, :], ln_h[0], writes=[LN1])
        dma("sp", b1[:, :], ln_h[1], writes=[LN1])
        dma("sp", bg[:, :], bg_h[:, :], writes=[BG])
        ms(epsb[:, :], EPS, [CC])
        ms(mhalf[:, :], -0.5, [CC])

        def layer_norm(yv, Y, gam, bet, LNB, out_ap, OUT, st_, mv_, rs_, ST_, MV_, RS_, eps_, mh_, CC_, nb_, NB_):
            for hh in range(2):
                P.op("dve", lambda e, o_=st_[:, hh, :], i_=yv[:, hh * 512:(hh + 1) * 512]: e.bn_stats(out=o_, in_=i_),
                     reads=[Y], writes=[ST_], multi=True)
            P.op("dve", lambda e, o_=mv_[:, :], i_=st_[:, :, :]: e.bn_aggr(out=o_, in_=i_), reads=[ST_], writes=[MV_])
            ts("pool", rs_[:, :], mv_[:, 1:2], eps_[:, 0:1], None, ALU.add, ALU.bypass, [MV_, CC_], [RS_])
            tt("pool", rs_[:, :], rs_[:, :], mh_[:, :], ALU.pow, [RS_, CC_], [RS_])
            stt(nb_[:, :], mv_[:, 0:1], -1.0, rs_[:, :], ALU.mult, ALU.mult, [MV_, RS_], [NB_])
            act(yv, yv, AF.Identity, [Y, RS_, NB_], [Y], bias=nb_[:, 0:1], scale=rs_[:, 0:1])
            tt("dve", yv, yv, gam, ALU.mult, [Y, LNB], [Y])
            tt("dve", out_ap, yv, bet, ALU.add, [Y, LNB], [OUT])

        def c1_R(t, j):
            rb = 4 + 2 * (j % 2)
            for hf in range(2):
                for c in range(8):
                    mm(ps[:, rb + hf, :], hT[:, c, j * 128:(j + 1) * 128], wo[:, c, hf * 512:(hf + 1) * 512],
                       c == 0, c == 7, [HT[c], WO], [PS[rb + hf]])

        def ln_multi(items, gam, bet, LNB):
            for it_ in items:
                for hh in range(2):
                    P.op("dve", lambda e, o_=it_["st"][:, hh, :], i_=it_["yv"][:, hh * 512:(hh + 1) * 512]:
                         e.bn_stats(out=o_, in_=i_), reads=[it_["Y"]], writes=[it_["ST"]], multi=True)
            for it_ in items:
                P.op("dve", lambda e, o_=it_["mv"][:, :], i_=it_["st"][:, :, :]: e.bn_aggr(out=o_, in_=i_),
                     reads=[it_["ST"]], writes=[it_["MV"]])
            for it_ in items:
                ts("pool", it_["rs"][:, :], it_["mv"][:, 1:2], epsb[:, 0:1], None, ALU.add, ALU.bypass,
                   [it_["MV"], CC], [it_["RS"]])
            for it_ in items:
                tt("pool", it_["rs"][:, :], it_["rs"][:, :], mhalf[:, :], ALU.pow, [it_["RS"], CC], [it_["RS"]])
            for it_ in items:
                stt(it_["nb"][:, :], it_["mv"][:, 0:1], -1.0, it_["rs"][:, :], ALU.mult, ALU.mult,
                    [it_["MV"], it_["RS"]], [it_["NB"]])
            for it_ in items:
                act(it_["yv"], it_["yv"], AF.Identity, [it_["Y"], it_["RS"], it_["NB"]], [it_["Y"]],
                    bias=it_["nb"][:, 0:1], scale=it_["rs"][:, 0:1])
            for it_ in items:
                tt("dve", it_["yv"], it_["yv"], gam, ALU.mult, [it_["Y"], LNB], [it_["Y"]])
            for it_ in items:
                tt("dve", it_["yv"], it_["yv"], bet, ALU.add, [it_["Y"], LNB], [it_["Y"]])

        def c1_chain2(t, js):
            items = []
            for j in js:
                i = j % 2
                blk = t * 4 + j
                dma("sp", xin[i][:, :], x_h[blk * 128:(blk + 1) * 128, :], writes=[XIN[i]], key=("xin", i))
            for j in js:
                i = j % 2
                rb = 4 + 2 * (j % 2)
                stt(xin[i][:, :], xin[i][:, :], ALPHA, ps[:, rb:rb + 2, :].rearrange("p a n -> p (a n)"),
                    ALU.mult, ALU.add, [XIN[i], PS[rb], PS[rb + 1]], [XIN[i]])
                items.append(dict(yv=xin[i][:, :], Y=XIN[i], st=st[i], mv=mv[i], rs=rs[i], nb=nbt[i],
                                  ST=STt[i], MV=MV[i], RS=RS[i], NB=NBT[i]))
            ln_multi(items, g1[:, :], b1[:, :], LN1)
            for j in js:
                i = j % 2
                blk = t * 4 + j
                dma("sp", x1_s[blk * 128:(blk + 1) * 128, :], xin[i][:, :], reads=[XIN[i]], key=("x1s", i))
                cp("act", x1b[j][:, :], xin[i][:, :], [XIN[i]], [X1B[j]])

        def c1_T(t, j):
            blk = t * 4 + j
            tb = 4 + j
            tpb = ps[:, tb, :].bitcast(BF16)
            for c in range(8):
                P.op("pe", lambda e, o_=tpb[:, c * 128:(c + 1) * 128], i_=x1b[j][:, c * 128:(c + 1) * 128],
                     id_=cb[:, C_ID:C_ID + 128]: e.transpose(out=o_, in_=i_, identity=id_),
                     reads=[X1B[j], CB], writes=[PS[tb]])
            cp("dve", xT[:, :, blk * 128:(blk + 1) * 128], tpb.rearrange("p (c n) -> p c n", c=8),
               [PS[tb]], [XT[t]])

        gi = 0
        bi = 0

        def wg_load(n):
            if n < NT * 8:
                dma("pool", wgc[n % 2][:, :, :], wg_h[n % 8, :, :, :], writes=[WG[n % 2]], key=("wg", n % 2))
        wg_load(0)
        for t in range(NT):
            sl = slice(t * 512, (t + 1) * 512)
            for c in range(8):
                wi = gi % 2
                gi2 = gi % 2
                gi += 1
                gb = 0
                for ab in range(2):
                    for k in range(8):
                        mm(ps[:, gb + ab, :], wgc[wi][:, k, ab * 128:(ab + 1) * 128], xT[:, k, sl], k == 0, k == 7,
                           [WG[wi], XT[t]], [PS[gb + ab]])
                    act(gt[gi2][:, ab, :], ps[:, gb + ab, :], AF.Sigmoid, [PS[gb + ab], BG], [GT[gi2]],
                        bias=bg[:, ab * 8 + c:ab * 8 + c + 1])
                wg_load(gi)
                for k in range(4):
                    mm(ps[:, 2, :], wa[:, k, c * 128:(c + 1) * 128], yAT[:, k, sl], k == 0, k == 3, [WAa, YA[t]], [PS[2]])
                for k in range(4):
                    mm(ps[:, 3, :], wb[:, k, c * 128:(c + 1) * 128], yBT[:, k, sl], k == 0, k == 3, [WBb, YB[t]], [PS[3]])
                tt("dve", gt[gi2][:, 0, :], gt[gi2][:, 0, :], ps[:, 2, :], ALU.mult, [GT[gi2], PS[2]], [GT[gi2]])
                tt("dve", gt[gi2][:, 1, :], gt[gi2][:, 1, :], ps[:, 3, :], ALU.mult, [GT[gi2], PS[3]], [GT[gi2]])
                if t > 0 and c % 2 == 1:
                    c1_T(t - 1, c // 2)
                tt("dve", hT[:, c, :], gt[gi2][:, 0, :], gt[gi2][:, 1, :], ALU.add, [GT[gi2]], [HT[c]])
            c1_R(t, 0)
            c1_R(t, 1)
            c1_chain2(t, (0, 1))
            c1_R(t, 2)
            c1_R(t, 3)
            c1_chain2(t, (2, 3))
        for j in range(4):
            c1_T(NT - 1, j)
        P.barrier()

        o = LOC2
        TQ = min(1024, T)
        NQt = T // TQ
        HPQ = TQ // 512
        wdn = M.at(o, [128, NJ, 1024], BF16); o += NJ * 2048
        WDN = Buf("wdn")
        hid = M.at(o, [128, NJ, TQ], BF16); o += NJ * TQ * 2
        HID = [[Buf("hid%d_%d" % (j, hh)) for hh in range(HPQ)] for j in range(NJ)]
        g2 = M.at(o, [128, D], F32); o += 4096
        b2 = M.at(o, [128, D], F32); o += 4096
        LN2 = Buf("ln2")
        cpm = M.at(o, [128, 2 * NJ, 4], F32); o += 2 * NJ * 16
        CPM = Buf("cpm")
        halo = M.at(o, [128, 2 * NJ, 2], F32); o += 2 * NJ * 8
        HALO = [Buf("halo%d" % jj) for jj in range(2 * NJ)]
        o = (o + 31) // 32 * 32
        wupc = [M.at(o + i * 4096, [128, 8, 256], BF16) for i in range(3)]; o += 12288
        WUP = [Buf("wup%d" % i) for i in range(3)]
        U = [[M.at(o + (i * 2 + g) * 2080, [128, 514], F32) for g in range(2)] for i in range(2)]; o += 4 * 2080
        UB = [[Buf("U%d_%d" % (i, g)) for g in range(2)] for i in range(2)]
        Aa = [[M.at(o + (i * 2 + g) * 2048, [128, 512], F32) for g in range(2)] for i in range(2)]; o += 4 * 2048
        AB = [[Buf("A%d_%d" % (i, g)) for g in range(2)] for i in range(2)]
        xin = [M.at(o + i * 4096, [128, D], F32) for i in range(2)]; o += 8192
        XIN = [Buf("xin2_%d" % i) for i in range(2)]
        st = [M.at(o + i * 64, [128, 2, 6], F32) for i in range(2)]; o += 128
        mv = [M.at(o + i * 32, [128, 2], F32) for i in range(2)]; o += 64
        rs = [M.at(o + i * 32, [128, 1], F32) for i in range(2)]; o += 64
        nbt = [M.at(o + i * 32, [128, 1], F32) for i in range(2)]; o += 64
        NBT = [Buf("nb2%d" % i) for i in range(2)]
        epsb = M.at(o, [128, 1], F32); o += 32
        mhalf = M.at(o, [128, 1], F32); o += 32
        STt = [Buf("st2%d" % i) for i in range(2)]
        MV = [Buf("mv2%d" % i) for i in range(2)]
        RS = [Buf("rs2%d" % i) for i in range(2)]
        CC = Buf("cc2")
        assert o <= 212992, o

        for j in range(0, NJ, 2):
            dma("pool", wdn[:, j:j + 2, :], wdn_h[:, j:j + 2, :], writes=[WDN])
        dma("sp", g2[:, :], ln_h[2], writes=[LN2])
        dma("sp", b2[:, :], ln_h[3], writes=[LN2])
        dma("sp", cpm[:, :, :], cp_h[:, :, :], writes=[CPM])
        ms(epsb[:, :], EPS, [CC])
        ms(mhalf[:, :], -0.5, [CC])
        ms(halo[:, :, :], 0.0, HALO)

        ui = 0
        wi_ = 0
        bi = 0
        wseq = [(q, j) for q in range(NQt) for j in range(NJ)]

        def wup_load(n):
            if n < len(wseq):
                dma("pool", wupc[n % 3][:, :, :], wup_h[wseq[n][1], :, :, :], writes=[WUP[n % 3]], key=("wup", n % 3))
        wup_load(0)
        wup_load(1)
        def ln_front(it_):
            for hh in range(2):
                P.op("dve", lambda e, o_=it_["st"][:, hh, :], i_=it_["yv"][:, hh * 512:(hh + 1) * 512]:
                     e.bn_stats(out=o_, in_=i_), reads=[it_["Y"]], writes=[it_["ST"]], multi=True)
            P.op("dve", lambda e, o_=it_["mv"][:, :], i_=it_["st"][:, :, :]: e.bn_aggr(out=o_, in_=i_),
                 reads=[it_["ST"]], writes=[it_["MV"]])
            ts("pool", it_["rs"][:, :], it_["mv"][:, 1:2], epsb[:, 0:1], None, ALU.add, ALU.bypass,
               [it_["MV"], CC], [it_["RS"]])
            tt("pool", it_["rs"][:, :], it_["rs"][:, :], mhalf[:, :], ALU.pow, [it_["RS"], CC], [it_["RS"]])
            stt(it_["nb"][:, :], it_["mv"][:, 0:1], -1.0, it_["rs"][:, :], ALU.mult, ALU.mult,
                [it_["MV"], it_["RS"]], [it_["NB"]])
            act(it_["yv"], it_["yv"], AF.Identity, [it_["Y"], it_["RS"], it_["NB"]], [it_["Y"]],
                bias=it_["nb"][:, 0:1], scale=it_["rs"][:, 0:1])

        def ln_back(it_, gam, bet, LNB):
            tt("dve", it_["yv"], it_["yv"], gam, ALU.mult, [it_["Y"], LNB], [it_["Y"]])
            tt("dve", it_["yv"], it_["yv"], bet, ALU.add, [it_["Y"], LNB], [it_["Y"]])

        pend = [None]

        def flush_pend():
            if pend[0] is not None:
                o_, a_, b_, rd, wr = pend[0]
                tt("pool", o_, a_, b_, ALU.mult, rd, wr)
                pend[0] = None

        def down_front(q, jb):
            blk = q * (TQ // 128) + jb
            i = blk % 2
            rb = 4 + 2 * i
            dma("sp", xin[i][:, :], x1_s[blk * 128:(blk + 1) * 128, :], writes=[XIN[i]], key=("xin", i))
            for hf in range(2):
                for j in range(NJ):
                    mm(ps[:, rb + hf, :], hid[:, j, jb * 128:(jb + 1) * 128], wdn[:, j, hf * 512:(hf + 1) * 512],
                       j == 0, j == NJ - 1, [HID[j][jb // 4], WDN], [PS[rb + hf]])
            stt(xin[i][:, :], xin[i][:, :], ALPHA, ps[:, rb:rb + 2, :].rearrange("p a n -> p (a n)"),
                ALU.mult, ALU.add, [XIN[i], PS[rb], PS[rb + 1]], [XIN[i]])
            it_ = dict(yv=xin[i][:, :], Y=XIN[i], st=st[i], mv=mv[i], rs=rs[i], nb=nbt[i],
                       ST=STt[i], MV=MV[i], RS=RS[i], NB=NBT[i], blk=blk, i=i)
            ln_front(it_)
            return it_

        def down_back(it_):
            ln_back(it_, g2[:, :], b2[:, :], LN2)
            blk, i = it_["blk"], it_["i"]
            dma("sp", out_h[blk * 128:(blk + 1) * 128, :], xin[i][:, :], reads=[XIN[i]], key=("out", i))

        for q in range(NQt):
            for j in range(NJ):
                wi = wi_ % 3
                wup_load(wi_ + 2)
                wi_ += 1
                for hh in range(HPQ):
                    tI = q * HPQ + hh
                    sl = slice(tI * 512, (tI + 1) * 512)
                    i = ui % 2
                    ui += 1
                    for g in range(2):
                        jj = g * NJ + j
                        b = 2 * i + g
                        for k in range(8):
                            mm(ps[:, b, :], wupc[wi][:, k, g * 128:(g + 1) * 128], xT[:, k, sl], k == 0, k == 7,
                               [WUP[wi], XT[tI]], [PS[b]])
                        cp("pool", U[i][g][:, 0:2], halo[:, jj, :], [HALO[jj]], [UB[i][g]])
                        P.op("act", lambda e, o_=U[i][g][:, 2:514], i_=ps[:, b, :]: e.activation(out=o_, in_=i_, func=AF.Copy),
                             reads=[PS[b]], writes=[UB[i][g]], multi=True)
                        act(Aa[i][g][:, :], ps[:, b, :], AF.Identity, [PS[b], CPM], [AB[i][g]],
                            bias=cpm[:, jj, 3:4], scale=cpm[:, jj, 2:3])
                        cp("pool", halo[:, jj, :], U[i][g][:, 512:514], [UB[i][g]], [HALO[jj]])
                        stt(Aa[i][g][:, :], U[i][g][:, 1:513], cpm[:, jj, 1:2], Aa[i][g][:, :], ALU.mult, ALU.add,
                            [UB[i][g], CPM, AB[i][g]], [AB[i][g]])
                        stt(Aa[i][g][:, :], U[i][g][:, 0:512], cpm[:, jj, 0:1], Aa[i][g][:, :], ALU.mult, ALU.add,
                            [UB[i][g], CPM, AB[i][g]], [AB[i][g]])
                    flush_pend()
                    act(Aa[i][0][:, :], Aa[i][0][:, :], AF.Silu, [AB[i][0]], [AB[i][0]])
                    pend[0] = (hid[:, j, hh * 512:(hh + 1) * 512], Aa[i][0][:, :], Aa[i][1][:, :],
                               [AB[i][0], AB[i][1]], [HID[j][hh]])
            flush_pend()
            prev = None
            for jb in range(TQ // 128):
                cur = down_front(q, jb)
                if prev is not None:
                    down_back(prev)
                prev = cur
            down_back(prev)
        P.barrier()
        P.emit(nc, stack)
    return nc


def host_prep(b, x, positions, w_in, b_gate, sinks, w_branch_a, w_branch_b, w_out,
              ln1_g, ln1_b, w_up, conv_w, conv_b, w_down, ln2_g, ln2_b, shared):
    m = dict(shared)
    m["xT"] = np.ascontiguousarray(x[b].T)
    m["x"] = np.ascontiguousarray(x[b])
    m["pos"] = np.ascontiguousarray(np.broadcast_to(positions[b][None, :].astype(np.int32), (128, positions.shape[1])))
    return m


def host_shared(w_in, b_gate, sinks, w_branch_a, w_branch_b, w_out,
                ln1_g, ln1_b, w_up, conv_w, conv_b, w_down, ln2_g, ln2_b):
    f = np.float32
    w_in = w_in[0]
    cbm = np.zeros((128, NCB), f)
    cbm[:, C_ID:C_ID + 128] = np.eye(128, dtype=f)
    rot = np.zeros((128, 128), f)
    for m_ in range(128):
        base, ml = (m_ // 64) * 64, m_ % 64
        if ml < 32:
            rot[base + ml + 32, m_] = -1.0
        else:
            rot[base + ml - 32, m_] = 1.0
    cbm[:, C_ROT:C_ROT + 128] = rot
    jj, ss = np.meshgrid(np.arange(128), np.arange(128), indexing="ij")
    cbm[:, C_TRI:C_TRI + 128] = np.where(jj >= ss, -8.0, 0.0)
    cbm[:, C_ONE:C_ONE + 128] = -8.0
    s_, t_ = np.meshgrid(np.arange(128), np.arange(128), indexing="ij")
    msb = (s_ < t_).astype(f)
    cbm[:, C_MSB:C_MSB + 128] = msb
    cbm[:, C_MSB + 128:C_MSB + 256] = msb
    mswa = np.concatenate([(s_ <= t_).astype(f), (s_ > t_).astype(f)], axis=1)
    cbm[:, C_MSWA:C_MSWA + 256] = mswa
    cbm[:, C_MSWA + 256:C_MSWA + 512] = mswa
    cbm[:, C_O64:C_O64 + 64] = 1.0
    cfm = np.zeros((128, 8), f)
    inv = (f(1.0) / np.power(f(10000.0), np.arange(0, 64, 2, dtype=f) / f(64.0))).astype(f)
    cfm[:, 0] = inv[np.arange(128) % 32]
    def kp(w):
        K, N = w.shape
        return np.ascontiguousarray(w.reshape(K // 128, 128, N).transpose(1, 0, 2))
    qa_cols = np.concatenate([np.r_[c * 64:(c + 1) * 64, (4 + c) * 64:(5 + c) * 64] for c in range(4)])
    wA = np.concatenate([w_in[:, 0:512][:, qa_cols], w_in[:, 512:640], w_in[:, 640:768]], axis=1)
    wB = np.stack([np.concatenate([w_in[:, 768 + hp * 128:768 + (hp + 1) * 128],
                                   w_in[:, 1280 + hp * 128:1280 + (hp + 1) * 128],
                                   w_in[:, 1792 + hp * 128:1792 + (hp + 1) * 128]], axis=1) for hp in range(4)])
    wgA, wgB = w_in[:, 2304:3328], w_in[:, 3328:4352]
    wg = np.stack([np.concatenate([wgA[:, c * 128:(c + 1) * 128], wgB[:, c * 128:(c + 1) * 128]], axis=1)
                   for c in range(8)])
    frow = np.array([[(c if p < 64 else 4 + c) * 64 + p % 64 for c in range(4)] for p in range(128)])
    wa = np.ascontiguousarray(w_branch_a[0][frow, :])
    wup = w_up[0]
    wupr = np.stack([np.concatenate([wup[:, j * 128:(j + 1) * 128], wup[:, DFF + j * 128:DFF + (j + 1) * 128]], axis=1)
                     for j in range(NJ)])
    cpm = np.zeros((128, 2 * NJ, 4), f)
    cw, cbias = conv_w[0], conv_b[0]
    for jj_ in range(2 * NJ):
        ch = jj_ * 128 + np.arange(128)
        cpm[:, jj_, 0:3] = cw[:, ch].T
        cpm[:, jj_, 3] = cbias[ch]
    bgm = np.zeros((128, 16), f)
    for c in range(8):
        bgm[:, c] = b_gate[0][c * 128:(c + 1) * 128]
        bgm[:, 8 + c] = b_gate[0][1024 + c * 128:1024 + (c + 1) * 128]
    skm = np.zeros((128, 4), f)
    for c in range(4):
        skm[0:64, c] = sinks[0][c]
        skm[64:128, c] = sinks[0][4 + c]
    lnm = np.stack([np.broadcast_to(v[0][None, :], (128, D)) for v in (ln1_g, ln1_b, ln2_g, ln2_b)]).astype(f)
    return dict(
        cb=cbm, cf=cfm,
        wA=kp(wA), wB=np.stack([kp(wB[hp]) for hp in range(4)]),
        wg=np.stack([kp(wg[c]) for c in range(8)]),
        wa=wa, wb=kp(w_branch_b[0]), wo=kp(w_out[0]),
        wup=np.stack([kp(wupr[j]) for j in range(NJ)]), wdn=kp(w_down[0]),
        cp=cpm, bg=bgm, sk=skm, ln=np.ascontiguousarray(lnm),
    )


_NC_CACHE = {}


def kernel(x, positions, w_in, b_gate, sinks, w_branch_a, w_branch_b, w_out,
           ln1_g, ln1_b, w_up, conv_w, conv_b, w_down, ln2_g, ln2_b):
    args = [np.asarray(a) for a in (x, positions, w_in, b_gate, sinks, w_branch_a, w_branch_b, w_out,
                                    ln1_g, ln1_b, w_up, conv_w, conv_b, w_down, ln2_g, ln2_b)]
    x, positions = args[0], args[1]
    B, T, _ = x.shape
    shared = host_shared(*args[2:])
    in_maps = [host_prep(b, x, positions, *args[2:], shared) for b in range(B)]
    nc = build(T)
    res = run_bass_kernel_spmd(nc, in_maps, core_ids=list(range(B)))
    return np.stack([np.asarray(r["out"]) for r in res.results]).astype(np.float32)
```

```python
import numpy as np
import concourse.bass as bass
import concourse.mybir as mybir
from concourse.bass_utils import run_bass_kernel_spmd

F32 = mybir.dt.float32
BF16 = mybir.dt.bfloat16
I32 = mybir.dt.int32
AF = mybir.ActivationFunctionType
ALU = mybir.AluOpType

D = 1024
DFF = 2816
NJ = DFF // 128
HD = 64
ALPHA = float(2.0 ** 0.25)
EPS = 1e-5
PI = float(np.pi)
INV2PI = float(np.float32(1.0 / (2.0 * np.pi)))
MAGIC = 12582912.0
CW1 = 6.28125
CW2 = float(2.0 * np.pi - 6.28125)
SBUF_BASE = 16384

C_ID, C_ROT, C_TRI, C_ONE, C_MSB, C_MSWA, C_O64, NCB = 0, 128, 256, 384, 512, 768, 1280, 1344


class Buf:
    __slots__ = ("name", "w", "r", "psum")

    def __init__(self, name, psum=False):
        self.name = name
        self.w = []
        self.r = []
        self.psum = psum


class Prog:
    ENGS = ("pe", "act", "dve", "pool", "sp")

    def __init__(self):
        self.ops = []
        self.bar_start = 0

    def op(self, eng, fn, reads=(), writes=(), dma=None, multi=False):
        i = len(self.ops)
        deps = set()
        for b in reads:
            deps.update(b.w)
            if b.psum:
                deps.update(r for r in b.r if self.ops[r]["eng"] != eng)
        for b in writes:
            if not multi:
                deps.update(b.w)
            deps.update(b.r)
        for b in reads:
            b.r.append(i)
        for b in writes:
            if multi and not b.r:
                b.w = b.w + [i]
            else:
                b.w = [i]
            b.r = []
        deps.discard(i)
        last = {}
        keep = set()
        for d in deps:
            od = self.ops[d]
            if od["dma"] is not None:
                keep.add(d)
            elif last.get(od["eng"], -1) < d:
                last[od["eng"]] = d
        keep.update(last.values())
        self.ops.append(dict(eng=eng, fn=fn, deps=keep, dma=dma))
        return i

    def barrier(self):
        last = {}
        deps = set()
        for idx in range(self.bar_start, len(self.ops)):
            o = self.ops[idx]
            if o["fn"] is None:
                continue
            if o["dma"] is not None:
                deps.add(idx)
            else:
                last[o["eng"]] = idx
        deps.update(last.values())
        for e in self.ENGS:
            self.ops.append(dict(eng=e, fn=None, deps=set(deps), dma=None))
        self.bar_start = len(self.ops)

    def emit(self, nc, stack):
        ops = self.ops
        need = set()
        for o in ops:
            for d in o["deps"]:
                od = ops[d]
                if o["eng"] == "pe" and od["eng"] == "pe" and od["dma"] is None:
                    continue
                need.add(d)
        esem = {e: stack.enter_context(nc.semaphore("s_" + e)) for e in self.ENGS}
        dsem = {}
        tick = {e: 0 for e in self.ENGS}
        dcnt = {}
        sig = {}
        for i, o in enumerate(ops):
            if i not in need or o["fn"] is None:
                continue
            if o["dma"] is not None:
                k = o["dma"]
                if k not in dsem:
                    dsem[k] = stack.enter_context(nc.semaphore("d_%d" % len(dsem)))
                    dcnt[k] = 0
                dcnt[k] += 16
                sig[i] = (dsem[k], dcnt[k], 16)
            else:
                tick[o["eng"]] += 1
                sig[i] = (esem[o["eng"]], tick[o["eng"]], 1)
        per = {e: [] for e in self.ENGS}
        for i, o in enumerate(ops):
            per[o["eng"]].append(i)
        self.stats = dict(ticks=dict(tick), nsem=len(dsem) + len(esem), nops=len(ops),
                          dmax=max(dcnt.values()) if dcnt else 0)
        block = stack.enter_context(nc.Block())

        def run(eng, ename):
            waited = {}
            for i in per[ename]:
                o = ops[i]
                for d in sorted(o["deps"]):
                    if d not in sig:
                        continue
                    od = ops[d]
                    if ename == "pe" and od["eng"] == "pe" and od["dma"] is None:
                        continue
                    sem, val, _ = sig[d]
                    key = id(sem)
                    if waited.get(key, 0) >= val:
                        continue
                    eng.wait_ge(sem, val)
                    waited[key] = val
                if o["fn"] is not None:
                    ins = o["fn"](eng)
                    if i in sig:
                        ins.then_inc(sig[i][0], sig[i][2])

        @block.tensor
        def _(e):
            run(e, "pe")

        @block.scalar
        def _(e):
            run(e, "act")

        @block.vector
        def _(e):
            run(e, "dve")

        @block.gpsimd
        def _(e):
            run(e, "pool")

        @block.sync
        def _(e):
            run(e, "sp")


class Mem:
    def __init__(self, nc):
        self.nc = nc
        self.n = 0

    def at(self, off, shape, dt):
        self.n += 1
        return self.nc.alloc_sbuf_tensor_at("t%d" % self.n, list(shape), dt, offset=SBUF_BASE + off).ap()


def build(T=4096, debug=False):
    NB = T // 128
    NT = T // 512
    nc = bass.Bass("TRN2", target_bir_lowering=False)
    P = Prog()
    M = Mem(nc)

    def din(name, shape, dt=F32):
        return nc.dram_tensor(name, list(shape), dt, kind="ExternalInput").ap()

    xT_h = din("xT", [D, T])
    x_h = din("x", [T, D])
    pos_h = din("pos", [128, T], I32)
    cb_h = din("cb", [128, NCB])
    cf_h = din("cf", [128, 8])
    wA_h = din("wA", [128, 8, 768])
    wB_h = din("wB", [4, 128, 8, 384])
    wg_h = din("wg", [8, 128, 8, 256])
    wa_h = din("wa", [128, 4, 1024])
    wb_h = din("wb", [128, 4, 1024])
    wo_h = din("wo", [128, 8, 1024])
    wup_h = din("wup", [NJ, 128, 8, 256])
    wdn_h = din("wdn", [128, NJ, 1024])
    cp_h = din("cp", [128, 2 * NJ, 4])
    bg_h = din("bg", [128, 16])
    sk_h = din("sk", [128, 4])
    ln_h = din("ln", [4, 128, D])
    out_h = nc.dram_tensor("out", [T, D], F32, kind="ExternalOutput").ap()
    x1_s = nc.dram_tensor("x1s", [T, D], F32, kind="Internal").ap()
    rt_s = nc.dram_tensor("rts", [2, 128, T], F32, kind="Internal").ap()
    if debug:
        dbgA = nc.dram_tensor("dbgA", [128, 4, T], F32, kind="ExternalOutput").ap()
        dbgB = nc.dram_tensor("dbgB", [128, 4, T], F32, kind="ExternalOutput").ap()

    import contextlib
    stack = contextlib.ExitStack()
    with stack:
        ps = stack.enter_context(nc.psum_tensor([128, 8, 512], F32))
        PS = [Buf("ps%d" % b, psum=True) for b in range(8)]

        o = 0
        cb = M.at(o, [128, NCB], BF16); o += 2 * NCB
        CB = Buf("cb")
        cf = M.at(o, [128, 8], F32); o += 32
        CF = Buf("cf")
        xT = M.at(o, [128, 8, T], BF16); o += 16 * T
        XT = [Buf("xT%d" % t) for t in range(NT)]
        LOC2 = o
        yAT = M.at(o, [128, 4, T], BF16); o += 8 * T
        YA = [Buf("yA%d" % t) for t in range(NT)]
        yBT = M.at(o, [128, 4, T], BF16); o += 8 * T
        YB = [Buf("yB%d" % t) for t in range(NT)]
        LOC = o
        assert LOC % 32 == 0

        def dma(eng, out, in_, reads=(), writes=(), key=None, multi=True):
            if key is None:
                key = ("w", writes[0].name) if writes else ("r", reads[0].name)
            return P.op(eng, lambda e, out=out, in_=in_: e.dma_start(out=out, in_=in_),
                        reads=reads, writes=writes, dma=key, multi=multi)

        def ms(ap, val, writes):
            return P.op("pool", lambda e, ap=ap, val=val: e.memset(ap, val), writes=writes)

        def mm(out, lhsT, rhs, start, stop, reads, writes):
            return P.op("pe", lambda e, out=out, lhsT=lhsT, rhs=rhs, start=start, stop=stop:
                        e.matmul(out, lhsT=lhsT, rhs=rhs, start=start, stop=stop, skip_group_check=True),
                        reads=reads, writes=writes)

        def act(out, in_, func, reads, writes, bias=0.0, scale=1.0):
            return P.op("act", lambda e, out=out, in_=in_, func=func, bias=bias, scale=scale:
                        e.activation(out=out, in_=in_, func=func, bias=bias, scale=scale),
                        reads=reads, writes=writes)

        def tt(eng, out, in0, in1, op, reads, writes):
            return P.op(eng, lambda e, out=out, in0=in0, in1=in1, op=op:
                        e.tensor_tensor(out=out, in0=in0, in1=in1, op=op), reads=reads, writes=writes)

        def ts(eng, out, in0, s1, s2, op0, op1, reads, writes):
            return P.op(eng, lambda e, out=out, in0=in0, s1=s1, s2=s2, op0=op0, op1=op1:
                        e.tensor_scalar(out=out, in0=in0, scalar1=s1, scalar2=s2, op0=op0, op1=op1),
                        reads=reads, writes=writes)

        def stt(out, in0, scalar, in1, op0, op1, reads, writes):
            return P.op("dve", lambda e, out=out, in0=in0, scalar=scalar, in1=in1, op0=op0, op1=op1:
                        e.scalar_tensor_tensor(out=out, in0=in0, scalar=scalar, in1=in1, op0=op0, op1=op1),
                        reads=reads, writes=writes)

        def cp(eng, out, in_, reads, writes):
            if eng == "act":
                return act(out, in_, AF.Copy, reads, writes)
            return P.op(eng, lambda e, out=out, in_=in_: e.tensor_copy(out=out, in_=in_),
                        reads=reads, writes=writes)

        dma("pool", cb[:, :], cb_h[:, :], writes=[CB])
        dma("sp", cf[:, :], cf_h[:, :], writes=[CF])
        xv = xT_h.rearrange("(c p) t -> p c t", p=128)

        def load_xT(t):
            dma("pool", xT[:, :, t * 512:(t + 1) * 512], xv[:, :, t * 512:(t + 1) * 512], writes=[XT[t]])
        load_xT(0)

        o = LOC
        QTA = M.at(o, [128, 2, T], BF16); o += 4 * T
        QA = [[Buf("qa%d_%d" % (c, t)) for t in range(NT)] for c in range(4)]
        KTA = M.at(o, [128, T], BF16); o += 2 * T
        KA = [Buf("ka%d" % t) for t in range(NT)]
        VA = M.at(o, [128, NB, 128], BF16); o += 2 * T
        VAb = [Buf("va%d" % t) for t in range(NT)]
        sk = M.at(o, [128, 4], F32); o += 32
        SK = Buf("sk")
        npi = M.at(o, [128, 1], F32); o += 32
        NPI = Buf("npi")
        A2 = o
        wA = M.at(o, [128, 8, 768], BF16); o += 8 * 768 * 2
        WA = Buf("wA")
        ang = M.at(o, [128, 512], F32); o += 2048
        tmpa = M.at(o, [128, 512], F32); o += 2048
        tmpb = M.at(o, [128, 512], F32); o += 2048
        posi = tmpb.bitcast(I32)
        TMPB = Buf("tmpb")
        cosTs = [M.at(o + i * 2048, [128, 512], F32) for i in range(2)]; o += 4096
        sinTs = [M.at(o + i * 2048, [128, 512], F32) for i in range(2)]; o += 4096
        COSs = [Buf("cos%d" % i) for i in range(2)]
        SINs = [Buf("sin%d" % i) for i in range(2)]
        ANG, TMPA = Buf("ang"), Buf("tmpa")
        POSI = TMPB
        q32 = [M.at(o + i * 2048, [128, 512], F32) for i in range(2)]; o += 4096
        qb = [M.at(o + i * 1024, [128, 512], BF16) for i in range(2)]; o += 2048
        t2 = [M.at(o + i * 2048, [128, 512], F32) for i in range(2)]; o += 4096
        Q32 = [Buf("q32_%d" % i) for i in range(2)]
        QB = [Buf("qb_%d" % i) for i in range(2)]
        T2 = [Buf("t2_%d" % i) for i in range(2)]
        assert o <= 212992, o

        for k in range(0, 8, 2):
            dma("pool", wA[:, k:k + 2, :], wA_h[:, k:k + 2, :], writes=[WA])
        for t in range(1, NT):
            load_xT(t)
        dma("sp", sk[:, :], sk_h[:, :], writes=[SK])
        ms(npi[:, :], -PI, [NPI])
        act(sk[:, :], sk[:, :], AF.Exp, [SK], [SK])

        ctr = [0]

        def rope_proj(wcols, dst, dstbuf, t, cosT, sinT, COS, SIN):
            i = ctr[0] % 2
            ctr[0] += 1
            b0, b1 = 2 * i, 2 * i + 1
            for k in range(8):
                mm(ps[:, b0, :], wA[:, k, wcols], xT[:, k, t * 512:(t + 1) * 512], k == 0, k == 7,
                   [WA, XT[t]], [PS[b0]])
            act(q32[i][:, :], ps[:, b0, :], AF.Copy, [PS[b0]], [Q32[i]])
            cp("act", qb[i][:, :], ps[:, b0, :], [PS[b0]], [QB[i]])
            mm(ps[:, b1, :], cb[:, C_ROT:C_ROT + 128], qb[i][:, :], True, True, [CB, QB[i]], [PS[b1]])
            tt("dve", q32[i][:, :], q32[i][:, :], cosT[:, :], ALU.mult, [Q32[i], COS], [Q32[i]])
            tt("dve", t2[i][:, :], ps[:, b1, :], sinT[:, :], ALU.mult, [PS[b1], SIN], [T2[i]])
            tt("dve", dst, q32[i][:, :], t2[i][:, :], ALU.add, [Q32[i], T2[i]], [dstbuf])

        Pt = [M.at(o + i * 1024, [128, 2, 256], BF16) for i in range(2)]; o += 2048
        PT = [Buf("pt%d" % i) for i in range(2)]
        dn = [M.at(o + i * 512, [128, 128], F32) for i in range(2)]; o += 1024
        DN = [Buf("dn%d" % i) for i in range(2)]
        assert o <= 212992, o
        msw = cb[:, C_MSWA:C_MSWA + 512].rearrange("p (h u) -> p h u", h=2)
        o64 = cb[:, C_O64:C_O64 + 64]
        for half in range(2):
            for t in range(NT):
                sl = slice(t * 512, (t + 1) * 512)
                cosT, sinT, COS, SIN = cosTs[t % 2], sinTs[t % 2], COSs[t % 2], SINs[t % 2]
                if half == 0:
                    dma("sp", posi[:, :], pos_h[:, sl], writes=[POSI])
                    cp("dve", ang[:, :], posi[:, :], [POSI], [ANG])
                    ts("dve", ang[:, :], ang[:, :], cf[:, 0:1], None, ALU.mult, ALU.bypass, [ANG, CF], [ANG])
                    def sin_of(base_ap, BASE, dst, DST):
                        ts("dve", tmpa[:, :], base_ap, INV2PI, MAGIC, ALU.mult, ALU.add, [BASE], [TMPA])
                        ts("dve", tmpa[:, :], tmpa[:, :], MAGIC, None, ALU.subtract, ALU.bypass, [TMPA], [TMPA])
                        stt(tmpb[:, :], tmpa[:, :], -CW1, base_ap, ALU.mult, ALU.add, [TMPA, BASE], [TMPB])
                        stt(tmpb[:, :], tmpa[:, :], -CW2, tmpb[:, :], ALU.mult, ALU.add, [TMPA, TMPB], [TMPB])
                        ts("dve", tmpb[:, :], tmpb[:, :], 3.141592, -3.141592, ALU.min, ALU.max, [TMPB], [TMPB])
                        act(dst, tmpb[:, :], AF.Sin, [TMPB], [DST])
                    sin_of(ang[:, :], ANG, sinT[:, :], SIN)
                    ts("dve", ang[:, :], ang[:, :], 0.5 * PI, None, ALU.add, ALU.bypass, [ANG], [ANG])
                    sin_of(ang[:, :], ANG, cosT[:, :], COS)
                    dma("sp", rt_s[0, :, sl], sinT[:, :], reads=[SIN], key=("rts", 0, t % 2))
                    dma("sp", rt_s[1, :, sl], cosT[:, :], reads=[COS], key=("rts", 1, t % 2))
                else:
                    dma("sp", sinT[:, :], rt_s[0, :, sl], writes=[SIN], key=("rtl", 0, t % 2))
                    dma("sp", cosT[:, :], rt_s[1, :, sl], writes=[COS], key=("rtl", 1, t % 2))
                if half == 0:
                    rope_proj(slice(512, 640), KTA[:, sl], KA[t], t, cosT, sinT, COS, SIN)
                for c in (2 * half, 2 * half + 1):
                    rope_proj(slice(c * 128, (c + 1) * 128), QTA[:, c % 2, sl], QA[c][t], t, cosT, sinT, COS, SIN)
                for j in range(4 if half == 0 else 0):
                    blk = t * 4 + j
                    for k in range(8):
                        mm(ps[:, 4, j * 128:(j + 1) * 128], xT[:, k, blk * 128:(blk + 1) * 128], wA[:, k, 640:768],
                           k == 0, k == 7, [XT[t], WA], [PS[4]])
                if half == 0:
                    cp("act", VA[:, t * 4:(t + 1) * 4, :], ps[:, 4, :].rearrange("p (j d) -> p j d", j=4), [PS[4]], [VAb[t]])

            P.barrier()
            asteps = [(c, kb) for c in (2 * half, 2 * half + 1) for kb in range(NB)]

            def a_front(n):
                c, kb = asteps[n]
                N = 256 if kb < NB - 1 else 128
                i = n % 2
                b0 = 2 * i
                t0 = kb * 128
                tq = [QA[c][(t0) // 512]] + ([QA[c][(t0 + 128) // 512]] if N == 256 else [])
                for h in range(2):
                    r = slice(64 * h, 64 * h + 64)
                    mm(ps[:, b0 + h, 0:N], KTA[r, t0:t0 + 128], QTA[r, c % 2, t0:t0 + N], True, True,
                       [KA[kb // 4]] + tq, [PS[b0 + h]])
                act(Pt[i][:, :, 0:N], ps[:, b0:b0 + 2, 0:N], AF.Exp, [PS[b0], PS[b0 + 1]], [PT[i]], scale=0.125)
                tt("dve", Pt[i][:, :, 0:N], Pt[i][:, :, 0:N], msw[:, :, 0:N], ALU.mult, [PT[i], CB], [PT[i]])

            def a_back(n):
                c, kb = asteps[n]
                N = 256 if kb < NB - 1 else 128
                i = n % 2
                t0 = kb * 128
                ob, db = 4 + kb % 2, 6 + kb % 2
                ob1, db1 = 4 + (kb + 1) % 2, 6 + (kb + 1) % 2
                for h in range(2):
                    r = slice(64 * h, 64 * h + 64)
                    vsl = VA[:, kb, r]
                    mm(ps[r, ob, 0:128], vsl, Pt[i][:, h, 0:128], kb == 0, True, [VAb[kb // 4], PT[i]], [PS[ob]])
                    mm(ps[r, db, 0:128], o64, Pt[i][:, h, 0:128], kb == 0, True, [CB, PT[i]], [PS[db]])
                j = kb % 2
                act(dn[j][:, :], ps[:, db, 0:128], AF.Ln, [PS[db], SK], [DN[j]], bias=sk[:, c:c + 1])
                act(dn[j][:, :], dn[j][:, :], AF.Exp, [DN[j]], [DN[j]], scale=-1.0)
                tt("dve", yAT[:, c, t0:t0 + 128], ps[:, ob, 0:128], dn[j][:, :], ALU.mult,
                   [PS[ob], DN[j]], [YA[kb // 4]])
                if N == 256:
                    for h in range(2):
                        r = slice(64 * h, 64 * h + 64)
                        vsl = VA[:, kb, r]
                        mm(ps[r, ob1, 0:128], vsl, Pt[i][:, h, 128:256], True, False, [VAb[kb // 4], PT[i]], [PS[ob1]])
                        mm(ps[r, db1, 0:128], o64, Pt[i][:, h, 128:256], True, False, [CB, PT[i]], [PS[db1]])

            a_front(0)
            for n in range(len(asteps)):
                if n + 1 < len(asteps):
                    a_front(n + 1)
                a_back(n)
            P.barrier()

        o = LOC
        QTBs = [M.at(o + i * 2 * T, [128, T], BF16) for i in range(2)]; o += 4 * T
        KTBs = [M.at(o + i * 2 * T, [128, T], BF16) for i in range(2)]; o += 4 * T
        VBs = [M.at(o + i * 2 * T, [128, NB, 128], BF16) for i in range(2)]; o += 4 * T
        QBbs = [[Buf("qB%d_%d" % (i, t)) for t in range(NT)] for i in range(2)]
        KBbs = [[Buf("kB%d_%d" % (i, t)) for t in range(NT)] for i in range(2)]
        VBbs = [[Buf("vB%d_%d" % (i, t)) for t in range(NT)] for i in range(2)]
        wB = [M.at(o + i * 6144, [128, 8, 384], BF16) for i in range(2)]; o += 12288
        WB = [Buf("wB%d" % i) for i in range(2)]
        NE = 1
        Et = [M.at(o + i * 4096, [128, 2, 512], F32) for i in range(NE)]; o += 4096 * NE
        ET = [Buf("E%d" % i) for i in range(NE)]
        NL = 2
        Lt = [M.at(o + i * 2048, [128, 2, 512], BF16) for i in range(NL)]; o += 2048 * NL
        LT = [Buf("L%d" % i) for i in range(NL)]
        Ac = [M.at(o + i * 2048, [128, 2, 512], BF16) for i in range(2)]; o += 4096
        AC = [Buf("Ac%d" % i) for i in range(2)]
        At = [M.at(o + i * 2048, [128, 2, 512], BF16) for i in range(2)]; o += 4096
        AT = [Buf("At%d" % i) for i in range(2)]
        assert o <= 212992, o
        msb2 = cb[:, C_MSB:C_MSB + 256].rearrange("p (h u) -> p h u", h=2)
        ntri = cb[:, C_TRI:C_TRI + 128]
        none_ = cb[:, C_ONE:C_ONE + 128]
        PJ = 7

        def wB_load(hp):
            for k in range(0, 8, 4):
                dma("pool", wB[hp % 2][:, k:k + 4, :], wB_h[hp, :, k:k + 4, :], writes=[WB[hp % 2]])

        def proj_ops(hp):
            w, W, si = wB[hp % 2], WB[hp % 2], hp % 2
            ops_ = []
            for t in range(NT):
                sl = slice(t * 512, (t + 1) * 512)
                for which, dst, dbuf in ((0, QTBs[si], QBbs[si]), (1, KTBs[si], KBbs[si])):
                    for k in range(8):
                        ops_.append(lambda k=k, which=which, sl=sl, t=t: mm(
                            ps[:, PJ, :], w[:, k, which * 128:(which + 1) * 128], xT[:, k, sl], k == 0, k == 7,
                            [W, XT[t]], [PS[PJ]]))
                    ops_.append(lambda dst=dst, dbuf=dbuf, sl=sl, t=t: cp("dve", dst[:, sl], ps[:, PJ, :], [PS[PJ]], [dbuf[t]]))
                for j in range(4):
                    blk = t * 4 + j
                    for k in range(8):
                        ops_.append(lambda k=k, j=j, blk=blk, t=t: mm(
                            ps[:, PJ, j * 128:(j + 1) * 128], xT[:, k, blk * 128:(blk + 1) * 128], w[:, k, 256:384],
                            k == 0, k == 7, [XT[t], W], [PS[PJ]]))
                ops_.append(lambda t=t: cp("dve", VBs[si][:, t * 4:(t + 1) * 4, :],
                                          ps[:, PJ, :].rearrange("p (j d) -> p j d", j=4), [PS[PJ]], [VBbs[si][t]]))
            return ops_

        wB_load(0)
        for f_ in proj_ops(0):
            f_()
        for hp in range(4):
            QTB, KTB, VB = QTBs[hp % 2], KTBs[hp % 2], VBs[hp % 2]
            QBb, KBb, VBb = QBbs[hp % 2], KBbs[hp % 2], VBbs[hp % 2]
            nxt = []
            if hp < 3:
                wB_load(hp + 1)
                nxt = proj_ops(hp + 1)

            steps = []
            for qt in range(NT):
                acur = None
                kbs = list(range(4 * qt + 3, -1, -1))
                for si, kb in enumerate(kbs):
                    off = max(0, kb - 4 * qt) * 128
                    st_ = dict(qt=qt, kb=kb, off=off, N=512 - off, t0=qt * 512 + off, diag=kb >= 4 * qt,
                               first=si == 0, last=kb == 0, ob=6, acur=acur)
                    if kb != 0:
                        st_["anew"] = 0 if si == 0 else 1 - acur
                        acur = st_["anew"]
                    steps.append(st_)
            for n_, st_ in enumerate(steps):
                st_["zp"] = 2 * (n_ % 3)
                st_["e"] = n_ % NE
                st_["l"] = n_ % NL
                st_["a"] = n_ % 2

            def s_z(S):
                off, zp, kb, t0, N = S["off"], S["zp"], S["kb"], S["t0"], S["N"]
                for h in range(2):
                    r = slice(64 * h, 64 * h + 64)
                    mm(ps[:, zp + h, off:512], KTB[r, kb * 128:(kb + 1) * 128], QTB[r, t0:t0 + N], True, False,
                       [KBb[kb // 4], QBb[S["qt"]]], [PS[zp + h]])

            def s_el(S):
                off, zp, e_i, l_i = S["off"], S["zp"], S["e"], S["l"]
                Zb = [PS[zp], PS[zp + 1]]
                act(Et[e_i][:, :, off:512], ps[:, zp:zp + 2, off:512], AF.Exp, Zb, [ET[e_i]], scale=0.125)
                act(Lt[l_i][:, :, off:512], Et[e_i][:, :, off:512], AF.Ln, [ET[e_i]], [LT[l_i]], bias=1.0)
                if S["diag"]:
                    tt("dve", Lt[l_i][:, :, off:off + 128], Lt[l_i][:, :, off:off + 128], msb2, ALU.mult,
                       [LT[l_i], CB], [LT[l_i]])
                if not S["last"]:
                    anew, acur = S["anew"], S["acur"]
                    if S["first"]:
                        ms(Ac[anew][:, :, :], 0.0, [AC[anew]])
                        cp("dve", Ac[anew][:, :, off:512], Lt[l_i][:, :, off:512], [LT[l_i]], [AC[anew]])
                    else:
                        if off > 0:
                            ms(Ac[anew][:, :, :], 0.0, [AC[anew]])
                        tt("dve", Ac[anew][:, :, off:512], Ac[acur][:, :, off:512], Lt[l_i][:, :, off:512], ALU.add,
                           [AC[acur], LT[l_i]], [AC[anew]])

            def s_tri(S):
                off, zp, l_i = S["off"], S["zp"], S["l"]
                for h in range(2):
                    mm(ps[:, zp + h, off:512], ntri, Lt[l_i][:, h, off:512], False, S["first"],
                       [CB, LT[l_i]], [PS[zp + h]])
                    if not S["first"]:
                        mm(ps[:, zp + h, off:512], none_, Ac[S["acur"]][:, h, off:512], False, True,
                           [CB, AC[S["acur"]]], [PS[zp + h]])

            def s_a(S):
                off, zp, a_i = S["off"], S["zp"], S["a"]
                Zb = [PS[zp], PS[zp + 1]]
                act(At[a_i][:, :, off:512], ps[:, zp:zp + 2, off:512], AF.Exp, Zb, [AT[a_i]], scale=0.125)
                if S["diag"]:
                    tt("dve", At[a_i][:, :, off:off + 128], At[a_i][:, :, off:off + 128], msb2, ALU.mult,
                       [AT[a_i], CB], [AT[a_i]])

            def s_av(S):
                off, a_i, ob, kb = S["off"], S["a"], S["ob"], S["kb"]
                for h in range(2):
                    r = slice(64 * h, 64 * h + 64)
                    mm(ps[r, ob, off:512], VB[:, kb, r], At[a_i][:, h, off:512], S["first"], S["last"],
                       [VBb[kb // 4], AT[a_i]], [PS[ob]])
                if S["last"]:
                    qt = S["qt"]
                    cp("dve", yBT[:, hp, qt * 512:(qt + 1) * 512], ps[:, ob, :], [PS[ob]], [YB[qt]])

            ns = len(steps)
            for tau in range(ns + 2):
                rounds_left = max(1, ns - 4 - tau)
                take = -(-len(nxt) // rounds_left) if tau < ns - 4 else len(nxt)
                for f_ in nxt[:take]:
                    f_()
                nxt = nxt[take:]
                if tau < ns:
                    s_z(steps[tau])
                if 1 <= tau <= ns:
                    s_tri(steps[tau - 1])
                if 2 <= tau:
                    s_av(steps[tau - 2])
                if tau < ns:
                    s_el(steps[tau])
                if 1 <= tau <= ns:
                    s_a(steps[tau - 1])
        P.barrier()

        if debug:
            o = LOC
            dtmp = M.at(o, [128, 4, 512], F32)
            DT = Buf("dtmp")
            for t in range(NT):
                sl = slice(t * 512, (t + 1) * 512)
                cp("dve", dtmp[:, :, :], yAT[:, :, sl], [YA[t]], [DT])
                dma("sp", dbgA[:, :, sl], dtmp[:, :, :], reads=[DT], key=("dbg", 0))
                cp("dve", dtmp[:, :, :], yBT[:, :, sl], [YB[t]], [DT])
                dma("sp", dbgB[:, :, sl], dtmp[:, :, :], reads=[DT], key=("dbg", 0))
            P.barrier()

        o = LOC
        wa = M.at(o, [128, 4, 1024], BF16); o += 8192
        wb = M.at(o, [128, 4, 1024], BF16); o += 8192
        wo = M.at(o, [128, 8, 1024], BF16); o += 16384
        WAa, WBb, WO = Buf("wa"), Buf("wb"), Buf("wo")
        g1 = M.at(o, [128, D], F32); o += 4096
        b1 = M.at(o, [128, D], F32); o += 4096
        LN1 = Buf("ln1")
        bg = M.at(o, [128, 16], F32); o += 64
        BG = Buf("bg")
        epsb = M.at(o, [128, 1], F32); o += 32
        mhalf = M.at(o, [128, 1], F32); o += 32
        CC = Buf("cc")
        wgc = [M.at(o + i * 4096, [128, 8, 256], BF16) for i in range(2)]; o += 8192
        WG = [Buf("wg%d" % i) for i in range(2)]
        gt = [M.at(o, [128, 2, 512], F32)] * 2; o += 4096
        GT = [Buf("gt")] * 2
        hT = M.at(o, [128, 8, 512], BF16); o += 8192
        HT = [Buf("hT%d" % c) for c in range(8)]
        xin = [M.at(o + i * 4096, [128, D], F32) for i in range(2)]; o += 8192
        XIN = [Buf("xin%d" % i) for i in range(2)]
        x1b = [M.at(o + i * 2048, [128, D], BF16) for i in range(4)]; o += 8192
        X1B = [Buf("x1b%d" % i) for i in range(4)]
        st = [M.at(o + i * 64, [128, 2, 6], F32) for i in range(2)]; o += 128
        mv = [M.at(o + i * 32, [128, 2], F32) for i in range(2)]; o += 64
        rs = [M.at(o + i * 32, [128, 1], F32) for i in range(2)]; o += 64
        nbt = [M.at(o + i * 32, [128, 1], F32) for i in range(2)]; o += 64
        NBT = [Buf("nb%d" % i) for i in range(2)]
        STt = [Buf("st%d" % i) for i in range(2)]
        MV = [Buf("mv%d" % i) for i in range(2)]
        RS = [Buf("rs%d" % i) for i in range(2)]
        assert o <= 212992, o

        for k in range(0, 4, 2):
            dma("pool", wa[:, k:k + 2, :], wa_h[:, k:k + 2, :], writes=[WAa])
            dma("pool", wb[:, k:k + 2, :], wb_h[:, k:k + 2, :], writes=[WBb])
        for k in range(0, 8, 2):
            dma("pool", wo[:, k:k + 2, :], wo_h[:, k:k + 2, :], writes=[WO])
        dma("sp", g1[:, :], ln_h[0], writes=[LN1])
        dma("sp", b1[:, :], ln_h[1], writes=[LN1])
        dma("sp", bg[:, :], bg_h[:, :], writes=[BG])
        ms(epsb[:, :], EPS, [CC])
        ms(mhalf[:, :], -0.5, [CC])

        def layer_norm(yv, Y, gam, bet, LNB, out_ap, OUT, st_, mv_, rs_, ST_, MV_, RS_, eps_, mh_, CC_, nb_, NB_):
            for hh in range(2):
                P.op("dve", lambda e, o_=st_[:, hh, :], i_=yv[:, hh * 512:(hh + 1) * 512]: e.bn_stats(out=o_, in_=i_),
                     reads=[Y], writes=[ST_], multi=True)
            P.op("dve", lambda e, o_=mv_[:, :], i_=st_[:, :, :]: e.bn_aggr(out=o_, in_=i_), reads=[ST_], writes=[MV_])
            ts("pool", rs_[:, :], mv_[:, 1:2], eps_[:, 0:1], None, ALU.add, ALU.bypass, [MV_, CC_], [RS_])
            tt("pool", rs_[:, :], rs_[:, :], mh_[:, :], ALU.pow, [RS_, CC_], [RS_])
            stt(nb_[:, :], mv_[:, 0:1], -1.0, rs_[:, :], ALU.mult, ALU.mult, [MV_, RS_], [NB_])
            act(yv, yv, AF.Identity, [Y, RS_, NB_], [Y], bias=nb_[:, 0:1], scale=rs_[:, 0:1])
            tt("dve", yv, yv, gam, ALU.mult, [Y, LNB], [Y])
            tt("dve", out_ap, yv, bet, ALU.add, [Y, LNB], [OUT])

        def c1_R(t, j):
            rb = 4 + 2 * (j % 2)
            for hf in range(2):
                for c in range(8):
                    mm(ps[:, rb + hf, :], hT[:, c, j * 128:(j + 1) * 128], wo[:, c, hf * 512:(hf + 1) * 512],
                       c == 0, c == 7, [HT[c], WO], [PS[rb + hf]])

        def ln_multi(items, gam, bet, LNB):
            for it_ in items:
                for hh in range(2):
                    P.op("dve", lambda e, o_=it_["st"][:, hh, :], i_=it_["yv"][:, hh * 512:(hh + 1) * 512]:
                         e.bn_stats(out=o_, in_=i_), reads=[it_["Y"]], writes=[it_["ST"]], multi=True)
            for it_ in items:
                P.op("dve", lambda e, o_=it_["mv"][:, :], i_=it_["st"][:, :, :]: e.bn_aggr(out=o_, in_=i_),
                     reads=[it_["ST"]], writes=[it_["MV"]])
            for it_ in items:
                ts("pool", it_["rs"][:, :], it_["mv"][:, 1:2], epsb[:, 0:1], None, ALU.add, ALU.bypass,
                   [it_["MV"], CC], [it_["RS"]])
            for it_ in items:
                tt("pool", it_["rs"][:, :], it_["rs"][:, :], mhalf[:, :], ALU.pow, [it_["RS"], CC], [it_["RS"]])
            for it_ in items:
                stt(it_["nb"][:, :], it_["mv"][:, 0:1], -1.0, it_["rs"][:, :], ALU.mult, ALU.mult,
                    [it_["MV"], it_["RS"]], [it_["NB"]])
            for it_ in items:
                act(it_["yv"], it_["yv"], AF.Identity, [it_["Y"], it_["RS"], it_["NB"]], [it_["Y"]],
                    bias=it_["nb"][:, 0:1], scale=it_["rs"][:, 0:1])
            for it_ in items:
                tt("dve", it_["yv"], it_["yv"], gam, ALU.mult, [it_["Y"], LNB], [it_["Y"]])
            for it_ in items:
                tt("dve", it_["yv"], it_["yv"], bet, ALU.add, [it_["Y"], LNB], [it_["Y"]])

        def c1_chain2(t, js):
            items = []
            for j in js:
                i = j % 2
                blk = t * 4 + j
                dma("sp", xin[i][:, :], x_h[blk * 128:(blk + 1) * 128, :], writes=[XIN[i]], key=("xin", i))
            for j in js:
                i = j % 2
                rb = 4 + 2 * (j % 2)
                stt(xin[i][:, :], xin[i][:, :], ALPHA, ps[:, rb:rb + 2, :].rearrange("p a n -> p (a n)"),
                    ALU.mult, ALU.add, [XIN[i], PS[rb], PS[rb + 1]], [XIN[i]])
                items.append(dict(yv=xin[i][:, :], Y=XIN[i], st=st[i], mv=mv[i], rs=rs[i], nb=nbt[i],
                                  ST=STt[i], MV=MV[i], RS=RS[i], NB=NBT[i]))
            ln_multi(items, g1[:, :], b1[:, :], LN1)
            for j in js:
                i = j % 2
                blk = t * 4 + j
                dma("sp", x1_s[blk * 128:(blk + 1) * 128, :], xin[i][:, :], reads=[XIN[i]], key=("x1s", i))
                cp("act", x1b[j][:, :], xin[i][:, :], [XIN[i]], [X1B[j]])

        def c1_T(t, j):
            blk = t * 4 + j
            tb = 4 + j
            tpb = ps[:, tb, :].bitcast(BF16)
            for c in range(8):
                P.op("pe", lambda e, o_=tpb[:, c * 128:(c + 1) * 128], i_=x1b[j][:, c * 128:(c + 1) * 128],
                     id_=cb[:, C_ID:C_ID + 128]: e.transpose(out=o_, in_=i_, identity=id_),
                     reads=[X1B[j], CB], writes=[PS[tb]])
            cp("dve", xT[:, :, blk * 128:(blk + 1) * 128], tpb.rearrange("p (c n) -> p c n", c=8),
               [PS[tb]], [XT[t]])

        gi = 0
        bi = 0

        def wg_load(n):
            if n < NT * 8:
                dma("pool", wgc[n % 2][:, :, :], wg_h[n % 8, :, :, :], writes=[WG[n % 2]], key=("wg", n % 2))
        wg_load(0)
        for t in range(NT):
            sl = slice(t * 512, (t + 1) * 512)
            for c in range(8):
                wi = gi % 2
                gi2 = gi % 2
                gi += 1
                gb = 0
                for ab in range(2):
                    for k in range(8):
                        mm(ps[:, gb + ab, :], wgc[wi][:, k, ab * 128:(ab + 1) * 128], xT[:, k, sl], k == 0, k == 7,
                           [WG[wi], XT[t]], [PS[gb + ab]])
                    act(gt[gi2][:, ab, :], ps[:, gb + ab, :], AF.Sigmoid, [PS[gb + ab], BG], [GT[gi2]],
                        bias=bg[:, ab * 8 + c:ab * 8 + c + 1])
                wg_load(gi)
                for k in range(4):
                    mm(ps[:, 2, :], wa[:, k, c * 128:(c + 1) * 128], yAT[:, k, sl], k == 0, k == 3, [WAa, YA[t]], [PS[2]])
                for k in range(4):
                    mm(ps[:, 3, :], wb[:, k, c * 128:(c + 1) * 128], yBT[:, k, sl], k == 0, k == 3, [WBb, YB[t]], [PS[3]])
                tt("dve", gt[gi2][:, 0, :], gt[gi2][:, 0, :], ps[:, 2, :], ALU.mult, [GT[gi2], PS[2]], [GT[gi2]])
                tt("dve", gt[gi2][:, 1, :], gt[gi2][:, 1, :], ps[:, 3, :], ALU.mult, [GT[gi2], PS[3]], [GT[gi2]])
                if t > 0 and c % 2 == 1:
                    c1_T(t - 1, c // 2)
                tt("dve", hT[:, c, :], gt[gi2][:, 0, :], gt[gi2][:, 1, :], ALU.add, [GT[gi2]], [HT[c]])
            c1_R(t, 0)
            c1_R(t, 1)
            c1_chain2(t, (0, 1))
            c1_R(t, 2)
            c1_R(t, 3)
            c1_chain2(t, (2, 3))
        for j in range(4):
            c1_T(NT - 1, j)
        P.barrier()

        o = LOC2
        TQ = min(1024, T)
        NQt = T // TQ
        HPQ = TQ // 512
        wdn = M.at(o, [128, NJ, 1024], BF16); o += NJ * 2048
        WDN = Buf("wdn")
        hid = M.at(o, [128, NJ, TQ], BF16); o += NJ * TQ * 2
        HID = [[Buf("hid%d_%d" % (j, hh)) for hh in range(HPQ)] for j in range(NJ)]
        g2 = M.at(o, [128, D], F32); o += 4096
        b2 = M.at(o, [128, D], F32); o += 4096
        LN2 = Buf("ln2")
        cpm = M.at(o, [128, 2 * NJ, 4], F32); o += 2 * NJ * 16
        CPM = Buf("cpm")
        halo = M.at(o, [128, 2 * NJ, 2], F32); o += 2 * NJ * 8
        HALO = [Buf("halo%d" % jj) for jj in range(2 * NJ)]
        o = (o + 31) // 32 * 32
        wupc = [M.at(o + i * 4096, [128, 8, 256], BF16) for i in range(3)]; o += 12288
        WUP = [Buf("wup%d" % i) for i in range(3)]
        U = [[M.at(o + (i * 2 + g) * 2080, [128, 514], F32) for g in range(2)] for i in range(2)]; o += 4 * 2080
        UB = [[Buf("U%d_%d" % (i, g)) for g in range(2)] for i in range(2)]
        Aa = [[M.at(o + (i * 2 + g) * 2048, [128, 512], F32) for g in range(2)] for i in range(2)]; o += 4 * 2048
        AB = [[Buf("A%d_%d" % (i, g)) for g in range(2)] for i in range(2)]
        xin = [M.at(o + i * 4096, [128, D], F32) for i in range(2)]; o += 8192
        XIN = [Buf("xin2_%d" % i) for i in range(2)]
        st = [M.at(o + i * 64, [128, 2, 6], F32) for i in range(2)]; o += 128
        mv = [M.at(o + i * 32, [128, 2], F32) for i in range(2)]; o += 64
        rs = [M.at(o + i * 32, [128, 1], F32) for i in range(2)]; o += 64
        nbt = [M.at(o + i * 32, [128, 1], F32) for i in range(2)]; o += 64
        NBT = [Buf("nb2%d" % i) for i in range(2)]
        epsb = M.at(o, [128, 1], F32); o += 32
        mhalf = M.at(o, [128, 1], F32); o += 32
        STt = [Buf("st2%d" % i) for i in range(2)]
        MV = [Buf("mv2%d" % i) for i in range(2)]
        RS = [Buf("rs2%d" % i) for i in range(2)]
        CC = Buf("cc2")
        assert o <= 212992, o

        for j in range(0, NJ, 2):
            dma("pool", wdn[:, j:j + 2, :], wdn_h[:, j:j + 2, :], writes=[WDN])
        dma("sp", g2[:, :], ln_h[2], writes=[LN2])
        dma("sp", b2[:, :], ln_h[3], writes=[LN2])
        dma("sp", cpm[:, :, :], cp_h[:, :, :], writes=[CPM])
        ms(epsb[:, :], EPS, [CC])
        ms(mhalf[:, :], -0.5, [CC])
        ms(halo[:, :, :], 0.0, HALO)

        ui = 0
        wi_ = 0
        bi = 0
        wseq = [(q, j) for q in range(NQt) for j in range(NJ)]

        def wup_load(n):
            if n < len(wseq):
                dma("pool", wupc[n % 3][:, :, :], wup_h[wseq[n][1], :, :, :], writes=[WUP[n % 3]], key=("wup", n % 3))
        wup_load(0)
        wup_load(1)
        def ln_front(it_):
            for hh in range(2):
                P.op("dve", lambda e, o_=it_["st"][:, hh, :], i_=it_["yv"][:, hh * 512:(hh + 1) * 512]:
                     e.bn_stats(out=o_, in_=i_), reads=[it_["Y"]], writes=[it_["ST"]], multi=True)
            P.op("dve", lambda e, o_=it_["mv"][:, :], i_=it_["st"][:, :, :]: e.bn_aggr(out=o_, in_=i_),
                 reads=[it_["ST"]], writes=[it_["MV"]])
            ts("pool", it_["rs"][:, :], it_["mv"][:, 1:2], epsb[:, 0:1], None, ALU.add, ALU.bypass,
               [it_["MV"], CC], [it_["RS"]])
            tt("pool", it_["rs"][:, :], it_["rs"][:, :], mhalf[:, :], ALU.pow, [it_["RS"], CC], [it_["RS"]])
            stt(it_["nb"][:, :], it_["mv"][:, 0:1], -1.0, it_["rs"][:, :], ALU.mult, ALU.mult,
                [it_["MV"], it_["RS"]], [it_["NB"]])
            act(it_["yv"], it_["yv"], AF.Identity, [it_["Y"], it_["RS"], it_["NB"]], [it_["Y"]],
                bias=it_["nb"][:, 0:1], scale=it_["rs"][:, 0:1])

        def ln_back(it_, gam, bet, LNB):
            tt("dve", it_["yv"], it_["yv"], gam, ALU.mult, [it_["Y"], LNB], [it_["Y"]])
            tt("dve", it_["yv"], it_["yv"], bet, ALU.add, [it_["Y"], LNB], [it_["Y"]])

        pend = [None]

        def flush_pend():
            if pend[0] is not None:
                o_, a_, b_, rd, wr = pend[0]
                tt("dve", o_, a_, b_, ALU.mult, rd, wr)
                pend[0] = None

        def down_front(q, jb):
            blk = q * (TQ // 128) + jb
            i = blk % 2
            rb = 4 + 2 * i
            dma("sp", xin[i][:, :], x1_s[blk * 128:(blk + 1) * 128, :], writes=[XIN[i]], key=("xin", i))
            for hf in range(2):
                for j in range(NJ):
                    mm(ps[:, rb + hf, :], hid[:, j, jb * 128:(jb + 1) * 128], wdn[:, j, hf * 512:(hf + 1) * 512],
                       j == 0, j == NJ - 1, [HID[j][jb // 4], WDN], [PS[rb + hf]])
            stt(xin[i][:, :], xin[i][:, :], ALPHA, ps[:, rb:rb + 2, :].rearrange("p a n -> p (a n)"),
                ALU.mult, ALU.add, [XIN[i], PS[rb], PS[rb + 1]], [XIN[i]])
            it_ = dict(yv=xin[i][:, :], Y=XIN[i], st=st[i], mv=mv[i], rs=rs[i], nb=nbt[i],
                       ST=STt[i], MV=MV[i], RS=RS[i], NB=NBT[i], blk=blk, i=i)
            ln_front(it_)
            return it_

        def down_back(it_):
            ln_back(it_, g2[:, :], b2[:, :], LN2)
            blk, i = it_["blk"], it_["i"]
            dma("sp", out_h[blk * 128:(blk + 1) * 128, :], xin[i][:, :], reads=[XIN[i]], key=("out", i))

        for q in range(NQt):
            for j in range(NJ):
                wi = wi_ % 3
                wup_load(wi_ + 2)
                wi_ += 1
                for hh in range(HPQ):
                    tI = q * HPQ + hh
                    sl = slice(tI * 512, (tI + 1) * 512)
                    i = ui % 2
                    ui += 1
                    for g in range(2):
                        jj = g * NJ + j
                        b = 2 * i + g
                        for k in range(8):
                            mm(ps[:, b, :], wupc[wi][:, k, g * 128:(g + 1) * 128], xT[:, k, sl], k == 0, k == 7,
                               [WUP[wi], XT[tI]], [PS[b]])
                        cp("pool", U[i][g][:, 0:2], halo[:, jj, :], [HALO[jj]], [UB[i][g]])
                        P.op("act", lambda e, o_=U[i][g][:, 2:514], i_=ps[:, b, :]: e.activation(out=o_, in_=i_, func=AF.Copy),
                             reads=[PS[b]], writes=[UB[i][g]], multi=True)
                        act(Aa[i][g][:, :], ps[:, b, :], AF.Identity, [PS[b], CPM], [AB[i][g]],
                            bias=cpm[:, jj, 3:4], scale=cpm[:, jj, 2:3])
                        cp("pool", halo[:, jj, :], U[i][g][:, 512:514], [UB[i][g]], [HALO[jj]])
                        stt(Aa[i][g][:, :], U[i][g][:, 1:513], cpm[:, jj, 1:2], Aa[i][g][:, :], ALU.mult, ALU.add,
                            [UB[i][g], CPM, AB[i][g]], [AB[i][g]])
                        stt(Aa[i][g][:, :], U[i][g][:, 0:512], cpm[:, jj, 0:1], Aa[i][g][:, :], ALU.mult, ALU.add,
                            [UB[i][g], CPM, AB[i][g]], [AB[i][g]])
                    flush_pend()
                    act(Aa[i][0][:, :], Aa[i][0][:, :], AF.Silu, [AB[i][0]], [AB[i][0]])
                    pend[0] = (hid[:, j, hh * 512:(hh + 1) * 512], Aa[i][0][:, :], Aa[i][1][:, :],
                               [AB[i][0], AB[i][1]], [HID[j][hh]])
            flush_pend()
            prev = None
            for jb in range(TQ // 128):
                cur = down_front(q, jb)
                if prev is not None:
                    down_back(prev)
                prev = cur
            down_back(prev)
        P.barrier()
        P.emit(nc, stack)
    return nc


def host_prep(b, x, positions, w_in, b_gate, sinks, w_branch_a, w_branch_b, w_out,
              ln1_g, ln1_b, w_up, conv_w, conv_b, w_down, ln2_g, ln2_b, shared):
    m = dict(shared)
    m["xT"] = np.ascontiguousarray(x[b].T)
    m["x"] = np.ascontiguousarray(x[b])
    m["pos"] = np.ascontiguousarray(np.broadcast_to(positions[b][None, :].astype(np.int32), (128, positions.shape[1])))
    return m


def host_shared(w_in, b_gate, sinks, w_branch_a, w_branch_b, w_out,
                ln1_g, ln1_b, w_up, conv_w, conv_b, w_down, ln2_g, ln2_b):
    f = np.float32
    w_in = w_in[0]
    cbm = np.zeros((128, NCB), f)
    cbm[:, C_ID:C_ID + 128] = np.eye(128, dtype=f)
    rot = np.zeros((128, 128), f)
    for m_ in range(128):
        base, ml = (m_ // 64) * 64, m_ % 64
        if ml < 32:
            rot[base + ml + 32, m_] = -1.0
        else:
            rot[base + ml - 32, m_] = 1.0
    cbm[:, C_ROT:C_ROT + 128] = rot
    jj, ss = np.meshgrid(np.arange(128), np.arange(128), indexing="ij")
    cbm[:, C_TRI:C_TRI + 128] = np.where(jj >= ss, -8.0, 0.0)
    cbm[:, C_ONE:C_ONE + 128] = -8.0
    s_, t_ = np.meshgrid(np.arange(128), np.arange(128), indexing="ij")
    msb = (s_ < t_).astype(f)
    cbm[:, C_MSB:C_MSB + 128] = msb
    cbm[:, C_MSB + 128:C_MSB + 256] = msb
    mswa = np.concatenate([(s_ <= t_).astype(f), (s_ > t_).astype(f)], axis=1)
    cbm[:, C_MSWA:C_MSWA + 256] = mswa
    cbm[:, C_MSWA + 256:C_MSWA + 512] = mswa
    cbm[:, C_O64:C_O64 + 64] = 1.0
    cfm = np.zeros((128, 8), f)
    inv = (f(1.0) / np.power(f(10000.0), np.arange(0, 64, 2, dtype=f) / f(64.0))).astype(f)
    cfm[:, 0] = inv[np.arange(128) % 32]
    def kp(w):
        K, N = w.shape
        return np.ascontiguousarray(w.reshape(K // 128, 128, N).transpose(1, 0, 2))
    qa_cols = np.concatenate([np.r_[c * 64:(c + 1) * 64, (4 + c) * 64:(5 + c) * 64] for c in range(4)])
    wA = np.concatenate([w_in[:, 0:512][:, qa_cols], w_in[:, 512:640], w_in[:, 640:768]], axis=1)
    wB = np.stack([np.concatenate([w_in[:, 768 + hp * 128:768 + (hp + 1) * 128],
                                   w_in[:, 1280 + hp * 128:1280 + (hp + 1) * 128],
                                   w_in[:, 1792 + hp * 128:1792 + (hp + 1) * 128]], axis=1) for hp in range(4)])
    wgA, wgB = w_in[:, 2304:3328], w_in[:, 3328:4352]
    wg = np.stack([np.concatenate([wgA[:, c * 128:(c + 1) * 128], wgB[:, c * 128:(c + 1) * 128]], axis=1)
                   for c in range(8)])
    frow = np.array([[(c if p < 64 else 4 + c) * 64 + p % 64 for c in range(4)] for p in range(128)])
    wa = np.ascontiguousarray(w_branch_a[0][frow, :])
    wup = w_up[0]
    wupr = np.stack([np.concatenate([wup[:, j * 128:(j + 1) * 128], wup[:, DFF + j * 128:DFF + (j + 1) * 128]], axis=1)
                     for j in range(NJ)])
    cpm = np.zeros((128, 2 * NJ, 4), f)
    cw, cbias = conv_w[0], conv_b[0]
    for jj_ in range(2 * NJ):
        ch = jj_ * 128 + np.arange(128)
        cpm[:, jj_, 0:3] = cw[:, ch].T
        cpm[:, jj_, 3] = cbias[ch]
    bgm = np.zeros((128, 16), f)
    for c in range(8):
        bgm[:, c] = b_gate[0][c * 128:(c + 1) * 128]
        bgm[:, 8 + c] = b_gate[0][1024 + c * 128:1024 + (c + 1) * 128]
    skm = np.zeros((128, 4), f)
    for c in range(4):
        skm[0:64, c] = sinks[0][c]
        skm[64:128, c] = sinks[0][4 + c]
    lnm = np.stack([np.broadcast_to(v[0][None, :], (128, D)) for v in (ln1_g, ln1_b, ln2_g, ln2_b)]).astype(f)
    return dict(
        cb=cbm, cf=cfm,
        wA=kp(wA), wB=np.stack([kp(wB[hp]) for hp in range(4)]),
        wg=np.stack([kp(wg[c]) for c in range(8)]),
        wa=wa, wb=kp(w_branch_b[0]), wo=kp(w_out[0]),
        wup=np.stack([kp(wupr[j]) for j in range(NJ)]), wdn=kp(w_down[0]),
        cp=cpm, bg=bgm, sk=skm, ln=np.ascontiguousarray(lnm),
    )


_NC_CACHE = {}


def kernel(x, positions, w_in, b_gate, sinks, w_branch_a, w_branch_b, w_out,
           ln1_g, ln1_b, w_up, conv_w, conv_b, w_down, ln2_g, ln2_b):
    args = [np.asarray(a) for a in (x, positions, w_in, b_gate, sinks, w_branch_a, w_branch_b, w_out,
                                    ln1_g, ln1_b, w_up, conv_w, conv_b, w_down, ln2_g, ln2_b)]
    x, positions = args[0], args[1]
    B, T, _ = x.shape
    shared = host_shared(*args[2:])
    in_maps = [host_prep(b, x, positions, *args[2:], shared) for b in range(B)]
    nc = build(T)
    res = run_bass_kernel_spmd(nc, in_maps, core_ids=list(range(B)))
    return np.stack([np.asarray(r["out"]) for r in res.results]).astype(np.float32)
```

```python
import numpy as np
import concourse.bass as bass
import concourse.mybir as mybir
from concourse.bass_utils import run_bass_kernel_spmd

F32 = mybir.dt.float32
BF16 = mybir.dt.bfloat16
I32 = mybir.dt.int32
AF = mybir.ActivationFunctionType
ALU = mybir.AluOpType

D = 1024
DFF = 2816
NJ = DFF // 128
HD = 64
ALPHA = float(2.0 ** 0.25)
EPS = 1e-5
PI = float(np.pi)
INV2PI = float(np.float32(1.0 / (2.0 * np.pi)))
MAGIC = 12582912.0
CW1 = 6.28125
CW2 = float(2.0 * np.pi - 6.28125)
SBUF_BASE = 16384

C_ID, C_ROT, C_TRI, C_ONE, C_MSB, C_MSWA, C_O64, NCB = 0, 128, 256, 384, 512, 768, 1280, 1344


class Buf:
    __slots__ = ("name", "w", "r", "psum")

    def __init__(self, name, psum=False):
        self.name = name
        self.w = []
        self.r = []
        self.psum = psum


class Prog:
    ENGS = ("pe", "act", "dve", "pool", "sp")

    def __init__(self):
        self.ops = []
        self.bar_start = 0

    def op(self, eng, fn, reads=(), writes=(), dma=None, multi=False):
        i = len(self.ops)
        deps = set()
        for b in reads:
            deps.update(b.w)
            if b.psum:
                deps.update(r for r in b.r if self.ops[r]["eng"] != eng)
        for b in writes:
            if not multi:
                deps.update(b.w)
            deps.update(b.r)
        for b in reads:
            b.r.append(i)
        for b in writes:
            if multi and not b.r:
                b.w = b.w + [i]
            else:
                b.w = [i]
            b.r = []
        deps.discard(i)
        last = {}
        keep = set()
        for d in deps:
            od = self.ops[d]
            if od["dma"] is not None:
                keep.add(d)
            elif last.get(od["eng"], -1) < d:
                last[od["eng"]] = d
        keep.update(last.values())
        self.ops.append(dict(eng=eng, fn=fn, deps=keep, dma=dma))
        return i

    def barrier(self):
        last = {}
        deps = set()
        for idx in range(self.bar_start, len(self.ops)):
            o = self.ops[idx]
            if o["fn"] is None:
                continue
            if o["dma"] is not None:
                deps.add(idx)
            else:
                last[o["eng"]] = idx
        deps.update(last.values())
        for e in self.ENGS:
            self.ops.append(dict(eng=e, fn=None, deps=set(deps), dma=None))
        self.bar_start = len(self.ops)

    def emit(self, nc, stack):
        ops = self.ops
        need = set()
        for o in ops:
            for d in o["deps"]:
                od = ops[d]
                if o["eng"] == "pe" and od["eng"] == "pe" and od["dma"] is None:
                    continue
                need.add(d)
        esem = {e: stack.enter_context(nc.semaphore("s_" + e)) for e in self.ENGS}
        dsem = {}
        tick = {e: 0 for e in self.ENGS}
        dcnt = {}
        sig = {}
        for i, o in enumerate(ops):
            if i not in need or o["fn"] is None:
                continue
            if o["dma"] is not None:
                k = o["dma"]
                if k not in dsem:
                    dsem[k] = stack.enter_context(nc.semaphore("d_%d" % len(dsem)))
                    dcnt[k] = 0
                dcnt[k] += 16
                sig[i] = (dsem[k], dcnt[k], 16)
            else:
                tick[o["eng"]] += 1
                sig[i] = (esem[o["eng"]], tick[o["eng"]], 1)
        per = {e: [] for e in self.ENGS}
        for i, o in enumerate(ops):
            per[o["eng"]].append(i)
        self.stats = dict(ticks=dict(tick), nsem=len(dsem) + len(esem), nops=len(ops),
                          dmax=max(dcnt.values()) if dcnt else 0)
        block = stack.enter_context(nc.Block())

        def run(eng, ename):
            waited = {}
            for i in per[ename]:
                o = ops[i]
                for d in sorted(o["deps"]):
                    if d not in sig:
                        continue
                    od = ops[d]
                    if ename == "pe" and od["eng"] == "pe" and od["dma"] is None:
                        continue
                    sem, val, _ = sig[d]
                    key = id(sem)
                    if waited.get(key, 0) >= val:
                        continue
                    eng.wait_ge(sem, val)
                    waited[key] = val
                if o["fn"] is not None:
                    ins = o["fn"](eng)
                    if i in sig:
                        ins.then_inc(sig[i][0], sig[i][2])

        @block.tensor
        def _(e):
            run(e, "pe")

        @block.scalar
        def _(e):
            run(e, "act")

        @block.vector
        def _(e):
            run(e, "dve")

        @block.gpsimd
        def _(e):
            run(e, "pool")

        @block.sync
        def _(e):
            run(e, "sp")


class Mem:
    def __init__(self, nc):
        self.nc = nc
        self.n = 0

    def at(self, off, shape, dt):
        self.n += 1
        return self.nc.alloc_sbuf_tensor_at("t%d" % self.n, list(shape), dt, offset=SBUF_BASE + off).ap()


def build(T=4096, debug=False):
    NB = T // 128
    NT = T // 512
    nc = bass.Bass("TRN2", target_bir_lowering=False)
    P = Prog()
    M = Mem(nc)

    def din(name, shape, dt=F32):
        return nc.dram_tensor(name, list(shape), dt, kind="ExternalInput").ap()

    xT_h = din("xT", [D, T])
    x_h = din("x", [T, D])
    pos_h = din("pos", [128, T], I32)
    cb_h = din("cb", [128, NCB])
    cf_h = din("cf", [128, 8])
    wA_h = din("wA", [128, 8, 768])
    wB_h = din("wB", [4, 128, 8, 384])
    wg_h = din("wg", [8, 128, 8, 256])
    wa_h = din("wa", [128, 4, 1024])
    wb_h = din("wb", [128, 4, 1024])
    wo_h = din("wo", [128, 8, 1024])
    wup_h = din("wup", [NJ, 128, 8, 256])
    wdn_h = din("wdn", [128, NJ, 1024])
    cp_h = din("cp", [128, 2 * NJ, 4])
    bg_h = din("bg", [128, 16])
    sk_h = din("sk", [128, 4])
    ln_h = din("ln", [4, 128, D])
    out_h = nc.dram_tensor("out", [T, D], F32, kind="ExternalOutput").ap()
    x1_s = nc.dram_tensor("x1s", [T, D], F32, kind="Internal").ap()
    rt_s = nc.dram_tensor("rts", [2, 128, T], F32, kind="Internal").ap()
    if debug:
        dbgA = nc.dram_tensor("dbgA", [128, 4, T], F32, kind="ExternalOutput").ap()
        dbgB = nc.dram_tensor("dbgB", [128, 4, T], F32, kind="ExternalOutput").ap()

    import contextlib
    stack = contextlib.ExitStack()
    with stack:
        ps = stack.enter_context(nc.psum_tensor([128, 8, 512], F32))
        PS = [Buf("ps%d" % b, psum=True) for b in range(8)]

        o = 0
        cb = M.at(o, [128, NCB], BF16); o += 2 * NCB
        CB = Buf("cb")
        cf = M.at(o, [128, 8], F32); o += 32
        CF = Buf("cf")
        xT = M.at(o, [128, 8, T], BF16); o += 16 * T
        XT = [Buf("xT%d" % t) for t in range(NT)]
        LOC2 = o
        yAT = M.at(o, [128, 4, T], BF16); o += 8 * T
        YA = [Buf("yA%d" % t) for t in range(NT)]
        yBT = M.at(o, [128, 4, T], BF16); o += 8 * T
        YB = [Buf("yB%d" % t) for t in range(NT)]
        LOC = o
        assert LOC % 32 == 0

        def dma(eng, out, in_, reads=(), writes=(), key=None, multi=True):
            if key is None:
                key = ("w", writes[0].name) if writes else ("r", reads[0].name)
            return P.op(eng, lambda e, out=out, in_=in_: e.dma_start(out=out, in_=in_),
                        reads=reads, writes=writes, dma=key, multi=multi)

        def ms(ap, val, writes):
            return P.op("pool", lambda e, ap=ap, val=val: e.memset(ap, val), writes=writes)

        def mm(out, lhsT, rhs, start, stop, reads, writes):
            return P.op("pe", lambda e, out=out, lhsT=lhsT, rhs=rhs, start=start, stop=stop:
                        e.matmul(out, lhsT=lhsT, rhs=rhs, start=start, stop=stop, skip_group_check=True),
                        reads=reads, writes=writes)

        def act(out, in_, func, reads, writes, bias=0.0, scale=1.0):
            return P.op("act", lambda e, out=out, in_=in_, func=func, bias=bias, scale=scale:
                        e.activation(out=out, in_=in_, func=func, bias=bias, scale=scale),
                        reads=reads, writes=writes)

        def tt(eng, out, in0, in1, op, reads, writes):
            return P.op(eng, lambda e, out=out, in0=in0, in1=in1, op=op:
                        e.tensor_tensor(out=out, in0=in0, in1=in1, op=op), reads=reads, writes=writes)

        def ts(eng, out, in0, s1, s2, op0, op1, reads, writes):
            return P.op(eng, lambda e, out=out, in0=in0, s1=s1, s2=s2, op0=op0, op1=op1:
                        e.tensor_scalar(out=out, in0=in0, scalar1=s1, scalar2=s2, op0=op0, op1=op1),
                        reads=reads, writes=writes)

        def stt(out, in0, scalar, in1, op0, op1, reads, writes):
            return P.op("dve", lambda e, out=out, in0=in0, scalar=scalar, in1=in1, op0=op0, op1=op1:
                        e.scalar_tensor_tensor(out=out, in0=in0, scalar=scalar, in1=in1, op0=op0, op1=op1),
                        reads=reads, writes=writes)

        def cp(eng, out, in_, reads, writes):
            if eng == "act":
                return act(out, in_, AF.Copy, reads, writes)
            return P.op(eng, lambda e, out=out, in_=in_: e.tensor_copy(out=out, in_=in_),
                        reads=reads, writes=writes)

        dma("pool", cb[:, :], cb_h[:, :], writes=[CB])
        dma("sp", cf[:, :], cf_h[:, :], writes=[CF])
        xv = xT_h.rearrange("(c p) t -> p c t", p=128)

        def load_xT(t):
            dma("pool", xT[:, :, t * 512:(t + 1) * 512], xv[:, :, t * 512:(t + 1) * 512], writes=[XT[t]])
        load_xT(0)

        o = LOC
        QTA = M.at(o, [128, 2, T], BF16); o += 4 * T
        QA = [[Buf("qa%d_%d" % (c, t)) for t in range(NT)] for c in range(4)]
        KTA = M.at(o, [128, T], BF16); o += 2 * T
        KA = [Buf("ka%d" % t) for t in range(NT)]
        VA = M.at(o, [128, NB, 128], BF16); o += 2 * T
        VAb = [Buf("va%d" % t) for t in range(NT)]
        sk = M.at(o, [128, 4], F32); o += 32
        SK = Buf("sk")
        npi = M.at(o, [128, 1], F32); o += 32
        NPI = Buf("npi")
        A2 = o
        wA = M.at(o, [128, 8, 768], BF16); o += 8 * 768 * 2
        WA = Buf("wA")
        ang = M.at(o, [128, 512], F32); o += 2048
        tmpa = M.at(o, [128, 512], F32); o += 2048
        tmpb = M.at(o, [128, 512], F32); o += 2048
        posi = tmpb.bitcast(I32)
        TMPB = Buf("tmpb")
        cosTs = [M.at(o + i * 2048, [128, 512], F32) for i in range(2)]; o += 4096
        sinTs = [M.at(o + i * 2048, [128, 512], F32) for i in range(2)]; o += 4096
        COSs = [Buf("cos%d" % i) for i in range(2)]
        SINs = [Buf("sin%d" % i) for i in range(2)]
        ANG, TMPA = Buf("ang"), Buf("tmpa")
        POSI = TMPB
        q32 = [M.at(o + i * 2048, [128, 512], F32) for i in range(2)]; o += 4096
        qb = [M.at(o + i * 1024, [128, 512], BF16) for i in range(2)]; o += 2048
        t2 = [M.at(o + i * 2048, [128, 512], F32) for i in range(2)]; o += 4096
        Q32 = [Buf("q32_%d" % i) for i in range(2)]
        QB = [Buf("qb_%d" % i) for i in range(2)]
        T2 = [Buf("t2_%d" % i) for i in range(2)]
        assert o <= 212992, o

        for k in range(0, 8, 2):
            dma("pool", wA[:, k:k + 2, :], wA_h[:, k:k + 2, :], writes=[WA])
        for t in range(1, NT):
            load_xT(t)
        dma("sp", sk[:, :], sk_h[:, :], writes=[SK])
        ms(npi[:, :], -PI, [NPI])
        act(sk[:, :], sk[:, :], AF.Exp, [SK], [SK])

        ctr = [0]

        def rope_proj(wcols, dst, dstbuf, t, cosT, sinT, COS, SIN):
            i = ctr[0] % 2
            ctr[0] += 1
            b0, b1 = 2 * i, 2 * i + 1
            for k in range(8):
                mm(ps[:, b0, :], wA[:, k, wcols], xT[:, k, t * 512:(t + 1) * 512], k == 0, k == 7,
                   [WA, XT[t]], [PS[b0]])
            act(q32[i][:, :], ps[:, b0, :], AF.Copy, [PS[b0]], [Q32[i]])
            cp("act", qb[i][:, :], ps[:, b0, :], [PS[b0]], [QB[i]])
            mm(ps[:, b1, :], cb[:, C_ROT:C_ROT + 128], qb[i][:, :], True, True, [CB, QB[i]], [PS[b1]])
            tt("dve", q32[i][:, :], q32[i][:, :], cosT[:, :], ALU.mult, [Q32[i], COS], [Q32[i]])
            tt("dve", t2[i][:, :], ps[:, b1, :], sinT[:, :], ALU.mult, [PS[b1], SIN], [T2[i]])
            tt("dve", dst, q32[i][:, :], t2[i][:, :], ALU.add, [Q32[i], T2[i]], [dstbuf])

        Pt = [M.at(o + i * 1024, [128, 2, 256], BF16) for i in range(2)]; o += 2048
        PT = [Buf("pt%d" % i) for i in range(2)]
        dn = [M.at(o + i * 512, [128, 128], F32) for i in range(2)]; o += 1024
        DN = [Buf("dn%d" % i) for i in range(2)]
        assert o <= 212992, o
        msw = cb[:, C_MSWA:C_MSWA + 512].rearrange("p (h u) -> p h u", h=2)
        o64 = cb[:, C_O64:C_O64 + 64]
        for half in range(2):
            for t in range(NT):
                sl = slice(t * 512, (t + 1) * 512)
                cosT, sinT, COS, SIN = cosTs[t % 2], sinTs[t % 2], COSs[t % 2], SINs[t % 2]
                if half == 0:
                    dma("sp", posi[:, :], pos_h[:, sl], writes=[POSI])
                    cp("dve", ang[:, :], posi[:, :], [POSI], [ANG])
                    ts("dve", ang[:, :], ang[:, :], cf[:, 0:1], None, ALU.mult, ALU.bypass, [ANG, CF], [ANG])
                    def sin_of(base_ap, BASE, dst, DST):
                        ts("dve", tmpa[:, :], base_ap, INV2PI, MAGIC, ALU.mult, ALU.add, [BASE], [TMPA])
                        ts("dve", tmpa[:, :], tmpa[:, :], MAGIC, None, ALU.subtract, ALU.bypass, [TMPA], [TMPA])
                        stt(tmpb[:, :], tmpa[:, :], -CW1, base_ap, ALU.mult, ALU.add, [TMPA, BASE], [TMPB])
                        stt(tmpb[:, :], tmpa[:, :], -CW2, tmpb[:, :], ALU.mult, ALU.add, [TMPA, TMPB], [TMPB])
                        ts("dve", tmpb[:, :], tmpb[:, :], 3.141592, -3.141592, ALU.min, ALU.max, [TMPB], [TMPB])
                        act(dst, tmpb[:, :], AF.Sin, [TMPB], [DST])
                    sin_of(ang[:, :], ANG, sinT[:, :], SIN)
                    ts("dve", ang[:, :], ang[:, :], 0.5 * PI, None, ALU.add, ALU.bypass, [ANG], [ANG])
                    sin_of(ang[:, :], ANG, cosT[:, :], COS)
                    dma("sp", rt_s[0, :, sl], sinT[:, :], reads=[SIN], key=("rts", 0, t % 2))
                    dma("sp", rt_s[1, :, sl], cosT[:, :], reads=[COS], key=("rts", 1, t % 2))
                else:
                    dma("sp", sinT[:, :], rt_s[0, :, sl], writes=[SIN], key=("rtl", 0, t % 2))
                    dma("sp", cosT[:, :], rt_s[1, :, sl], writes=[COS], key=("rtl", 1, t % 2))
                if half == 0:
                    rope_proj(slice(512, 640), KTA[:, sl], KA[t], t, cosT, sinT, COS, SIN)
                for c in (2 * half, 2 * half + 1):
                    rope_proj(slice(c * 128, (c + 1) * 128), QTA[:, c % 2, sl], QA[c][t], t, cosT, sinT, COS, SIN)
                for j in range(4 if half == 0 else 0):
                    blk = t * 4 + j
                    for k in range(8):
                        mm(ps[:, 4, j * 128:(j + 1) * 128], xT[:, k, blk * 128:(blk + 1) * 128], wA[:, k, 640:768],
                           k == 0, k == 7, [XT[t], WA], [PS[4]])
                if half == 0:
                    cp("act", VA[:, t * 4:(t + 1) * 4, :], ps[:, 4, :].rearrange("p (j d) -> p j d", j=4), [PS[4]], [VAb[t]])

            P.barrier()
            asteps = [(c, kb) for c in (2 * half, 2 * half + 1) for kb in range(NB)]

            def a_front(n):
                c, kb = asteps[n]
                N = 256 if kb < NB - 1 else 128
                i = n % 2
                b0 = 2 * i
                t0 = kb * 128
                tq = [QA[c][(t0) // 512]] + ([QA[c][(t0 + 128) // 512]] if N == 256 else [])
                for h in range(2):
                    r = slice(64 * h, 64 * h + 64)
                    mm(ps[:, b0 + h, 0:N], KTA[r, t0:t0 + 128], QTA[r, c % 2, t0:t0 + N], True, True,
                       [KA[kb // 4]] + tq, [PS[b0 + h]])
                act(Pt[i][:, :, 0:N], ps[:, b0:b0 + 2, 0:N], AF.Exp, [PS[b0], PS[b0 + 1]], [PT[i]], scale=0.125)
                tt("dve", Pt[i][:, :, 0:N], Pt[i][:, :, 0:N], msw[:, :, 0:N], ALU.mult, [PT[i], CB], [PT[i]])

            def a_back(n):
                c, kb = asteps[n]
                N = 256 if kb < NB - 1 else 128
                i = n % 2
                t0 = kb * 128
                ob, db = 4 + kb % 2, 6 + kb % 2
                ob1, db1 = 4 + (kb + 1) % 2, 6 + (kb + 1) % 2
                for h in range(2):
                    r = slice(64 * h, 64 * h + 64)
                    vsl = VA[:, kb, r]
                    mm(ps[r, ob, 0:128], vsl, Pt[i][:, h, 0:128], kb == 0, True, [VAb[kb // 4], PT[i]], [PS[ob]])
                    mm(ps[r, db, 0:128], o64, Pt[i][:, h, 0:128], kb == 0, True, [CB, PT[i]], [PS[db]])
                j = kb % 2
                act(dn[j][:, :], ps[:, db, 0:128], AF.Ln, [PS[db], SK], [DN[j]], bias=sk[:, c:c + 1])
                act(dn[j][:, :], dn[j][:, :], AF.Exp, [DN[j]], [DN[j]], scale=-1.0)
                tt("dve", yAT[:, c, t0:t0 + 128], ps[:, ob, 0:128], dn[j][:, :], ALU.mult,
                   [PS[ob], DN[j]], [YA[kb // 4]])
                if N == 256:
                    for h in range(2):
                        r = slice(64 * h, 64 * h + 64)
                        vsl = VA[:, kb, r]
                        mm(ps[r, ob1, 0:128], vsl, Pt[i][:, h, 128:256], True, False, [VAb[kb // 4], PT[i]], [PS[ob1]])
                        mm(ps[r, db1, 0:128], o64, Pt[i][:, h, 128:256], True, False, [CB, PT[i]], [PS[db1]])

            a_front(0)
            for n in range(len(asteps)):
                if n + 1 < len(asteps):
                    a_front(n + 1)
                a_back(n)
            P.barrier()

        o = LOC
        QTBs = [M.at(o + i * 2 * T, [128, T], BF16) for i in range(2)]; o += 4 * T
        KTBs = [M.at(o + i * 2 * T, [128, T], BF16) for i in range(2)]; o += 4 * T
        VBs = [M.at(o + i * 2 * T, [128, NB, 128], BF16) for i in range(2)]; o += 4 * T
        QBbs = [[Buf("qB%d_%d" % (i, t)) for t in range(NT)] for i in range(2)]
        KBbs = [[Buf("kB%d_%d" % (i, t)) for t in range(NT)] for i in range(2)]
        VBbs = [[Buf("vB%d_%d" % (i, t)) for t in range(NT)] for i in range(2)]
        wB = [M.at(o + i * 6144, [128, 8, 384], BF16) for i in range(2)]; o += 12288
        WB = [Buf("wB%d" % i) for i in range(2)]
        NE = 1
        Et = [M.at(o + i * 4096, [128, 2, 512], F32) for i in range(NE)]; o += 4096 * NE
        ET = [Buf("E%d" % i) for i in range(NE)]
        NL = 2
        Lt = [M.at(o + i * 2048, [128, 2, 512], BF16) for i in range(NL)]; o += 2048 * NL
        LT = [Buf("L%d" % i) for i in range(NL)]
        Ac = [M.at(o + i * 2048, [128, 2, 512], BF16) for i in range(2)]; o += 4096
        AC = [Buf("Ac%d" % i) for i in range(2)]
        At = [M.at(o + i * 2048, [128, 2, 512], BF16) for i in range(2)]; o += 4096
        AT = [Buf("At%d" % i) for i in range(2)]
        assert o <= 212992, o
        msb2 = cb[:, C_MSB:C_MSB + 256].rearrange("p (h u) -> p h u", h=2)
        ntri = cb[:, C_TRI:C_TRI + 128]
        none_ = cb[:, C_ONE:C_ONE + 128]
        PJ = 7

        def wB_load(hp):
            for k in range(0, 8, 4):
                dma("pool", wB[hp % 2][:, k:k + 4, :], wB_h[hp, :, k:k + 4, :], writes=[WB[hp % 2]])

        def proj_ops(hp):
            w, W, si = wB[hp % 2], WB[hp % 2], hp % 2
            ops_ = []
            for t in range(NT):
                sl = slice(t * 512, (t + 1) * 512)
                for which, dst, dbuf in ((0, QTBs[si], QBbs[si]), (1, KTBs[si], KBbs[si])):
                    for k in range(8):
                        ops_.append(lambda k=k, which=which, sl=sl, t=t: mm(
                            ps[:, PJ, :], w[:, k, which * 128:(which + 1) * 128], xT[:, k, sl], k == 0, k == 7,
                            [W, XT[t]], [PS[PJ]]))
                    ops_.append(lambda dst=dst, dbuf=dbuf, sl=sl, t=t: cp("dve", dst[:, sl], ps[:, PJ, :], [PS[PJ]], [dbuf[t]]))
                for j in range(4):
                    blk = t * 4 + j
                    for k in range(8):
                        ops_.append(lambda k=k, j=j, blk=blk, t=t: mm(
                            ps[:, PJ, j * 128:(j + 1) * 128], xT[:, k, blk * 128:(blk + 1) * 128], w[:, k, 256:384],
                            k == 0, k == 7, [XT[t], W], [PS[PJ]]))
                ops_.append(lambda t=t: cp("dve", VBs[si][:, t * 4:(t + 1) * 4, :],
                                          ps[:, PJ, :].rearrange("p (j d) -> p j d", j=4), [PS[PJ]], [VBbs[si][t]]))
            return ops_

        wB_load(0)
        for f_ in proj_ops(0):
            f_()
        for hp in range(4):
            QTB, KTB, VB = QTBs[hp % 2], KTBs[hp % 2], VBs[hp % 2]
            QBb, KBb, VBb = QBbs[hp % 2], KBbs[hp % 2], VBbs[hp % 2]
            nxt = []
            if hp < 3:
                wB_load(hp + 1)
                nxt = proj_ops(hp + 1)

            steps = []
            for qt in range(NT):
                acur = None
                kbs = list(range(4 * qt + 3, -1, -1))
                for si, kb in enumerate(kbs):
                    off = max(0, kb - 4 * qt) * 128
                    st_ = dict(qt=qt, kb=kb, off=off, N=512 - off, t0=qt * 512 + off, diag=kb >= 4 * qt,
                               first=si == 0, last=kb == 0, ob=6, acur=acur)
                    if kb != 0:
                        st_["anew"] = 0 if si == 0 else 1 - acur
                        acur = st_["anew"]
                    steps.append(st_)
            for n_, st_ in enumerate(steps):
                st_["zp"] = 2 * (n_ % 3)
                st_["e"] = n_ % NE
                st_["l"] = n_ % NL
                st_["a"] = n_ % 2

            def s_z(S):
                off, zp, kb, t0, N = S["off"], S["zp"], S["kb"], S["t0"], S["N"]
                for h in range(2):
                    r = slice(64 * h, 64 * h + 64)
                    mm(ps[:, zp + h, off:512], KTB[r, kb * 128:(kb + 1) * 128], QTB[r, t0:t0 + N], True, False,
                       [KBb[kb // 4], QBb[S["qt"]]], [PS[zp + h]])

            def s_el(S):
                off, zp, e_i, l_i = S["off"], S["zp"], S["e"], S["l"]
                Zb = [PS[zp], PS[zp + 1]]
                act(Et[e_i][:, :, off:512], ps[:, zp:zp + 2, off:512], AF.Exp, Zb, [ET[e_i]], scale=0.125)
                act(Lt[l_i][:, :, off:512], Et[e_i][:, :, off:512], AF.Ln, [ET[e_i]], [LT[l_i]], bias=1.0)
                if S["diag"]:
                    tt("dve", Lt[l_i][:, :, off:off + 128], Lt[l_i][:, :, off:off + 128], msb2, ALU.mult,
                       [LT[l_i], CB], [LT[l_i]])
                if not S["last"]:
                    anew, acur = S["anew"], S["acur"]
                    if S["first"]:
                        ms(Ac[anew][:, :, :], 0.0, [AC[anew]])
                        cp("dve", Ac[anew][:, :, off:512], Lt[l_i][:, :, off:512], [LT[l_i]], [AC[anew]])
                    else:
                        if off > 0:
                            ms(Ac[anew][:, :, :], 0.0, [AC[anew]])
                        tt("dve", Ac[anew][:, :, off:512], Ac[acur][:, :, off:512], Lt[l_i][:, :, off:512], ALU.add,
                           [AC[acur], LT[l_i]], [AC[anew]])

            def s_tri(S):
                off, zp, l_i = S["off"], S["zp"], S["l"]
                for h in range(2):
                    mm(ps[:, zp + h, off:512], ntri, Lt[l_i][:, h, off:512], False, S["first"],
                       [CB, LT[l_i]], [PS[zp + h]])
                    if not S["first"]:
                        mm(ps[:, zp + h, off:512], none_, Ac[S["acur"]][:, h, off:512], False, True,
                           [CB, AC[S["acur"]]], [PS[zp + h]])

            def s_a(S):
                off, zp, a_i = S["off"], S["zp"], S["a"]
                Zb = [PS[zp], PS[zp + 1]]
                act(At[a_i][:, :, off:512], ps[:, zp:zp + 2, off:512], AF.Exp, Zb, [AT[a_i]], scale=0.125)
                if S["diag"]:
                    tt("dve", At[a_i][:, :, off:off + 128], At[a_i][:, :, off:off + 128], msb2, ALU.mult,
                       [AT[a_i], CB], [AT[a_i]])

            def s_av(S):
                off, a_i, ob, kb = S["off"], S["a"], S["ob"], S["kb"]
                for h in range(2):
                    r = slice(64 * h, 64 * h + 64)
                    mm(ps[r, ob, off:512], VB[:, kb, r], At[a_i][:, h, off:512], S["first"], S["last"],
                       [VBb[kb // 4], AT[a_i]], [PS[ob]])
                if S["last"]:
                    qt = S["qt"]
                    cp("dve", yBT[:, hp, qt * 512:(qt + 1) * 512], ps[:, ob, :], [PS[ob]], [YB[qt]])

            ns = len(steps)
            for tau in range(ns + 2):
                rounds_left = max(1, ns - 4 - tau)
                take = -(-len(nxt) // rounds_left) if tau < ns - 4 else len(nxt)
                for f_ in nxt[:take]:
                    f_()
                nxt = nxt[take:]
                if tau < ns:
                    s_z(steps[tau])
                if 1 <= tau <= ns:
                    s_tri(steps[tau - 1])
                if 2 <= tau:
                    s_av(steps[tau - 2])
                if tau < ns:
                    s_el(steps[tau])
                if 1 <= tau <= ns:
                    s_a(steps[tau - 1])
        P.barrier()

        if debug:
            o = LOC
            dtmp = M.at(o, [128, 4, 512], F32)
            DT = Buf("dtmp")
            for t in range(NT):
                sl = slice(t * 512, (t + 1) * 512)
                cp("dve", dtmp[:, :, :], yAT[:, :, sl], [YA[t]], [DT])
                dma("sp", dbgA[:, :, sl], dtmp[:, :, :], reads=[DT], key=("dbg", 0))
                cp("dve", dtmp[:, :, :], yBT[:, :, sl], [YB[t]], [DT])
                dma("sp", dbgB[:, :, sl], dtmp[:, :, :], reads=[DT], key=("dbg", 0))
            P.barrier()

        o = LOC
        wa = M.at(o, [128, 4, 1024], BF16); o += 8192
        wb = M.at(o, [128, 4, 1024], BF16); o += 8192
        wo = M.at(o, [128, 8, 1024], BF16); o += 16384
        WAa, WBb, WO = Buf("wa"), Buf("wb"), Buf("wo")
        g1 = M.at(o, [128, D], F32); o += 4096
        b1 = M.at(o, [128, D], F32); o += 4096
        LN1 = Buf("ln1")
        bg = M.at(o, [128, 16], F32); o += 64
        BG = Buf("bg")
        epsb = M.at(o, [128, 1], F32); o += 32
        mhalf = M.at(o, [128, 1], F32); o += 32
        CC = Buf("cc")
        wgc = [M.at(o + i * 4096, [128, 8, 256], BF16) for i in range(2)]; o += 8192
        WG = [Buf("wg%d" % i) for i in range(2)]
        gt = [M.at(o, [128, 2, 512], F32)] * 2; o += 4096
        GT = [Buf("gt")] * 2
        hT = M.at(o, [128, 8, 512], BF16); o += 8192
        HT = [Buf("hT%d" % c) for c in range(8)]
        xin = [M.at(o + i * 4096, [128, D], F32) for i in range(2)]; o += 8192
        XIN = [Buf("xin%d" % i) for i in range(2)]
        x1b = [M.at(o + i * 2048, [128, D], BF16) for i in range(4)]; o += 8192
        X1B = [Buf("x1b%d" % i) for i in range(4)]
        st = [M.at(o + i * 64, [128, 2, 6], F32) for i in range(2)]; o += 128
        mv = [M.at(o + i * 32, [128, 2], F32) for i in range(2)]; o += 64
        rs = [M.at(o + i * 32, [128, 1], F32) for i in range(2)]; o += 64
        nbt = [M.at(o + i * 32, [128, 1], F32) for i in range(2)]; o += 64
        NBT = [Buf("nb%d" % i) for i in range(2)]
        STt = [Buf("st%d" % i) for i in range(2)]
        MV = [Buf("mv%d" % i) for i in range(2)]
        RS = [Buf("rs%d" % i) for i in range(2)]
        assert o <= 212992, o

        for k in range(0, 4, 2):
            dma("pool", wa[:, k:k + 2, :], wa_h[:, k:k + 2, :], writes=[WAa])
            dma("pool", wb[:, k:k + 2, :], wb_h[:, k:k + 2, :], writes=[WBb])
        for k in range(0, 8, 2):
            dma("pool", wo[:, k:k + 2, :], wo_h[:, k:k + 2, :], writes=[WO])
        dma("sp", g1[:, :], ln_h[0], writes=[LN1])
        dma("sp", b1[:, :], ln_h[1], writes=[LN1])
        dma("sp", bg[:, :], bg_h[:, :], writes=[BG])
        ms(epsb[:, :], EPS, [CC])
        ms(mhalf[:, :], -0.5, [CC])

        def layer_norm(yv, Y, gam, bet, LNB, out_ap, OUT, st_, mv_, rs_, ST_, MV_, RS_, eps_, mh_, CC_, nb_, NB_):
            for hh in range(2):
                P.op("dve", lambda e, o_=st_[:, hh, :], i_=yv[:, hh * 512:(hh + 1) * 512]: e.bn_stats(out=o_, in_=i_),
                     reads=[Y], writes=[ST_], multi=True)
            P.op("dve", lambda e, o_=mv_[:, :], i_=st_[:, :, :]: e.bn_aggr(out=o_, in_=i_), reads=[ST_], writes=[MV_])
            ts("pool", rs_[:, :], mv_[:, 1:2], eps_[:, 0:1], None, ALU.add, ALU.bypass, [MV_, CC_], [RS_])
            tt("pool", rs_[:, :], rs_[:, :], mh_[:, :], ALU.pow, [RS_, CC_], [RS_])
            stt(nb_[:, :], mv_[:, 0:1], -1.0, rs_[:, :], ALU.mult, ALU.mult, [MV_, RS_], [NB_])
            act(yv, yv, AF.Identity, [Y, RS_, NB_], [Y], bias=nb_[:, 0:1], scale=rs_[:, 0:1])
            tt("dve", yv, yv, gam, ALU.mult, [Y, LNB], [Y])
            tt("dve", out_ap, yv, bet, ALU.add, [Y, LNB], [OUT])

        def c1_R(t, j):
            rb = 4 + 2 * (j % 2)
            for hf in range(2):
                for c in range(8):
                    mm(ps[:, rb + hf, :], hT[:, c, j * 128:(j + 1) * 128], wo[:, c, hf * 512:(hf + 1) * 512],
                       c == 0, c == 7, [HT[c], WO], [PS[rb + hf]])

        def ln_multi(items, gam, bet, LNB):
            for it_ in items:
                for hh in range(2):
                    P.op("dve", lambda e, o_=it_["st"][:, hh, :], i_=it_["yv"][:, hh * 512:(hh + 1) * 512]:
                         e.bn_stats(out=o_, in_=i_), reads=[it_["Y"]], writes=[it_["ST"]], multi=True)
            for it_ in items:
                P.op("dve", lambda e, o_=it_["mv"][:, :], i_=it_["st"][:, :, :]: e.bn_aggr(out=o_, in_=i_),
                     reads=[it_["ST"]], writes=[it_["MV"]])
            for it_ in items:
                ts("pool", it_["rs"][:, :], it_["mv"][:, 1:2], epsb[:, 0:1], None, ALU.add, ALU.bypass,
                   [it_["MV"], CC], [it_["RS"]])
            for it_ in items:
                tt("pool", it_["rs"][:, :], it_["rs"][:, :], mhalf[:, :], ALU.pow, [it_["RS"], CC], [it_["RS"]])
            for it_ in items:
                stt(it_["nb"][:, :], it_["mv"][:, 0:1], -1.0, it_["rs"][:, :], ALU.mult, ALU.mult,
                    [it_["MV"], it_["RS"]], [it_["NB"]])
            for it_ in items:
                act(it_["yv"], it_["yv"], AF.Identity, [it_["Y"], it_["RS"], it_["NB"]], [it_["Y"]],
                    bias=it_["nb"][:, 0:1], scale=it_["rs"][:, 0:1])
            for it_ in items:
                tt("dve", it_["yv"], it_["yv"], gam, ALU.mult, [it_["Y"], LNB], [it_["Y"]])
            for it_ in items:
                tt("dve", it_["yv"], it_["yv"], bet, ALU.add, [it_["Y"], LNB], [it_["Y"]])

        def c1_chain2(t, js):
            items = []
            for j in js:
                i = j % 2
                blk = t * 4 + j
                dma("sp", xin[i][:, :], x_h[blk * 128:(blk + 1) * 128, :], writes=[XIN[i]], key=("xin", i))
            for j in js:
                i = j % 2
                rb = 4 + 2 * (j % 2)
                stt(xin[i][:, :], xin[i][:, :], ALPHA, ps[:, rb:rb + 2, :].rearrange("p a n -> p (a n)"),
                    ALU.mult, ALU.add, [XIN[i], PS[rb], PS[rb + 1]], [XIN[i]])
                items.append(dict(yv=xin[i][:, :], Y=XIN[i], st=st[i], mv=mv[i], rs=rs[i], nb=nbt[i],
                                  ST=STt[i], MV=MV[i], RS=RS[i], NB=NBT[i]))
            ln_multi(items, g1[:, :], b1[:, :], LN1)
            for j in js:
                i = j % 2
                blk = t * 4 + j
                dma("sp", x1_s[blk * 128:(blk + 1) * 128, :], xin[i][:, :], reads=[XIN[i]], key=("x1s", i))
                cp("act", x1b[j][:, :], xin[i][:, :], [XIN[i]], [X1B[j]])

        def c1_T(t, j):
            blk = t * 4 + j
            tb = 4 + j
            tpb = ps[:, tb, :].bitcast(BF16)
            for c in range(8):
                P.op("pe", lambda e, o_=tpb[:, c * 128:(c + 1) * 128], i_=x1b[j][:, c * 128:(c + 1) * 128],
                     id_=cb[:, C_ID:C_ID + 128]: e.transpose(out=o_, in_=i_, identity=id_),
                     reads=[X1B[j], CB], writes=[PS[tb]])
            cp("dve", xT[:, :, blk * 128:(blk + 1) * 128], tpb.rearrange("p (c n) -> p c n", c=8),
               [PS[tb]], [XT[t]])

        gi = 0
        bi = 0

        def wg_load(n):
            if n < NT * 8:
                dma("pool", wgc[n % 2][:, :, :], wg_h[n % 8, :, :, :], writes=[WG[n % 2]], key=("wg", n % 2))
        wg_load(0)
        for t in range(NT):
            sl = slice(t * 512, (t + 1) * 512)
            for c in range(8):
                wi = gi % 2
                gi2 = gi % 2
                gi += 1
                gb = 0
                for ab in range(2):
                    for k in range(8):
                        mm(ps[:, gb + ab, :], wgc[wi][:, k, ab * 128:(ab + 1) * 128], xT[:, k, sl], k == 0, k == 7,
                           [WG[wi], XT[t]], [PS[gb + ab]])
                    act(gt[gi2][:, ab, :], ps[:, gb + ab, :], AF.Sigmoid, [PS[gb + ab], BG], [GT[gi2]],
                        bias=bg[:, ab * 8 + c:ab * 8 + c + 1])
                wg_load(gi)
                for k in range(4):
                    mm(ps[:, 2, :], wa[:, k, c * 128:(c + 1) * 128], yAT[:, k, sl], k == 0, k == 3, [WAa, YA[t]], [PS[2]])
                for k in range(4):
                    mm(ps[:, 3, :], wb[:, k, c * 128:(c + 1) * 128], yBT[:, k, sl], k == 0, k == 3, [WBb, YB[t]], [PS[3]])
                tt("dve", gt[gi2][:, 0, :], gt[gi2][:, 0, :], ps[:, 2, :], ALU.mult, [GT[gi2], PS[2]], [GT[gi2]])
                tt("dve", gt[gi2][:, 1, :], gt[gi2][:, 1, :], ps[:, 3, :], ALU.mult, [GT[gi2], PS[3]], [GT[gi2]])
                if t > 0 and c % 2 == 1:
                    c1_T(t - 1, c // 2)
                tt("dve", hT[:, c, :], gt[gi2][:, 0, :], gt[gi2][:, 1, :], ALU.add, [GT[gi2]], [HT[c]])
            c1_R(t, 0)
            c1_R(t, 1)
            c1_chain2(t, (0, 1))
            c1_R(t, 2)
            c1_R(t, 3)
            c1_chain2(t, (2, 3))
        for j in range(4):
            c1_T(NT - 1, j)
        P.barrier()

        o = LOC2
        TQ = min(1024, T)
        NQt = T // TQ
        HPQ = TQ // 512
        wdn = M.at(o, [128, NJ, 1024], BF16); o += NJ * 2048
        WDN = Buf("wdn")
        hid = M.at(o, [128, NJ, TQ], BF16); o += NJ * TQ * 2
        HID = [[Buf("hid%d_%d" % (j, hh)) for hh in range(HPQ)] for j in range(NJ)]
        g2 = M.at(o, [128, D], F32); o += 4096
        b2 = M.at(o, [128, D], F32); o += 4096
        LN2 = Buf("ln2")
        cpm = M.at(o, [128, 2 * NJ, 4], F32); o += 2 * NJ * 16
        CPM = Buf("cpm")
        halo = M.at(o, [128, 2 * NJ, 2], F32); o += 2 * NJ * 8
        HALO = [Buf("halo%d" % jj) for jj in range(2 * NJ)]
        o = (o + 31) // 32 * 32
        wupc = [M.at(o + i * 4096, [128, 8, 256], BF16) for i in range(3)]; o += 12288
        WUP = [Buf("wup%d" % i) for i in range(3)]
        U = [[M.at(o + (i * 2 + g) * 2080, [128, 514], F32) for g in range(2)] for i in range(2)]; o += 4 * 2080
        UB = [[Buf("U%d_%d" % (i, g)) for g in range(2)] for i in range(2)]
        Aa = [[M.at(o + (i * 2 + g) * 2048, [128, 512], F32) for g in range(2)] for i in range(3)]; o += 6 * 2048
        AB = [[Buf("A%d_%d" % (i, g)) for g in range(2)] for i in range(3)]
        xin = [M.at(o + i * 4096, [128, D], F32) for i in range(2)]; o += 8192
        XIN = [Buf("xin2_%d" % i) for i in range(2)]
        st = [M.at(o + i * 64, [128, 2, 6], F32) for i in range(2)]; o += 128
        mv = [M.at(o + i * 32, [128, 2], F32) for i in range(2)]; o += 64
        rs = [M.at(o + i * 32, [128, 1], F32) for i in range(2)]; o += 64
        nbt = [M.at(o + i * 32, [128, 1], F32) for i in range(2)]; o += 64
        NBT = [Buf("nb2%d" % i) for i in range(2)]
        epsb = M.at(o, [128, 1], F32); o += 32
        mhalf = M.at(o, [128, 1], F32); o += 32
        STt = [Buf("st2%d" % i) for i in range(2)]
        MV = [Buf("mv2%d" % i) for i in range(2)]
        RS = [Buf("rs2%d" % i) for i in range(2)]
        CC = Buf("cc2")
        assert o <= 212992, o

        for j in range(0, NJ, 2):
            dma("pool", wdn[:, j:j + 2, :], wdn_h[:, j:j + 2, :], writes=[WDN])
        dma("sp", g2[:, :], ln_h[2], writes=[LN2])
        dma("sp", b2[:, :], ln_h[3], writes=[LN2])
        dma("sp", cpm[:, :, :], cp_h[:, :, :], writes=[CPM])
        ms(epsb[:, :], EPS, [CC])
        ms(mhalf[:, :], -0.5, [CC])
        ms(halo[:, :, :], 0.0, HALO)

        ui = 0
        wi_ = 0
        bi = 0
        wseq = [(q, j) for q in range(NQt) for j in range(NJ)]

        def wup_load(n):
            if n < len(wseq):
                dma("pool", wupc[n % 3][:, :, :], wup_h[wseq[n][1], :, :, :], writes=[WUP[n % 3]], key=("wup", n % 3))
        wup_load(0)
        wup_load(1)
        def ln_front(it_):
            for hh in range(2):
                P.op("dve", lambda e, o_=it_["st"][:, hh, :], i_=it_["yv"][:, hh * 512:(hh + 1) * 512]:
                     e.bn_stats(out=o_, in_=i_), reads=[it_["Y"]], writes=[it_["ST"]], multi=True)
            P.op("dve", lambda e, o_=it_["mv"][:, :], i_=it_["st"][:, :, :]: e.bn_aggr(out=o_, in_=i_),
                 reads=[it_["ST"]], writes=[it_["MV"]])
            ts("pool", it_["rs"][:, :], it_["mv"][:, 1:2], epsb[:, 0:1], None, ALU.add, ALU.bypass,
               [it_["MV"], CC], [it_["RS"]])
            tt("pool", it_["rs"][:, :], it_["rs"][:, :], mhalf[:, :], ALU.pow, [it_["RS"], CC], [it_["RS"]])
            stt(it_["nb"][:, :], it_["mv"][:, 0:1], -1.0, it_["rs"][:, :], ALU.mult, ALU.mult,
                [it_["MV"], it_["RS"]], [it_["NB"]])
            act(it_["yv"], it_["yv"], AF.Identity, [it_["Y"], it_["RS"], it_["NB"]], [it_["Y"]],
                bias=it_["nb"][:, 0:1], scale=it_["rs"][:, 0:1])

        def ln_back(it_, gam, bet, LNB):
            tt("dve", it_["yv"], it_["yv"], gam, ALU.mult, [it_["Y"], LNB], [it_["Y"]])
            tt("dve", it_["yv"], it_["yv"], bet, ALU.add, [it_["Y"], LNB], [it_["Y"]])

        pend = [None]

        def flush_pend():
            if pend[0] is not None:
                o_, a_, b_, rd, wr = pend[0]
                act(a_, a_, AF.Silu, [rd[0]], [rd[0]])
                tt("dve", o_, a_, b_, ALU.mult, rd, wr)
                pend[0] = None

        def down_front(q, jb):
            blk = q * (TQ // 128) + jb
            i = blk % 2
            rb = 4 + 2 * i
            dma("sp", xin[i][:, :], x1_s[blk * 128:(blk + 1) * 128, :], writes=[XIN[i]], key=("xin", i))
            for hf in range(2):
                for j in range(NJ):
                    mm(ps[:, rb + hf, :], hid[:, j, jb * 128:(jb + 1) * 128], wdn[:, j, hf * 512:(hf + 1) * 512],
                       j == 0, j == NJ - 1, [HID[j][jb // 4], WDN], [PS[rb + hf]])
            stt(xin[i][:, :], xin[i][:, :], ALPHA, ps[:, rb:rb + 2, :].rearrange("p a n -> p (a n)"),
                ALU.mult, ALU.add, [XIN[i], PS[rb], PS[rb + 1]], [XIN[i]])
            it_ = dict(yv=xin[i][:, :], Y=XIN[i], st=st[i], mv=mv[i], rs=rs[i], nb=nbt[i],
                       ST=STt[i], MV=MV[i], RS=RS[i], NB=NBT[i], blk=blk, i=i)
            ln_front(it_)
            return it_

        def down_back(it_):
            ln_back(it_, g2[:, :], b2[:, :], LN2)
            blk, i = it_["blk"], it_["i"]
            dma("sp", out_h[blk * 128:(blk + 1) * 128, :], xin[i][:, :], reads=[XIN[i]], key=("out", i))

        for q in range(NQt):
            for j in range(NJ):
                wi = wi_ % 3
                wup_load(wi_ + 2)
                wi_ += 1
                for hh in range(HPQ):
                    tI = q * HPQ + hh
                    sl = slice(tI * 512, (tI + 1) * 512)
                    i = ui % 2
                    ia = ui % 3
                    ui += 1
                    for g in range(2):
                        jj = g * NJ + j
                        b = 2 * i + g
                        for k in range(8):
                            mm(ps[:, b, :], wupc[wi][:, k, g * 128:(g + 1) * 128], xT[:, k, sl], k == 0, k == 7,
                               [WUP[wi], XT[tI]], [PS[b]])
                        cp("pool", U[i][g][:, 0:2], halo[:, jj, :], [HALO[jj]], [UB[i][g]])
                        P.op("act", lambda e, o_=U[i][g][:, 2:514], i_=ps[:, b, :]: e.activation(out=o_, in_=i_, func=AF.Copy),
                             reads=[PS[b]], writes=[UB[i][g]], multi=True)
                        act(Aa[ia][g][:, :], ps[:, b, :], AF.Identity, [PS[b], CPM], [AB[ia][g]],
                            bias=cpm[:, jj, 3:4], scale=cpm[:, jj, 2:3])
                        cp("pool", halo[:, jj, :], U[i][g][:, 512:514], [UB[i][g]], [HALO[jj]])
                        stt(Aa[ia][g][:, :], U[i][g][:, 1:513], cpm[:, jj, 1:2], Aa[ia][g][:, :], ALU.mult, ALU.add,
                            [UB[i][g], CPM, AB[ia][g]], [AB[ia][g]])
                        stt(Aa[ia][g][:, :], U[i][g][:, 0:512], cpm[:, jj, 0:1], Aa[ia][g][:, :], ALU.mult, ALU.add,
                            [UB[i][g], CPM, AB[ia][g]], [AB[ia][g]])
                    flush_pend()
                    pend[0] = (hid[:, j, hh * 512:(hh + 1) * 512], Aa[ia][0][:, :], Aa[ia][1][:, :],
                               [AB[ia][0], AB[ia][1]], [HID[j][hh]])
            flush_pend()
            prev = None
            for jb in range(TQ // 128):
                cur = down_front(q, jb)
                if prev is not None:
                    down_back(prev)
                prev = cur
            down_back(prev)
        P.barrier()
        P.emit(nc, stack)
    return nc


def host_prep(b, x, positions, w_in, b_gate, sinks, w_branch_a, w_branch_b, w_out,
              ln1_g, ln1_b, w_up, conv_w, conv_b, w_down, ln2_g, ln2_b, shared):
    m = dict(shared)
    m["xT"] = np.ascontiguousarray(x[b].T)
    m["x"] = np.ascontiguousarray(x[b])
    m["pos"] = np.ascontiguousarray(np.broadcast_to(positions[b][None, :].astype(np.int32), (128, positions.shape[1])))
    return m


def host_shared(w_in, b_gate, sinks, w_branch_a, w_branch_b, w_out,
                ln1_g, ln1_b, w_up, conv_w, conv_b, w_down, ln2_g, ln2_b):
    f = np.float32
    w_in = w_in[0]
    cbm = np.zeros((128, NCB), f)
    cbm[:, C_ID:C_ID + 128] = np.eye(128, dtype=f)
    rot = np.zeros((128, 128), f)
    for m_ in range(128):
        base, ml = (m_ // 64) * 64, m_ % 64
        if ml < 32:
            rot[base + ml + 32, m_] = -1.0
        else:
            rot[base + ml - 32, m_] = 1.0
    cbm[:, C_ROT:C_ROT + 128] = rot
    jj, ss = np.meshgrid(np.arange(128), np.arange(128), indexing="ij")
    cbm[:, C_TRI:C_TRI + 128] = np.where(jj >= ss, -8.0, 0.0)
    cbm[:, C_ONE:C_ONE + 128] = -8.0
    s_, t_ = np.meshgrid(np.arange(128), np.arange(128), indexing="ij")
    msb = (s_ < t_).astype(f)
    cbm[:, C_MSB:C_MSB + 128] = msb
    cbm[:, C_MSB + 128:C_MSB + 256] = msb
    mswa = np.concatenate([(s_ <= t_).astype(f), (s_ > t_).astype(f)], axis=1)
    cbm[:, C_MSWA:C_MSWA + 256] = mswa
    cbm[:, C_MSWA + 256:C_MSWA + 512] = mswa
    cbm[:, C_O64:C_O64 + 64] = 1.0
    cfm = np.zeros((128, 8), f)
    inv = (f(1.0) / np.power(f(10000.0), np.arange(0, 64, 2, dtype=f) / f(64.0))).astype(f)
    cfm[:, 0] = inv[np.arange(128) % 32]
    def kp(w):
        K, N = w.shape
        return np.ascontiguousarray(w.reshape(K // 128, 128, N).transpose(1, 0, 2))
    qa_cols = np.concatenate([np.r_[c * 64:(c + 1) * 64, (4 + c) * 64:(5 + c) * 64] for c in range(4)])
    wA = np.concatenate([w_in[:, 0:512][:, qa_cols], w_in[:, 512:640], w_in[:, 640:768]], axis=1)
    wB = np.stack([np.concatenate([w_in[:, 768 + hp * 128:768 + (hp + 1) * 128],
                                   w_in[:, 1280 + hp * 128:1280 + (hp + 1) * 128],
                                   w_in[:, 1792 + hp * 128:1792 + (hp + 1) * 128]], axis=1) for hp in range(4)])
    wgA, wgB = w_in[:, 2304:3328], w_in[:, 3328:4352]
    wg = np.stack([np.concatenate([wgA[:, c * 128:(c + 1) * 128], wgB[:, c * 128:(c + 1) * 128]], axis=1)
                   for c in range(8)])
    frow = np.array([[(c if p < 64 else 4 + c) * 64 + p % 64 for c in range(4)] for p in range(128)])
    wa = np.ascontiguousarray(w_branch_a[0][frow, :])
    wup = w_up[0]
    wupr = np.stack([np.concatenate([wup[:, j * 128:(j + 1) * 128], wup[:, DFF + j * 128:DFF + (j + 1) * 128]], axis=1)
                     for j in range(NJ)])
    cpm = np.zeros((128, 2 * NJ, 4), f)
    cw, cbias = conv_w[0], conv_b[0]
    for jj_ in range(2 * NJ):
        ch = jj_ * 128 + np.arange(128)
        cpm[:, jj_, 0:3] = cw[:, ch].T
        cpm[:, jj_, 3] = cbias[ch]
    bgm = np.zeros((128, 16), f)
    for c in range(8):
        bgm[:, c] = b_gate[0][c * 128:(c + 1) * 128]
        bgm[:, 8 + c] = b_gate[0][1024 + c * 128:1024 + (c + 1) * 128]
    skm = np.zeros((128, 4), f)
    for c in range(4):
        skm[0:64, c] = sinks[0][c]
        skm[64:128, c] = sinks[0][4 + c]
    lnm = np.stack([np.broadcast_to(v[0][None, :], (128, D)) for v in (ln1_g, ln1_b, ln2_g, ln2_b)]).astype(f)
    return dict(
        cb=cbm, cf=cfm,
        wA=kp(wA), wB=np.stack([kp(wB[hp]) for hp in range(4)]),
        wg=np.stack([kp(wg[c]) for c in range(8)]),
        wa=wa, wb=kp(w_branch_b[0]), wo=kp(w_out[0]),
        wup=np.stack([kp(wupr[j]) for j in range(NJ)]), wdn=kp(w_down[0]),
        cp=cpm, bg=bgm, sk=skm, ln=np.ascontiguousarray(lnm),
    )


_NC_CACHE = {}


def kernel(x, positions, w_in, b_gate, sinks, w_branch_a, w_branch_b, w_out,
           ln1_g, ln1_b, w_up, conv_w, conv_b, w_down, ln2_g, ln2_b):
    args = [np.asarray(a) for a in (x, positions, w_in, b_gate, sinks, w_branch_a, w_branch_b, w_out,
                                    ln1_g, ln1_b, w_up, conv_w, conv_b, w_down, ln2_g, ln2_b)]
    x, positions = args[0], args[1]
    B, T, _ = x.shape
    shared = host_shared(*args[2:])
    in_maps = [host_prep(b, x, positions, *args[2:], shared) for b in range(B)]
    nc = build(T)
    res = run_bass_kernel_spmd(nc, in_maps, core_ids=list(range(B)))
    return np.stack([np.asarray(r["out"]) for r in res.results]).astype(np.float32)
```

```python
import numpy as np
import concourse.bass as bass
import concourse.mybir as mybir
from concourse.bass_utils import run_bass_kernel_spmd

F32 = mybir.dt.float32
BF16 = mybir.dt.bfloat16
I32 = mybir.dt.int32
AF = mybir.ActivationFunctionType
ALU = mybir.AluOpType

D = 1024
DFF = 2816
NJ = DFF // 128
HD = 64
ALPHA = float(2.0 ** 0.25)
EPS = 1e-5
PI = float(np.pi)
INV2PI = float(np.float32(1.0 / (2.0 * np.pi)))
MAGIC = 12582912.0
CW1 = 6.28125
CW2 = float(2.0 * np.pi - 6.28125)
SBUF_BASE = 16384

C_ID, C_ROT, C_TRI, C_ONE, C_MSB, C_MSWA, C_O64, NCB = 0, 128, 256, 384, 512, 768, 1280, 1344


class Buf:
    __slots__ = ("name", "w", "r", "psum")

    def __init__(self, name, psum=False):
        self.name = name
        self.w = []
        self.r = []
        self.psum = psum


class Prog:
    ENGS = ("pe", "act", "dve", "pool", "sp")

    def __init__(self):
        self.ops = []
        self.bar_start = 0

    def op(self, eng, fn, reads=(), writes=(), dma=None, multi=False):
        i = len(self.ops)
        deps = set()
        for b in reads:
            deps.update(b.w)
            if b.psum:
                deps.update(r for r in b.r if self.ops[r]["eng"] != eng)
        for b in writes:
            if not multi:
                deps.update(b.w)
            deps.update(b.r)
        for b in reads:
            b.r.append(i)
        for b in writes:
            if multi and not b.r:
                b.w = b.w + [i]
            else:
                b.w = [i]
            b.r = []
        deps.discard(i)
        last = {}
        keep = set()
        for d in deps:
            od = self.ops[d]
            if od["dma"] is not None:
                keep.add(d)
            elif last.get(od["eng"], -1) < d:
                last[od["eng"]] = d
        keep.update(last.values())
        self.ops.append(dict(eng=eng, fn=fn, deps=keep, dma=dma))
        return i

    def barrier(self):
        last = {}
        deps = set()
        for idx in range(self.bar_start, len(self.ops)):
            o = self.ops[idx]
            if o["fn"] is None:
                continue
            if o["dma"] is not None:
                deps.add(idx)
            else:
                last[o["eng"]] = idx
        deps.update(last.values())
        for e in self.ENGS:
            self.ops.append(dict(eng=e, fn=None, deps=set(deps), dma=None))
        self.bar_start = len(self.ops)

    def emit(self, nc, stack):
        ops = self.ops
        need = set()
        for o in ops:
            for d in o["deps"]:
                od = ops[d]
                if o["eng"] == "pe" and od["eng"] == "pe" and od["dma"] is None:
                    continue
                need.add(d)
        esem = {e: stack.enter_context(nc.semaphore("s_" + e)) for e in self.ENGS}
        dsem = {}
        tick = {e: 0 for e in self.ENGS}
        dcnt = {}
        sig = {}
        for i, o in enumerate(ops):
            if i not in need or o["fn"] is None:
                continue
            if o["dma"] is not None:
                k = o["dma"]
                if k not in dsem:
                    dsem[k] = stack.enter_context(nc.semaphore("d_%d" % len(dsem)))
                    dcnt[k] = 0
                dcnt[k] += 16
                sig[i] = (dsem[k], dcnt[k], 16)
            else:
                tick[o["eng"]] += 1
                sig[i] = (esem[o["eng"]], tick[o["eng"]], 1)
        per = {e: [] for e in self.ENGS}
        for i, o in enumerate(ops):
            per[o["eng"]].append(i)
        self.stats = dict(ticks=dict(tick), nsem=len(dsem) + len(esem), nops=len(ops),
                          dmax=max(dcnt.values()) if dcnt else 0)
        block = stack.enter_context(nc.Block())

        def run(eng, ename):
            waited = {}
            for i in per[ename]:
                o = ops[i]
                for d in sorted(o["deps"]):
                    if d not in sig:
                        continue
                    od = ops[d]
                    if ename == "pe" and od["eng"] == "pe" and od["dma"] is None:
                        continue
                    sem, val, _ = sig[d]
                    key = id(sem)
                    if waited.get(key, 0) >= val:
                        continue
                    eng.wait_ge(sem, val)
                    waited[key] = val
                if o["fn"] is not None:
                    ins = o["fn"](eng)
                    if i in sig:
                        ins.then_inc(sig[i][0], sig[i][2])

        @block.tensor
        def _(e):
            run(e, "pe")

        @block.scalar
        def _(e):
            run(e, "act")

        @block.vector
        def _(e):
            run(e, "dve")

        @block.gpsimd
        def _(e):
            run(e, "pool")

        @block.sync
        def _(e):
            run(e, "sp")


class Mem:
    def __init__(self, nc):
        self.nc = nc
        self.n = 0

    def at(self, off, shape, dt):
        self.n += 1
        return self.nc.alloc_sbuf_tensor_at("t%d" % self.n, list(shape), dt, offset=SBUF_BASE + off).ap()


def build(T=4096, debug=False):
    NB = T // 128
    NT = T // 512
    nc = bass.Bass("TRN2", target_bir_lowering=False)
    P = Prog()
    M = Mem(nc)

    def din(name, shape, dt=F32):
        return nc.dram_tensor(name, list(shape), dt, kind="ExternalInput").ap()

    xT_h = din("xT", [D, T])
    x_h = din("x", [T, D])
    pos_h = din("pos", [128, T], I32)
    cb_h = din("cb", [128, NCB])
    cf_h = din("cf", [128, 8])
    wA_h = din("wA", [128, 8, 768])
    wB_h = din("wB", [4, 128, 8, 384])
    wg_h = din("wg", [8, 128, 8, 256])
    wa_h = din("wa", [128, 4, 1024])
    wb_h = din("wb", [128, 4, 1024])
    wo_h = din("wo", [128, 8, 1024])
    wup_h = din("wup", [NJ, 128, 8, 256])
    wdn_h = din("wdn", [128, NJ, 1024])
    cp_h = din("cp", [128, 2 * NJ, 4])
    bg_h = din("bg", [128, 16])
    sk_h = din("sk", [128, 4])
    ln_h = din("ln", [4, 128, D])
    out_h = nc.dram_tensor("out", [T, D], F32, kind="ExternalOutput").ap()
    x1_s = nc.dram_tensor("x1s", [T, D], F32, kind="Internal").ap()
    rt_s = nc.dram_tensor("rts", [2, 128, T], F32, kind="Internal").ap()
    if debug:
        dbgA = nc.dram_tensor("dbgA", [128, 4, T], F32, kind="ExternalOutput").ap()
        dbgB = nc.dram_tensor("dbgB", [128, 4, T], F32, kind="ExternalOutput").ap()

    import contextlib
    stack = contextlib.ExitStack()
    with stack:
        ps = stack.enter_context(nc.psum_tensor([128, 8, 512], F32))
        PS = [Buf("ps%d" % b, psum=True) for b in range(8)]

        o = 0
        cb = M.at(o, [128, NCB], BF16); o += 2 * NCB
        CB = Buf("cb")
        cf = M.at(o, [128, 8], F32); o += 32
        CF = Buf("cf")
        xT = M.at(o, [128, 8, T], BF16); o += 16 * T
        XT = [Buf("xT%d" % t) for t in range(NT)]
        LOC2 = o
        yAT = M.at(o, [128, 4, T], BF16); o += 8 * T
        YA = [Buf("yA%d" % t) for t in range(NT)]
        yBT = M.at(o, [128, 4, T], BF16); o += 8 * T
        YB = [Buf("yB%d" % t) for t in range(NT)]
        LOC = o
        assert LOC % 32 == 0

        def dma(eng, out, in_, reads=(), writes=(), key=None, multi=True):
            if key is None:
                key = ("w", writes[0].name) if writes else ("r", reads[0].name)
            return P.op(eng, lambda e, out=out, in_=in_: e.dma_start(out=out, in_=in_),
                        reads=reads, writes=writes, dma=key, multi=multi)

        def ms(ap, val, writes):
            return P.op("pool", lambda e, ap=ap, val=val: e.memset(ap, val), writes=writes)

        def mm(out, lhsT, rhs, start, stop, reads, writes):
            return P.op("pe", lambda e, out=out, lhsT=lhsT, rhs=rhs, start=start, stop=stop:
                        e.matmul(out, lhsT=lhsT, rhs=rhs, start=start, stop=stop, skip_group_check=True),
                        reads=reads, writes=writes)

        def act(out, in_, func, reads, writes, bias=0.0, scale=1.0):
            return P.op("act", lambda e, out=out, in_=in_, func=func, bias=bias, scale=scale:
                        e.activation(out=out, in_=in_, func=func, bias=bias, scale=scale),
                        reads=reads, writes=writes)

        def tt(eng, out, in0, in1, op, reads, writes):
            return P.op(eng, lambda e, out=out, in0=in0, in1=in1, op=op:
                        e.tensor_tensor(out=out, in0=in0, in1=in1, op=op), reads=reads, writes=writes)

        def ts(eng, out, in0, s1, s2, op0, op1, reads, writes):
            return P.op(eng, lambda e, out=out, in0=in0, s1=s1, s2=s2, op0=op0, op1=op1:
                        e.tensor_scalar(out=out, in0=in0, scalar1=s1, scalar2=s2, op0=op0, op1=op1),
                        reads=reads, writes=writes)

        def stt(out, in0, scalar, in1, op0, op1, reads, writes):
            return P.op("dve", lambda e, out=out, in0=in0, scalar=scalar, in1=in1, op0=op0, op1=op1:
                        e.scalar_tensor_tensor(out=out, in0=in0, scalar=scalar, in1=in1, op0=op0, op1=op1),
                        reads=reads, writes=writes)

        def cp(eng, out, in_, reads, writes):
            if eng == "act":
                return act(out, in_, AF.Copy, reads, writes)
            return P.op(eng, lambda e, out=out, in_=in_: e.tensor_copy(out=out, in_=in_),
                        reads=reads, writes=writes)

        dma("pool", cb[:, :], cb_h[:, :], writes=[CB])
        dma("sp", cf[:, :], cf_h[:, :], writes=[CF])
        xv = xT_h.rearrange("(c p) t -> p c t", p=128)

        def load_xT(t):
            dma("pool", xT[:, :, t * 512:(t + 1) * 512], xv[:, :, t * 512:(t + 1) * 512], writes=[XT[t]])
        load_xT(0)

        o = LOC
        QTA = M.at(o, [128, 2, T], BF16); o += 4 * T
        QA = [[Buf("qa%d_%d" % (c, t)) for t in range(NT)] for c in range(4)]
        KTA = M.at(o, [128, T], BF16); o += 2 * T
        KA = [Buf("ka%d" % t) for t in range(NT)]
        VA = M.at(o, [128, NB, 128], BF16); o += 2 * T
        VAb = [Buf("va%d" % t) for t in range(NT)]
        sk = M.at(o, [128, 4], F32); o += 32
        SK = Buf("sk")
        npi = M.at(o, [128, 1], F32); o += 32
        NPI = Buf("npi")
        A2 = o
        wA = M.at(o, [128, 8, 768], BF16); o += 8 * 768 * 2
        WA = Buf("wA")
        ang = M.at(o, [128, 512], F32); o += 2048
        tmpa = M.at(o, [128, 512], F32); o += 2048
        tmpb = M.at(o, [128, 512], F32); o += 2048
        posi = tmpb.bitcast(I32)
        TMPB = Buf("tmpb")
        cosTs = [M.at(o + i * 2048, [128, 512], F32) for i in range(2)]; o += 4096
        sinTs = [M.at(o + i * 2048, [128, 512], F32) for i in range(2)]; o += 4096
        COSs = [Buf("cos%d" % i) for i in range(2)]
        SINs = [Buf("sin%d" % i) for i in range(2)]
        ANG, TMPA = Buf("ang"), Buf("tmpa")
        POSI = TMPB
        q32 = [M.at(o + i * 2048, [128, 512], F32) for i in range(2)]; o += 4096
        qb = [M.at(o + i * 1024, [128, 512], BF16) for i in range(2)]; o += 2048
        t2 = [M.at(o + i * 2048, [128, 512], F32) for i in range(2)]; o += 4096
        Q32 = [Buf("q32_%d" % i) for i in range(2)]
        QB = [Buf("qb_%d" % i) for i in range(2)]
        T2 = [Buf("t2_%d" % i) for i in range(2)]
        assert o <= 212992, o

        for k in range(0, 8, 2):
            dma("pool", wA[:, k:k + 2, :], wA_h[:, k:k + 2, :], writes=[WA])
        for t in range(1, NT):
            load_xT(t)
        dma("sp", sk[:, :], sk_h[:, :], writes=[SK])
        ms(npi[:, :], -PI, [NPI])
        act(sk[:, :], sk[:, :], AF.Exp, [SK], [SK])

        ctr = [0]

        def rope_proj(wcols, dst, dstbuf, t, cosT, sinT, COS, SIN):
            i = ctr[0] % 2
            ctr[0] += 1
            b0, b1 = 2 * i, 2 * i + 1
            for k in range(8):
                mm(ps[:, b0, :], wA[:, k, wcols], xT[:, k, t * 512:(t + 1) * 512], k == 0, k == 7,
                   [WA, XT[t]], [PS[b0]])
            act(q32[i][:, :], ps[:, b0, :], AF.Copy, [PS[b0]], [Q32[i]])
            cp("act", qb[i][:, :], ps[:, b0, :], [PS[b0]], [QB[i]])
            mm(ps[:, b1, :], cb[:, C_ROT:C_ROT + 128], qb[i][:, :], True, True, [CB, QB[i]], [PS[b1]])
            tt("dve", q32[i][:, :], q32[i][:, :], cosT[:, :], ALU.mult, [Q32[i], COS], [Q32[i]])
            tt("dve", t2[i][:, :], ps[:, b1, :], sinT[:, :], ALU.mult, [PS[b1], SIN], [T2[i]])
            tt("dve", dst, q32[i][:, :], t2[i][:, :], ALU.add, [Q32[i], T2[i]], [dstbuf])

        Pt = [M.at(o + i * 1024, [128, 2, 256], BF16) for i in range(2)]; o += 2048
        PT = [Buf("pt%d" % i) for i in range(2)]
        dn = [M.at(o + i * 512, [128, 128], F32) for i in range(2)]; o += 1024
        DN = [Buf("dn%d" % i) for i in range(2)]
        assert o <= 212992, o
        msw = cb[:, C_MSWA:C_MSWA + 512].rearrange("p (h u) -> p h u", h=2)
        o64 = cb[:, C_O64:C_O64 + 64]
        for half in range(2):
            for t in range(NT):
                sl = slice(t * 512, (t + 1) * 512)
                cosT, sinT, COS, SIN = cosTs[t % 2], sinTs[t % 2], COSs[t % 2], SINs[t % 2]
                if half == 0:
                    dma("sp", posi[:, :], pos_h[:, sl], writes=[POSI])
                    cp("dve", ang[:, :], posi[:, :], [POSI], [ANG])
                    ts("dve", ang[:, :], ang[:, :], cf[:, 0:1], None, ALU.mult, ALU.bypass, [ANG, CF], [ANG])
                    def sin_of(base_ap, BASE, dst, DST):
                        ts("dve", tmpa[:, :], base_ap, INV2PI, MAGIC, ALU.mult, ALU.add, [BASE], [TMPA])
                        ts("dve", tmpa[:, :], tmpa[:, :], MAGIC, None, ALU.subtract, ALU.bypass, [TMPA], [TMPA])
                        stt(tmpb[:, :], tmpa[:, :], -CW1, base_ap, ALU.mult, ALU.add, [TMPA, BASE], [TMPB])
                        stt(tmpb[:, :], tmpa[:, :], -CW2, tmpb[:, :], ALU.mult, ALU.add, [TMPA, TMPB], [TMPB])
                        ts("dve", tmpb[:, :], tmpb[:, :], 3.141592, -3.141592, ALU.min, ALU.max, [TMPB], [TMPB])
                        act(dst, tmpb[:, :], AF.Sin, [TMPB], [DST])
                    sin_of(ang[:, :], ANG, sinT[:, :], SIN)
                    ts("dve", ang[:, :], ang[:, :], 0.5 * PI, None, ALU.add, ALU.bypass, [ANG], [ANG])
                    sin_of(ang[:, :], ANG, cosT[:, :], COS)
                    dma("sp", rt_s[0, :, sl], sinT[:, :], reads=[SIN], key=("rts", 0, t % 2))
                    dma("sp", rt_s[1, :, sl], cosT[:, :], reads=[COS], key=("rts", 1, t % 2))
                else:
                    dma("sp", sinT[:, :], rt_s[0, :, sl], writes=[SIN], key=("rtl", 0, t % 2))
                    dma("sp", cosT[:, :], rt_s[1, :, sl], writes=[COS], key=("rtl", 1, t % 2))
                if half == 0:
                    rope_proj(slice(512, 640), KTA[:, sl], KA[t], t, cosT, sinT, COS, SIN)
                for c in (2 * half, 2 * half + 1):
                    rope_proj(slice(c * 128, (c + 1) * 128), QTA[:, c % 2, sl], QA[c][t], t, cosT, sinT, COS, SIN)
                for j in range(4 if half == 0 else 0):
                    blk = t * 4 + j
                    for k in range(8):
                        mm(ps[:, 4, j * 128:(j + 1) * 128], xT[:, k, blk * 128:(blk + 1) * 128], wA[:, k, 640:768],
                           k == 0, k == 7, [XT[t], WA], [PS[4]])
                if half == 0:
                    cp("act", VA[:, t * 4:(t + 1) * 4, :], ps[:, 4, :].rearrange("p (j d) -> p j d", j=4), [PS[4]], [VAb[t]])

            P.barrier()
            asteps = [(c, kb) for c in (2 * half, 2 * half + 1) for kb in range(NB)]

            def a_front(n):
                c, kb = asteps[n]
                N = 256 if kb < NB - 1 else 128
                i = n % 2
                b0 = 2 * i
                t0 = kb * 128
                tq = [QA[c][(t0) // 512]] + ([QA[c][(t0 + 128) // 512]] if N == 256 else [])
                for h in range(2):
                    r = slice(64 * h, 64 * h + 64)
                    mm(ps[:, b0 + h, 0:N], KTA[r, t0:t0 + 128], QTA[r, c % 2, t0:t0 + N], True, True,
                       [KA[kb // 4]] + tq, [PS[b0 + h]])
                act(Pt[i][:, :, 0:N], ps[:, b0:b0 + 2, 0:N], AF.Exp, [PS[b0], PS[b0 + 1]], [PT[i]], scale=0.125)
                tt("dve", Pt[i][:, :, 0:N], Pt[i][:, :, 0:N], msw[:, :, 0:N], ALU.mult, [PT[i], CB], [PT[i]])

            def a_back(n):
                c, kb = asteps[n]
                N = 256 if kb < NB - 1 else 128
                i = n % 2
                t0 = kb * 128
                ob, db = 4 + kb % 2, 6 + kb % 2
                ob1, db1 = 4 + (kb + 1) % 2, 6 + (kb + 1) % 2
                for h in range(2):
                    r = slice(64 * h, 64 * h + 64)
                    vsl = VA[:, kb, r]
                    mm(ps[r, ob, 0:128], vsl, Pt[i][:, h, 0:128], kb == 0, True, [VAb[kb // 4], PT[i]], [PS[ob]])
                    mm(ps[r, db, 0:128], o64, Pt[i][:, h, 0:128], kb == 0, True, [CB, PT[i]], [PS[db]])
                j = kb % 2
                act(dn[j][:, :], ps[:, db, 0:128], AF.Ln, [PS[db], SK], [DN[j]], bias=sk[:, c:c + 1])
                act(dn[j][:, :], dn[j][:, :], AF.Exp, [DN[j]], [DN[j]], scale=-1.0)
                tt("dve", yAT[:, c, t0:t0 + 128], ps[:, ob, 0:128], dn[j][:, :], ALU.mult,
                   [PS[ob], DN[j]], [YA[kb // 4]])
                if N == 256:
                    for h in range(2):
                        r = slice(64 * h, 64 * h + 64)
                        vsl = VA[:, kb, r]
                        mm(ps[r, ob1, 0:128], vsl, Pt[i][:, h, 128:256], True, False, [VAb[kb // 4], PT[i]], [PS[ob1]])
                        mm(ps[r, db1, 0:128], o64, Pt[i][:, h, 128:256], True, False, [CB, PT[i]], [PS[db1]])

            a_front(0)
            for n in range(len(asteps)):
                if n + 1 < len(asteps):
                    a_front(n + 1)
                a_back(n)
            P.barrier()

        o = LOC
        QTBs = [M.at(o + i * 2 * T, [128, T], BF16) for i in range(2)]; o += 4 * T
        KTBs = [M.at(o + i * 2 * T, [128, T], BF16) for i in range(2)]; o += 4 * T
        VBs = [M.at(o + i * 2 * T, [128, NB, 128], BF16) for i in range(2)]; o += 4 * T
        QBbs = [[Buf("qB%d_%d" % (i, t)) for t in range(NT)] for i in range(2)]
        KBbs = [[Buf("kB%d_%d" % (i, t)) for t in range(NT)] for i in range(2)]
        VBbs = [[Buf("vB%d_%d" % (i, t)) for t in range(NT)] for i in range(2)]
        wB = [M.at(o + i * 6144, [128, 8, 384], BF16) for i in range(2)]; o += 12288
        WB = [Buf("wB%d" % i) for i in range(2)]
        NE = 1
        Et = [M.at(o + i * 4096, [128, 2, 512], F32) for i in range(NE)]; o += 4096 * NE
        ET = [Buf("E%d" % i) for i in range(NE)]
        NL = 2
        Lt = [M.at(o + i * 2048, [128, 2, 512], BF16) for i in range(NL)]; o += 2048 * NL
        LT = [Buf("L%d" % i) for i in range(NL)]
        Ac = [M.at(o + i * 2048, [128, 2, 512], BF16) for i in range(2)]; o += 4096
        AC = [Buf("Ac%d" % i) for i in range(2)]
        At = [M.at(o + i * 2048, [128, 2, 512], BF16) for i in range(2)]; o += 4096
        AT = [Buf("At%d" % i) for i in range(2)]
        assert o <= 212992, o
        msb2 = cb[:, C_MSB:C_MSB + 256].rearrange("p (h u) -> p h u", h=2)
        ntri = cb[:, C_TRI:C_TRI + 128]
        none_ = cb[:, C_ONE:C_ONE + 128]
        PJ = 7

        def wB_load(hp):
            for k in range(0, 8, 4):
                dma("pool", wB[hp % 2][:, k:k + 4, :], wB_h[hp, :, k:k + 4, :], writes=[WB[hp % 2]])

        def proj_ops(hp):
            w, W, si = wB[hp % 2], WB[hp % 2], hp % 2
            ops_ = []
            for t in range(NT):
                sl = slice(t * 512, (t + 1) * 512)
                for which, dst, dbuf in ((0, QTBs[si], QBbs[si]), (1, KTBs[si], KBbs[si])):
                    for k in range(8):
                        ops_.append(lambda k=k, which=which, sl=sl, t=t: mm(
                            ps[:, PJ, :], w[:, k, which * 128:(which + 1) * 128], xT[:, k, sl], k == 0, k == 7,
                            [W, XT[t]], [PS[PJ]]))
                    ops_.append(lambda dst=dst, dbuf=dbuf, sl=sl, t=t: cp("dve", dst[:, sl], ps[:, PJ, :], [PS[PJ]], [dbuf[t]]))
                for j in range(4):
                    blk = t * 4 + j
                    for k in range(8):
                        ops_.append(lambda k=k, j=j, blk=blk, t=t: mm(
                            ps[:, PJ, j * 128:(j + 1) * 128], xT[:, k, blk * 128:(blk + 1) * 128], w[:, k, 256:384],
                            k == 0, k == 7, [XT[t], W], [PS[PJ]]))
                ops_.append(lambda t=t: cp("dve", VBs[si][:, t * 4:(t + 1) * 4, :],
                                          ps[:, PJ, :].rearrange("p (j d) -> p j d", j=4), [PS[PJ]], [VBbs[si][t]]))
            return ops_

        wB_load(0)
        for f_ in proj_ops(0):
            f_()
        for hp in range(4):
            QTB, KTB, VB = QTBs[hp % 2], KTBs[hp % 2], VBs[hp % 2]
            QBb, KBb, VBb = QBbs[hp % 2], KBbs[hp % 2], VBbs[hp % 2]
            nxt = []
            if hp < 3:
                wB_load(hp + 1)
                nxt = proj_ops(hp + 1)

            steps = []
            for qt in range(NT):
                acur = None
                kbs = list(range(4 * qt + 3, -1, -1))
                for si, kb in enumerate(kbs):
                    off = max(0, kb - 4 * qt) * 128
                    st_ = dict(qt=qt, kb=kb, off=off, N=512 - off, t0=qt * 512 + off, diag=kb >= 4 * qt,
                               first=si == 0, last=kb == 0, ob=6, acur=acur)
                    if kb != 0:
                        st_["anew"] = 0 if si == 0 else 1 - acur
                        acur = st_["anew"]
                    steps.append(st_)
            for n_, st_ in enumerate(steps):
                st_["zp"] = 2 * (n_ % 3)
                st_["e"] = n_ % NE
                st_["l"] = n_ % NL
                st_["a"] = n_ % 2

            def s_z(S):
                off, zp, kb, t0, N = S["off"], S["zp"], S["kb"], S["t0"], S["N"]
                for h in range(2):
                    r = slice(64 * h, 64 * h + 64)
                    mm(ps[:, zp + h, off:512], KTB[r, kb * 128:(kb + 1) * 128], QTB[r, t0:t0 + N], True, False,
                       [KBb[kb // 4], QBb[S["qt"]]], [PS[zp + h]])

            def s_el(S):
                off, zp, e_i, l_i = S["off"], S["zp"], S["e"], S["l"]
                Zb = [PS[zp], PS[zp + 1]]
                act(Et[e_i][:, :, off:512], ps[:, zp:zp + 2, off:512], AF.Exp, Zb, [ET[e_i]], scale=0.125)
                act(Lt[l_i][:, :, off:512], Et[e_i][:, :, off:512], AF.Ln, [ET[e_i]], [LT[l_i]], bias=1.0)
                if S["diag"]:
                    tt("dve", Lt[l_i][:, :, off:off + 128], Lt[l_i][:, :, off:off + 128], msb2, ALU.mult,
                       [LT[l_i], CB], [LT[l_i]])
                if not S["last"]:
                    anew, acur = S["anew"], S["acur"]
                    if S["first"]:
                        ms(Ac[anew][:, :, :], 0.0, [AC[anew]])
                        cp("dve", Ac[anew][:, :, off:512], Lt[l_i][:, :, off:512], [LT[l_i]], [AC[anew]])
                    else:
                        if off > 0:
                            ms(Ac[anew][:, :, :], 0.0, [AC[anew]])
                        tt("dve", Ac[anew][:, :, off:512], Ac[acur][:, :, off:512], Lt[l_i][:, :, off:512], ALU.add,
                           [AC[acur], LT[l_i]], [AC[anew]])

            def s_tri(S):
                off, zp, l_i = S["off"], S["zp"], S["l"]
                for h in range(2):
                    mm(ps[:, zp + h, off:512], ntri, Lt[l_i][:, h, off:512], False, S["first"],
                       [CB, LT[l_i]], [PS[zp + h]])
                    if not S["first"]:
                        mm(ps[:, zp + h, off:512], none_, Ac[S["acur"]][:, h, off:512], False, True,
                           [CB, AC[S["acur"]]], [PS[zp + h]])

            def s_a(S):
                off, zp, a_i = S["off"], S["zp"], S["a"]
                Zb = [PS[zp], PS[zp + 1]]
                act(At[a_i][:, :, off:512], ps[:, zp:zp + 2, off:512], AF.Exp, Zb, [AT[a_i]], scale=0.125)
                if S["diag"]:
                    tt("dve", At[a_i][:, :, off:off + 128], At[a_i][:, :, off:off + 128], msb2, ALU.mult,
                       [AT[a_i], CB], [AT[a_i]])

            def s_av(S):
                off, a_i, ob, kb = S["off"], S["a"], S["ob"], S["kb"]
                for h in range(2):
                    r = slice(64 * h, 64 * h + 64)
                    mm(ps[r, ob, off:512], VB[:, kb, r], At[a_i][:, h, off:512], S["first"], S["last"],
                       [VBb[kb // 4], AT[a_i]], [PS[ob]])
                if S["last"]:
                    qt = S["qt"]
                    cp("dve", yBT[:, hp, qt * 512:(qt + 1) * 512], ps[:, ob, :], [PS[ob]], [YB[qt]])

            ns = len(steps)
            for tau in range(ns + 2):
                rounds_left = max(1, ns - 4 - tau)
                take = -(-len(nxt) // rounds_left) if tau < ns - 4 else len(nxt)
                for f_ in nxt[:take]:
                    f_()
                nxt = nxt[take:]
                if tau < ns:
                    s_z(steps[tau])
                if 1 <= tau <= ns:
                    s_tri(steps[tau - 1])
                if 2 <= tau:
                    s_av(steps[tau - 2])
                if tau < ns:
                    s_el(steps[tau])
                if 1 <= tau <= ns:
                    s_a(steps[tau - 1])
        P.barrier()

        if debug:
            o = LOC
            dtmp = M.at(o, [128, 4, 512], F32)
            DT = Buf("dtmp")
            for t in range(NT):
                sl = slice(t * 512, (t + 1) * 512)
                cp("dve", dtmp[:, :, :], yAT[:, :, sl], [YA[t]], [DT])
                dma("sp", dbgA[:, :, sl], dtmp[:, :, :], reads=[DT], key=("dbg", 0))
                cp("dve", dtmp[:, :, :], yBT[:, :, sl], [YB[t]], [DT])
                dma("sp", dbgB[:, :, sl], dtmp[:, :, :], reads=[DT], key=("dbg", 0))
            P.barrier()

        o = LOC
        wa = M.at(o, [128, 4, 1024], BF16); o += 8192
        wb = M.at(o, [128, 4, 1024], BF16); o += 8192
        wo = M.at(o, [128, 8, 1024], BF16); o += 16384
        WAa, WBb, WO = Buf("wa"), Buf("wb"), Buf("wo")
        g1 = M.at(o, [128, D], F32); o += 4096
        b1 = M.at(o, [128, D], F32); o += 4096
        LN1 = Buf("ln1")
        bg = M.at(o, [128, 16], F32); o += 64
        BG = Buf("bg")
        epsb = M.at(o, [128, 1], F32); o += 32
        mhalf = M.at(o, [128, 1], F32); o += 32
        CC = Buf("cc")
        wgc = [M.at(o + i * 4096, [128, 8, 256], BF16) for i in range(2)]; o += 8192
        WG = [Buf("wg%d" % i) for i in range(2)]
        gt = [M.at(o, [128, 2, 512], F32)] * 2; o += 4096
        GT = [Buf("gt")] * 2
        hT = M.at(o, [128, 8, 512], BF16); o += 8192
        HT = [Buf("hT%d" % c) for c in range(8)]
        xin = [M.at(o + i * 4096, [128, D], F32) for i in range(2)]; o += 8192
        XIN = [Buf("xin%d" % i) for i in range(2)]
        x1b = [M.at(o + i * 2048, [128, D], BF16) for i in range(4)]; o += 8192
        X1B = [Buf("x1b%d" % i) for i in range(4)]
        st = [M.at(o + i * 64, [128, 2, 6], F32) for i in range(4)]; o += 256
        mv = [M.at(o + i * 32, [128, 2], F32) for i in range(4)]; o += 128
        rs = [M.at(o + i * 32, [128, 1], F32) for i in range(4)]; o += 128
        nbt = [M.at(o + i * 32, [128, 1], F32) for i in range(4)]; o += 128
        NBT = [Buf("nb%d" % i) for i in range(4)]
        STt = [Buf("st%d" % i) for i in range(4)]
        MV = [Buf("mv%d" % i) for i in range(4)]
        RS = [Buf("rs%d" % i) for i in range(4)]
        assert o <= 212992, o

        for k in range(0, 4, 2):
            dma("pool", wa[:, k:k + 2, :], wa_h[:, k:k + 2, :], writes=[WAa])
            dma("pool", wb[:, k:k + 2, :], wb_h[:, k:k + 2, :], writes=[WBb])
        for k in range(0, 8, 2):
            dma("pool", wo[:, k:k + 2, :], wo_h[:, k:k + 2, :], writes=[WO])
        dma("sp", g1[:, :], ln_h[0], writes=[LN1])
        dma("sp", b1[:, :], ln_h[1], writes=[LN1])
        dma("sp", bg[:, :], bg_h[:, :], writes=[BG])
        ms(epsb[:, :], EPS, [CC])
        ms(mhalf[:, :], -0.5, [CC])

        def layer_norm(yv, Y, gam, bet, LNB, out_ap, OUT, st_, mv_, rs_, ST_, MV_, RS_, eps_, mh_, CC_, nb_, NB_):
            for hh in range(2):
                P.op("dve", lambda e, o_=st_[:, hh, :], i_=yv[:, hh * 512:(hh + 1) * 512]: e.bn_stats(out=o_, in_=i_),
                     reads=[Y], writes=[ST_], multi=True)
            P.op("dve", lambda e, o_=mv_[:, :], i_=st_[:, :, :]: e.bn_aggr(out=o_, in_=i_), reads=[ST_], writes=[MV_])
            ts("pool", rs_[:, :], mv_[:, 1:2], eps_[:, 0:1], None, ALU.add, ALU.bypass, [MV_, CC_], [RS_])
            tt("pool", rs_[:, :], rs_[:, :], mh_[:, :], ALU.pow, [RS_, CC_], [RS_])
            stt(nb_[:, :], mv_[:, 0:1], -1.0, rs_[:, :], ALU.mult, ALU.mult, [MV_, RS_], [NB_])
            act(yv, yv, AF.Identity, [Y, RS_, NB_], [Y], bias=nb_[:, 0:1], scale=rs_[:, 0:1])
            tt("dve", yv, yv, gam, ALU.mult, [Y, LNB], [Y])
            tt("dve", out_ap, yv, bet, ALU.add, [Y, LNB], [OUT])

        def c1_R(t, j):
            rb = 4 + 2 * (j % 2)
            for hf in range(2):
                for c in range(8):
                    mm(ps[:, rb + hf, :], hT[:, c, j * 128:(j + 1) * 128], wo[:, c, hf * 512:(hf + 1) * 512],
                       c == 0, c == 7, [HT[c], WO], [PS[rb + hf]])

        def c1_R(t, j):
            rb = 4 + 2 * (j % 2)
            for hf in range(2):
                for c in range(8):
                    mm(ps[:, rb + hf, :], hT[:, c, j * 128:(j + 1) * 128], wo[:, c, hf * 512:(hf + 1) * 512],
                       c == 0, c == 7, [HT[c], WO], [PS[rb + hf]])

        def c1_tail(t):
            def xload(j):
                blk = t * 4 + j
                dma("sp", xin[j % 2][:, :], x_h[blk * 128:(blk + 1) * 128, :], writes=[XIN[j % 2]], key=("xin", j % 2))

            def evac(j):
                rb = 4 + 2 * (j % 2)
                stt(xin[j % 2][:, :], xin[j % 2][:, :], ALPHA, ps[:, rb:rb + 2, :].rearrange("p a n -> p (a n)"),
                    ALU.mult, ALU.add, [XIN[j % 2], PS[rb], PS[rb + 1]], [XIN[j % 2]])

            xload(0)
            xload(1)
            c1_R(t, 0)
            c1_R(t, 1)
            evac(0)
            evac(1)
            c1_R(t, 2)
            c1_R(t, 3)
            L = []

            def ln_stages(js):
                its = [dict(yv=xin[j % 2][:, :], Y=XIN[j % 2], st=st[j], mv=mv[j], rs=rs[j], nb=nbt[j],
                            ST=STt[j], MV=MV[j], RS=RS[j], NB=NBT[j], j=j, blk=t * 4 + j) for j in js]
                for it_ in its:
                    for hh in range(2):
                        L.append(lambda it_=it_, hh=hh: P.op(
                            "dve", lambda e, o_=it_["st"][:, hh, :], i_=it_["yv"][:, hh * 512:(hh + 1) * 512]:
                            e.bn_stats(out=o_, in_=i_), reads=[it_["Y"]], writes=[it_["ST"]], multi=True))
                for it_ in its:
                    L.append(lambda it_=it_: P.op("dve", lambda e, o_=it_["mv"][:, :], i_=it_["st"][:, :, :]:
                                                  e.bn_aggr(out=o_, in_=i_), reads=[it_["ST"]], writes=[it_["MV"]]))
                for it_ in its:
                    L.append(lambda it_=it_: ts("pool", it_["rs"][:, :], it_["mv"][:, 1:2], epsb[:, 0:1], None,
                                                ALU.add, ALU.bypass, [it_["MV"], CC], [it_["RS"]]))
                for it_ in its:
                    L.append(lambda it_=it_: tt("pool", it_["rs"][:, :], it_["rs"][:, :], mhalf[:, :], ALU.pow,
                                                [it_["RS"], CC], [it_["RS"]]))
                for it_ in its:
                    L.append(lambda it_=it_: stt(it_["nb"][:, :], it_["mv"][:, 0:1], -1.0, it_["rs"][:, :],
                                                 ALU.mult, ALU.mult, [it_["MV"], it_["RS"]], [it_["NB"]]))
                for it_ in its:
                    L.append(lambda it_=it_: act(it_["yv"], it_["yv"], AF.Identity, [it_["Y"], it_["RS"], it_["NB"]],
                                                 [it_["Y"]], bias=it_["nb"][:, 0:1], scale=it_["rs"][:, 0:1]))
                for it_ in its:
                    L.append(lambda it_=it_: tt("dve", it_["yv"], it_["yv"], g1[:, :], ALU.mult, [it_["Y"], LN1], [it_["Y"]]))
                for it_ in its:
                    L.append(lambda it_=it_: tt("dve", it_["yv"], it_["yv"], b1[:, :], ALU.add, [it_["Y"], LN1], [it_["Y"]]))
                for it_ in its:
                    j, blk = it_["j"], it_["blk"]
                    L.append(lambda j=j, blk=blk: dma("sp", x1_s[blk * 128:(blk + 1) * 128, :], xin[j % 2][:, :],
                                                       reads=[XIN[j % 2]], key=("x1s", j % 2)))
                    L.append(lambda j=j: cp("act", x1b[j][:, :], xin[j % 2][:, :], [XIN[j % 2]], [X1B[j]]))

            def t_stages(j):
                blk = t * 4 + j
                tb = 4 + j
                tpb = ps[:, tb, :].bitcast(BF16)
                for c in range(8):
                    L.append(lambda c=c: P.op(
                        "pe", lambda e, o_=tpb[:, c * 128:(c + 1) * 128], i_=x1b[j][:, c * 128:(c + 1) * 128],
                        id_=cb[:, C_ID:C_ID + 128]: e.transpose(out=o_, in_=i_, identity=id_),
                        reads=[X1B[j], CB], writes=[PS[tb]]))
                L.append(lambda: cp("act", xT[:, :, blk * 128:(blk + 1) * 128], tpb.rearrange("p (c n) -> p c n", c=8),
                                    [PS[tb]], [XT[t]]))

            ln_stages((0, 1))
            L.append(lambda: xload(2))
            L.append(lambda: xload(3))
            L.append(lambda: evac(2))
            L.append(lambda: evac(3))
            t_stages(0)
            t_stages(1)
            ln_stages((2, 3))
            t_stages(2)
            t_stages(3)
            return L

        gi = 0
        bi = 0
        pend1 = []

        def wg_load(n):
            if n < NT * 8:
                dma("pool", wgc[n % 2][:, :, :], wg_h[n % 8, :, :, :], writes=[WG[n % 2]], key=("wg", n % 2))
        wg_load(0)
        for t in range(NT):
            sl = slice(t * 512, (t + 1) * 512)
            for c in range(8):
                wi = gi % 2
                gi2 = gi % 2
                gi += 1
                gb = 0
                for ab in range(2):
                    for k in range(8):
                        mm(ps[:, gb + ab, :], wgc[wi][:, k, ab * 128:(ab + 1) * 128], xT[:, k, sl], k == 0, k == 7,
                           [WG[wi], XT[t]], [PS[gb + ab]])
                    act(gt[gi2][:, ab, :], ps[:, gb + ab, :], AF.Sigmoid, [PS[gb + ab], BG], [GT[gi2]],
                        bias=bg[:, ab * 8 + c:ab * 8 + c + 1])
                wg_load(gi)
                for k in range(4):
                    mm(ps[:, 2, :], wa[:, k, c * 128:(c + 1) * 128], yAT[:, k, sl], k == 0, k == 3, [WAa, YA[t]], [PS[2]])
                for k in range(4):
                    mm(ps[:, 3, :], wb[:, k, c * 128:(c + 1) * 128], yBT[:, k, sl], k == 0, k == 3, [WBb, YB[t]], [PS[3]])
                tt("dve", gt[gi2][:, 0, :], gt[gi2][:, 0, :], ps[:, 2, :], ALU.mult, [GT[gi2], PS[2]], [GT[gi2]])
                tt("dve", gt[gi2][:, 1, :], gt[gi2][:, 1, :], ps[:, 3, :], ALU.mult, [GT[gi2], PS[3]], [GT[gi2]])
                take = -(-len(pend1) // (8 - c))
                for f_ in pend1[:take]:
                    f_()
                del pend1[:take]
                tt("dve", hT[:, c, :], gt[gi2][:, 0, :], gt[gi2][:, 1, :], ALU.add, [GT[gi2]], [HT[c]])
            assert not pend1
            pend1.extend(c1_tail(t))
        for f_ in pend1:
            f_()
        del pend1[:]
        P.barrier()

        o = LOC2
        TQ = min(1024, T)
        NQt = T // TQ
        HPQ = TQ // 512
        wdn = M.at(o, [128, NJ, 1024], BF16); o += NJ * 2048
        WDN = Buf("wdn")
        hid = M.at(o, [128, NJ, TQ], BF16); o += NJ * TQ * 2
        HID = [[Buf("hid%d_%d" % (j, hh)) for hh in range(HPQ)] for j in range(NJ)]
        g2 = M.at(o, [128, D], F32); o += 4096
        b2 = M.at(o, [128, D], F32); o += 4096
        LN2 = Buf("ln2")
        cpm = M.at(o, [128, 2 * NJ, 4], F32); o += 2 * NJ * 16
        CPM = Buf("cpm")
        halo = M.at(o, [128, 2 * NJ, 2], F32); o += 2 * NJ * 8
        HALO = [Buf("halo%d" % jj) for jj in range(2 * NJ)]
        o = (o + 31) // 32 * 32
        wupc = [M.at(o + i * 4096, [128, 8, 256], BF16) for i in range(3)]; o += 12288
        WUP = [Buf("wup%d" % i) for i in range(3)]
        U = [[M.at(o + (i * 2 + g) * 2080, [128, 514], F32) for g in range(2)] for i in range(2)]; o += 4 * 2080
        UB = [[Buf("U%d_%d" % (i, g)) for g in range(2)] for i in range(2)]
        Aa = [[M.at(o + (i * 2 + g) * 2048, [128, 512], F32) for g in range(2)] for i in range(3)]; o += 6 * 2048
        AB = [[Buf("A%d_%d" % (i, g)) for g in range(2)] for i in range(3)]
        xin = [M.at(o + i * 4096, [128, D], F32) for i in range(2)]; o += 8192
        XIN = [Buf("xin2_%d" % i) for i in range(2)]
        st = [M.at(o + i * 64, [128, 2, 6], F32) for i in range(2)]; o += 128
        mv = [M.at(o + i * 32, [128, 2], F32) for i in range(2)]; o += 64
        rs = [M.at(o + i * 32, [128, 1], F32) for i in range(2)]; o += 64
        nbt = [M.at(o + i * 32, [128, 1], F32) for i in range(2)]; o += 64
        NBT = [Buf("nb2%d" % i) for i in range(2)]
        epsb = M.at(o, [128, 1], F32); o += 32
        mhalf = M.at(o, [128, 1], F32); o += 32
        STt = [Buf("st2%d" % i) for i in range(2)]
        MV = [Buf("mv2%d" % i) for i in range(2)]
        RS = [Buf("rs2%d" % i) for i in range(2)]
        CC = Buf("cc2")
        assert o <= 212992, o

        for j in range(0, NJ, 2):
            dma("pool", wdn[:, j:j + 2, :], wdn_h[:, j:j + 2, :], writes=[WDN])
        dma("sp", g2[:, :], ln_h[2], writes=[LN2])
        dma("sp", b2[:, :], ln_h[3], writes=[LN2])
        dma("sp", cpm[:, :, :], cp_h[:, :, :], writes=[CPM])
        ms(epsb[:, :], EPS, [CC])
        ms(mhalf[:, :], -0.5, [CC])
        ms(halo[:, :, :], 0.0, HALO)

        ui = 0
        wi_ = 0
        bi = 0
        wseq = [(q, j) for q in range(NQt) for j in range(NJ)]

        def wup_load(n):
            if n < len(wseq):
                dma("pool", wupc[n % 3][:, :, :], wup_h[wseq[n][1], :, :, :], writes=[WUP[n % 3]], key=("wup", n % 3))
        wup_load(0)
        wup_load(1)
        def ln_front(it_):
            for hh in range(2):
                P.op("dve", lambda e, o_=it_["st"][:, hh, :], i_=it_["yv"][:, hh * 512:(hh + 1) * 512]:
                     e.bn_stats(out=o_, in_=i_), reads=[it_["Y"]], writes=[it_["ST"]], multi=True)
            P.op("dve", lambda e, o_=it_["mv"][:, :], i_=it_["st"][:, :, :]: e.bn_aggr(out=o_, in_=i_),
                 reads=[it_["ST"]], writes=[it_["MV"]])
            ts("pool", it_["rs"][:, :], it_["mv"][:, 1:2], epsb[:, 0:1], None, ALU.add, ALU.bypass,
               [it_["MV"], CC], [it_["RS"]])
            tt("pool", it_["rs"][:, :], it_["rs"][:, :], mhalf[:, :], ALU.pow, [it_["RS"], CC], [it_["RS"]])
            stt(it_["nb"][:, :], it_["mv"][:, 0:1], -1.0, it_["rs"][:, :], ALU.mult, ALU.mult,
                [it_["MV"], it_["RS"]], [it_["NB"]])
            act(it_["yv"], it_["yv"], AF.Identity, [it_["Y"], it_["RS"], it_["NB"]], [it_["Y"]],
                bias=it_["nb"][:, 0:1], scale=it_["rs"][:, 0:1])

        def ln_back(it_, gam, bet, LNB):
            tt("dve", it_["yv"], it_["yv"], gam, ALU.mult, [it_["Y"], LNB], [it_["Y"]])
            tt("dve", it_["yv"], it_["yv"], bet, ALU.add, [it_["Y"], LNB], [it_["Y"]])

        pend = [None]

        def flush_pend():
            if pend[0] is not None:
                o_, a_, b_, rd, wr = pend[0]
                act(a_, a_, AF.Silu, [rd[0]], [rd[0]])
                tt("dve", o_, a_, b_, ALU.mult, rd, wr)
                pend[0] = None

        def down_front(q, jb):
            blk = q * (TQ // 128) + jb
            i = blk % 2
            rb = 4 + 2 * i
            dma("sp", xin[i][:, :], x1_s[blk * 128:(blk + 1) * 128, :], writes=[XIN[i]], key=("xin", i))
            for hf in range(2):
                for j in range(NJ):
                    mm(ps[:, rb + hf, :], hid[:, j, jb * 128:(jb + 1) * 128], wdn[:, j, hf * 512:(hf + 1) * 512],
                       j == 0, j == NJ - 1, [HID[j][jb // 4], WDN], [PS[rb + hf]])
            stt(xin[i][:, :], xin[i][:, :], ALPHA, ps[:, rb:rb + 2, :].rearrange("p a n -> p (a n)"),
                ALU.mult, ALU.add, [XIN[i], PS[rb], PS[rb + 1]], [XIN[i]])
            it_ = dict(yv=xin[i][:, :], Y=XIN[i], st=st[i], mv=mv[i], rs=rs[i], nb=nbt[i],
                       ST=STt[i], MV=MV[i], RS=RS[i], NB=NBT[i], blk=blk, i=i)
            ln_front(it_)
            return it_

        def down_back(it_):
            ln_back(it_, g2[:, :], b2[:, :], LN2)
            blk, i = it_["blk"], it_["i"]
            dma("sp", out_h[blk * 128:(blk + 1) * 128, :], xin[i][:, :], reads=[XIN[i]], key=("out", i))

        for q in range(NQt):
            for j in range(NJ):
                wi = wi_ % 3
                wup_load(wi_ + 2)
                wi_ += 1
                for hh in range(HPQ):
                    tI = q * HPQ + hh
                    sl = slice(tI * 512, (tI + 1) * 512)
                    i = ui % 2
                    ia = ui % 3
                    ui += 1
                    for g in range(2):
                        jj = g * NJ + j
                        b = 2 * i + g
                        for k in range(8):
                            mm(ps[:, b, :], wupc[wi][:, k, g * 128:(g + 1) * 128], xT[:, k, sl], k == 0, k == 7,
                               [WUP[wi], XT[tI]], [PS[b]])
                        cp("pool", U[i][g][:, 0:2], halo[:, jj, :], [HALO[jj]], [UB[i][g]])
                        P.op("act", lambda e, o_=U[i][g][:, 2:514], i_=ps[:, b, :]: e.activation(out=o_, in_=i_, func=AF.Copy),
                             reads=[PS[b]], writes=[UB[i][g]], multi=True)
                        act(Aa[ia][g][:, :], ps[:, b, :], AF.Identity, [PS[b], CPM], [AB[ia][g]],
                            bias=cpm[:, jj, 3:4], scale=cpm[:, jj, 2:3])
                        cp("pool", halo[:, jj, :], U[i][g][:, 512:514], [UB[i][g]], [HALO[jj]])
                        stt(Aa[ia][g][:, :], U[i][g][:, 1:513], cpm[:, jj, 1:2], Aa[ia][g][:, :], ALU.mult, ALU.add,
                            [UB[i][g], CPM, AB[ia][g]], [AB[ia][g]])
                        stt(Aa[ia][g][:, :], U[i][g][:, 0:512], cpm[:, jj, 0:1], Aa[ia][g][:, :], ALU.mult, ALU.add,
                            [UB[i][g], CPM, AB[ia][g]], [AB[ia][g]])
                    flush_pend()
                    pend[0] = (hid[:, j, hh * 512:(hh + 1) * 512], Aa[ia][0][:, :], Aa[ia][1][:, :],
                               [AB[ia][0], AB[ia][1]], [HID[j][hh]])
            flush_pend()
            prev = None
            for jb in range(TQ // 128):
                cur = down_front(q, jb)
                if prev is not None:
                    down_back(prev)
                prev = cur
            down_back(prev)
        P.barrier()
        P.emit(nc, stack)
    return nc


def host_prep(b, x, positions, w_in, b_gate, sinks, w_branch_a, w_branch_b, w_out,
              ln1_g, ln1_b, w_up, conv_w, conv_b, w_down, ln2_g, ln2_b, shared):
    m = dict(shared)
    m["xT"] = np.ascontiguousarray(x[b].T)
    m["x"] = np.ascontiguousarray(x[b])
    m["pos"] = np.ascontiguousarray(np.broadcast_to(positions[b][None, :].astype(np.int32), (128, positions.shape[1])))
    return m


def host_shared(w_in, b_gate, sinks, w_branch_a, w_branch_b, w_out,
                ln1_g, ln1_b, w_up, conv_w, conv_b, w_down, ln2_g, ln2_b):
    f = np.float32
    w_in = w_in[0]
    cbm = np.zeros((128, NCB), f)
    cbm[:, C_ID:C_ID + 128] = np.eye(128, dtype=f)
    rot = np.zeros((128, 128), f)
    for m_ in range(128):
        base, ml = (m_ // 64) * 64, m_ % 64
        if ml < 32:
            rot[base + ml + 32, m_] = -1.0
        else:
            rot[base + ml - 32, m_] = 1.0
    cbm[:, C_ROT:C_ROT + 128] = rot
    jj, ss = np.meshgrid(np.arange(128), np.arange(128), indexing="ij")
    cbm[:, C_TRI:C_TRI + 128] = np.where(jj >= ss, -8.0, 0.0)
    cbm[:, C_ONE:C_ONE + 128] = -8.0
    s_, t_ = np.meshgrid(np.arange(128), np.arange(128), indexing="ij")
    msb = (s_ < t_).astype(f)
    cbm[:, C_MSB:C_MSB + 128] = msb
    cbm[:, C_MSB + 128:C_MSB + 256] = msb
    mswa = np.concatenate([(s_ <= t_).astype(f), (s_ > t_).astype(f)], axis=1)
    cbm[:, C_MSWA:C_MSWA + 256] = mswa
    cbm[:, C_MSWA + 256:C_MSWA + 512] = mswa
    cbm[:, C_O64:C_O64 + 64] = 1.0
    cfm = np.zeros((128, 8), f)
    inv = (f(1.0) / np.power(f(10000.0), np.arange(0, 64, 2, dtype=f) / f(64.0))).astype(f)
    cfm[:, 0] = inv[np.arange(128) % 32]
    def kp(w):
        K, N = w.shape
        return np.ascontiguousarray(w.reshape(K // 128, 128, N).transpose(1, 0, 2))
    qa_cols = np.concatenate([np.r_[c * 64:(c + 1) * 64, (4 + c) * 64:(5 + c) * 64] for c in range(4)])
    wA = np.concatenate([w_in[:, 0:512][:, qa_cols], w_in[:, 512:640], w_in[:, 640:768]], axis=1)
    wB = np.stack([np.concatenate([w_in[:, 768 + hp * 128:768 + (hp + 1) * 128],
                                   w_in[:, 1280 + hp * 128:1280 + (hp + 1) * 128],
                                   w_in[:, 1792 + hp * 128:1792 + (hp + 1) * 128]], axis=1) for hp in range(4)])
    wgA, wgB = w_in[:, 2304:3328], w_in[:, 3328:4352]
    wg = np.stack([np.concatenate([wgA[:, c * 128:(c + 1) * 128], wgB[:, c * 128:(c + 1) * 128]], axis=1)
                   for c in range(8)])
    frow = np.array([[(c if p < 64 else 4 + c) * 64 + p % 64 for c in range(4)] for p in range(128)])
    wa = np.ascontiguousarray(w_branch_a[0][frow, :])
    wup = w_up[0]
    wupr = np.stack([np.concatenate([wup[:, j * 128:(j + 1) * 128], wup[:, DFF + j * 128:DFF + (j + 1) * 128]], axis=1)
                     for j in range(NJ)])
    cpm = np.zeros((128, 2 * NJ, 4), f)
    cw, cbias = conv_w[0], conv_b[0]
    for jj_ in range(2 * NJ):
        ch = jj_ * 128 + np.arange(128)
        cpm[:, jj_, 0:3] = cw[:, ch].T
        cpm[:, jj_, 3] = cbias[ch]
    bgm = np.zeros((128, 16), f)
    for c in range(8):
        bgm[:, c] = b_gate[0][c * 128:(c + 1) * 128]
        bgm[:, 8 + c] = b_gate[0][1024 + c * 128:1024 + (c + 1) * 128]
    skm = np.zeros((128, 4), f)
    for c in range(4):
        skm[0:64, c] = sinks[0][c]
        skm[64:128, c] = sinks[0][4 + c]
    lnm = np.stack([np.broadcast_to(v[0][None, :], (128, D)) for v in (ln1_g, ln1_b, ln2_g, ln2_b)]).astype(f)
    return dict(
        cb=cbm, cf=cfm,
        wA=kp(wA), wB=np.stack([kp(wB[hp]) for hp in range(4)]),
        wg=np.stack([kp(wg[c]) for c in range(8)]),
        wa=wa, wb=kp(w_branch_b[0]), wo=kp(w_out[0]),
        wup=np.stack([kp(wupr[j]) for j in range(NJ)]), wdn=kp(w_down[0]),
        cp=cpm, bg=bgm, sk=skm, ln=np.ascontiguousarray(lnm),
    )


_NC_CACHE = {}


def kernel(x, positions, w_in, b_gate, sinks, w_branch_a, w_branch_b, w_out,
           ln1_g, ln1_b, w_up, conv_w, conv_b, w_down, ln2_g, ln2_b):
    args = [np.asarray(a) for a in (x, positions, w_in, b_gate, sinks, w_branch_a, w_branch_b, w_out,
                                    ln1_g, ln1_b, w_up, conv_w, conv_b, w_down, ln2_g, ln2_b)]
    x, positions = args[0], args[1]
    B, T, _ = x.shape
    shared = host_shared(*args[2:])
    in_maps = [host_prep(b, x, positions, *args[2:], shared) for b in range(B)]
    nc = build(T)
    res = run_bass_kernel_spmd(nc, in_maps, core_ids=list(range(B)))
    return np.stack([np.asarray(r["out"]) for r in res.results]).astype(np.float32)
```

```python
import numpy as np
import concourse.bass as bass
import concourse.mybir as mybir
from concourse.bass_utils import run_bass_kernel_spmd

F32 = mybir.dt.float32
BF16 = mybir.dt.bfloat16
I32 = mybir.dt.int32
AF = mybir.ActivationFunctionType
ALU = mybir.AluOpType

D = 1024
DFF = 2816
NJ = DFF // 128
HD = 64
ALPHA = float(2.0 ** 0.25)
EPS = 1e-5
PI = float(np.pi)
INV2PI = float(np.float32(1.0 / (2.0 * np.pi)))
MAGIC = 12582912.0
CW1 = 6.28125
CW2 = float(2.0 * np.pi - 6.28125)
SBUF_BASE = 16384

C_ID, C_ROT, C_TRI, C_ONE, C_MSB, C_MSWA, C_O64, NCB = 0, 128, 256, 384, 512, 768, 1280, 1344


class Buf:
    __slots__ = ("name", "w", "r", "psum")

    def __init__(self, name, psum=False):
        self.name = name
        self.w = []
        self.r = []
        self.psum = psum


class Prog:
    ENGS = ("pe", "act", "dve", "pool", "sp")

    def __init__(self):
        self.ops = []
        self.bar_start = 0

    def op(self, eng, fn, reads=(), writes=(), dma=None, multi=False):
        i = len(self.ops)
        deps = set()
        for b in reads:
            deps.update(b.w)
            if b.psum:
                deps.update(r for r in b.r if self.ops[r]["eng"] != eng)
        for b in writes:
            if not multi:
                deps.update(b.w)
            deps.update(b.r)
        for b in reads:
            b.r.append(i)
        for b in writes:
            if multi and not b.r:
                b.w = b.w + [i]
            else:
                b.w = [i]
            b.r = []
        deps.discard(i)
        last = {}
        keep = set()
        for d in deps:
            od = self.ops[d]
            if od["dma"] is not None:
                keep.add(d)
            elif last.get(od["eng"], -1) < d:
                last[od["eng"]] = d
        keep.update(last.values())
        self.ops.append(dict(eng=eng, fn=fn, deps=keep, dma=dma))
        return i

    def barrier(self):
        last = {}
        deps = set()
        for idx in range(self.bar_start, len(self.ops)):
            o = self.ops[idx]
            if o["fn"] is None:
                continue
            if o["dma"] is not None:
                deps.add(idx)
            else:
                last[o["eng"]] = idx
        deps.update(last.values())
        for e in self.ENGS:
            self.ops.append(dict(eng=e, fn=None, deps=set(deps), dma=None))
        self.bar_start = len(self.ops)

    def emit(self, nc, stack):
        ops = self.ops
        need = set()
        for o in ops:
            for d in o["deps"]:
                od = ops[d]
                if o["eng"] == "pe" and od["eng"] == "pe" and od["dma"] is None:
                    continue
                need.add(d)
        esem = {e: stack.enter_context(nc.semaphore("s_" + e)) for e in self.ENGS}
        dsem = {}
        tick = {e: 0 for e in self.ENGS}
        dcnt = {}
        sig = {}
        for i, o in enumerate(ops):
            if i not in need or o["fn"] is None:
                continue
            if o["dma"] is not None:
                k = o["dma"]
                if k not in dsem:
                    dsem[k] = stack.enter_context(nc.semaphore("d_%d" % len(dsem)))
                    dcnt[k] = 0
                dcnt[k] += 16
                sig[i] = (dsem[k], dcnt[k], 16)
            else:
                tick[o["eng"]] += 1
                sig[i] = (esem[o["eng"]], tick[o["eng"]], 1)
        per = {e: [] for e in self.ENGS}
        for i, o in enumerate(ops):
            per[o["eng"]].append(i)
        self.stats = dict(ticks=dict(tick), nsem=len(dsem) + len(esem), nops=len(ops),
                          dmax=max(dcnt.values()) if dcnt else 0)
        block = stack.enter_context(nc.Block())

        def run(eng, ename):
            waited = {}
            for i in per[ename]:
                o = ops[i]
                for d in sorted(o["deps"]):
                    if d not in sig:
                        continue
                    od = ops[d]
                    if ename == "pe" and od["eng"] == "pe" and od["dma"] is None:
                        continue
                    sem, val, _ = sig[d]
                    key = id(sem)
                    if waited.get(key, 0) >= val:
                        continue
                    eng.wait_ge(sem, val)
                    waited[key] = val
                if o["fn"] is not None:
                    ins = o["fn"](eng)
                    if i in sig:
                        ins.then_inc(sig[i][0], sig[i][2])

        @block.tensor
        def _(e):
            run(e, "pe")

        @block.scalar
        def _(e):
            run(e, "act")

        @block.vector
        def _(e):
            run(e, "dve")

        @block.gpsimd
        def _(e):
            run(e, "pool")

        @block.sync
        def _(e):
            run(e, "sp")


class Mem:
    def __init__(self, nc):
        self.nc = nc
        self.n = 0

    def at(self, off, shape, dt):
        self.n += 1
        return self.nc.alloc_sbuf_tensor_at("t%d" % self.n, list(shape), dt, offset=SBUF_BASE + off).ap()


def build(T=4096, debug=False):
    NB = T // 128
    NT = T // 512
    nc = bass.Bass("TRN2", target_bir_lowering=False)
    P = Prog()
    M = Mem(nc)

    def din(name, shape, dt=F32):
        return nc.dram_tensor(name, list(shape), dt, kind="ExternalInput").ap()

    xT_h = din("xT", [D, T])
    x_h = din("x", [T, D])
    pos_h = din("pos", [128, T], I32)
    cb_h = din("cb", [128, NCB])
    cf_h = din("cf", [128, 8])
    wA_h = din("wA", [128, 8, 768])
    wB_h = din("wB", [4, 128, 8, 384])
    wg_h = din("wg", [8, 128, 8, 256])
    wa_h = din("wa", [128, 4, 1024])
    wb_h = din("wb", [128, 4, 1024])
    wo_h = din("wo", [128, 8, 1024])
    wup_h = din("wup", [NJ, 128, 8, 256])
    wdn_h = din("wdn", [128, NJ, 1024])
    cp_h = din("cp", [128, 2 * NJ, 4])
    bg_h = din("bg", [128, 16])
    sk_h = din("sk", [128, 4])
    ln_h = din("ln", [4, 128, D])
    out_h = nc.dram_tensor("out", [T, D], F32, kind="ExternalOutput").ap()
    x1_s = nc.dram_tensor("x1s", [T, D], F32, kind="Internal").ap()
    rt_s = nc.dram_tensor("rts", [2, 128, T], F32, kind="Internal").ap()
    if debug:
        dbgA = nc.dram_tensor("dbgA", [128, 4, T], F32, kind="ExternalOutput").ap()
        dbgB = nc.dram_tensor("dbgB", [128, 4, T], F32, kind="ExternalOutput").ap()

    import contextlib
    stack = contextlib.ExitStack()
    with stack:
        ps = stack.enter_context(nc.psum_tensor([128, 8, 512], F32))
        PS = [Buf("ps%d" % b, psum=True) for b in range(8)]

        o = 0
        cb = M.at(o, [128, NCB], BF16); o += 2 * NCB
        CB = Buf("cb")
        cf = M.at(o, [128, 8], F32); o += 32
        CF = Buf("cf")
        xT = M.at(o, [128, 8, T], BF16); o += 16 * T
        XT = [Buf("xT%d" % t) for t in range(NT)]
        LOC2 = o
        yAT = M.at(o, [128, 4, T], BF16); o += 8 * T
        YA = [Buf("yA%d" % t) for t in range(NT)]
        yBT = M.at(o, [128, 4, T], BF16); o += 8 * T
        YB = [Buf("yB%d" % t) for t in range(NT)]
        LOC = o
        assert LOC % 32 == 0

        def dma(eng, out, in_, reads=(), writes=(), key=None, multi=True):
            if key is None:
                key = ("w", writes[0].name) if writes else ("r", reads[0].name)
            return P.op(eng, lambda e, out=out, in_=in_: e.dma_start(out=out, in_=in_),
                        reads=reads, writes=writes, dma=key, multi=multi)

        def ms(ap, val, writes):
            return P.op("pool", lambda e, ap=ap, val=val: e.memset(ap, val), writes=writes)

        def mm(out, lhsT, rhs, start, stop, reads, writes):
            return P.op("pe", lambda e, out=out, lhsT=lhsT, rhs=rhs, start=start, stop=stop:
                        e.matmul(out, lhsT=lhsT, rhs=rhs, start=start, stop=stop, skip_group_check=True),
                        reads=reads, writes=writes)

        def act(out, in_, func, reads, writes, bias=0.0, scale=1.0):
            return P.op("act", lambda e, out=out, in_=in_, func=func, bias=bias, scale=scale:
                        e.activation(out=out, in_=in_, func=func, bias=bias, scale=scale),
                        reads=reads, writes=writes)

        def tt(eng, out, in0, in1, op, reads, writes):
            return P.op(eng, lambda e, out=out, in0=in0, in1=in1, op=op:
                        e.tensor_tensor(out=out, in0=in0, in1=in1, op=op), reads=reads, writes=writes)

        def ts(eng, out, in0, s1, s2, op0, op1, reads, writes):
            return P.op(eng, lambda e, out=out, in0=in0, s1=s1, s2=s2, op0=op0, op1=op1:
                        e.tensor_scalar(out=out, in0=in0, scalar1=s1, scalar2=s2, op0=op0, op1=op1),
                        reads=reads, writes=writes)

        def stt(out, in0, scalar, in1, op0, op1, reads, writes):
            return P.op("dve", lambda e, out=out, in0=in0, scalar=scalar, in1=in1, op0=op0, op1=op1:
                        e.scalar_tensor_tensor(out=out, in0=in0, scalar=scalar, in1=in1, op0=op0, op1=op1),
                        reads=reads, writes=writes)

        def cp(eng, out, in_, reads, writes):
            if eng == "act":
                return act(out, in_, AF.Copy, reads, writes)
            return P.op(eng, lambda e, out=out, in_=in_: e.tensor_copy(out=out, in_=in_),
                        reads=reads, writes=writes)

        dma("pool", cb[:, :], cb_h[:, :], writes=[CB])
        dma("sp", cf[:, :], cf_h[:, :], writes=[CF])
        xv = xT_h.rearrange("(c p) t -> p c t", p=128)

        def load_xT(t):
            dma("pool", xT[:, :, t * 512:(t + 1) * 512], xv[:, :, t * 512:(t + 1) * 512], writes=[XT[t]])
        load_xT(0)

        o = LOC
        QTA = M.at(o, [128, 2, T], BF16); o += 4 * T
        QA = [[Buf("qa%d_%d" % (c, t)) for t in range(NT)] for c in range(4)]
        KTA = M.at(o, [128, T], BF16); o += 2 * T
        KA = [Buf("ka%d" % t) for t in range(NT)]
        VA = M.at(o, [128, NB, 128], BF16); o += 2 * T
        VAb = [Buf("va%d" % t) for t in range(NT)]
        sk = M.at(o, [128, 4], F32); o += 32
        SK = Buf("sk")
        npi = M.at(o, [128, 1], F32); o += 32
        NPI = Buf("npi")
        A2 = o
        wA = M.at(o, [128, 8, 768], BF16); o += 8 * 768 * 2
        WA = Buf("wA")
        ang = M.at(o, [128, 512], F32); o += 2048
        tmpa = M.at(o, [128, 512], F32); o += 2048
        tmpb = M.at(o, [128, 512], F32); o += 2048
        posi = tmpb.bitcast(I32)
        TMPB = Buf("tmpb")
        cosTs = [M.at(o + i * 2048, [128, 512], F32) for i in range(2)]; o += 4096
        sinTs = [M.at(o + i * 2048, [128, 512], F32) for i in range(2)]; o += 4096
        COSs = [Buf("cos%d" % i) for i in range(2)]
        SINs = [Buf("sin%d" % i) for i in range(2)]
        ANG, TMPA = Buf("ang"), Buf("tmpa")
        POSI = TMPB
        q32 = [M.at(o + i * 2048, [128, 512], F32) for i in range(2)]; o += 4096
        qb = [M.at(o + i * 1024, [128, 512], BF16) for i in range(2)]; o += 2048
        t2 = [M.at(o + i * 2048, [128, 512], F32) for i in range(2)]; o += 4096
        Q32 = [Buf("q32_%d" % i) for i in range(2)]
        QB = [Buf("qb_%d" % i) for i in range(2)]
        T2 = [Buf("t2_%d" % i) for i in range(2)]
        assert o <= 212992, o

        for k in range(0, 8, 2):
            dma("pool", wA[:, k:k + 2, :], wA_h[:, k:k + 2, :], writes=[WA])
        for t in range(1, NT):
            load_xT(t)
        dma("sp", sk[:, :], sk_h[:, :], writes=[SK])
        ms(npi[:, :], -PI, [NPI])
        act(sk[:, :], sk[:, :], AF.Exp, [SK], [SK])

        ctr = [0]
        rp_pend = [None]

        def rope_flush():
            if rp_pend[0] is not None:
                f_ = rp_pend[0]
                rp_pend[0] = None
                f_()

        def rope_proj(wcols, dst, dstbuf, t, cosT, sinT, COS, SIN):
            i = ctr[0] % 2
            ctr[0] += 1
            b0, b1 = 2 * i, 2 * i + 1
            for k in range(8):
                mm(ps[:, b0, :], wA[:, k, wcols], xT[:, k, t * 512:(t + 1) * 512], k == 0, k == 7,
                   [WA, XT[t]], [PS[b0]])
            act(q32[i][:, :], ps[:, b0, :], AF.Copy, [PS[b0]], [Q32[i]])
            cp("act", qb[i][:, :], ps[:, b0, :], [PS[b0]], [QB[i]])
            rope_flush()

            def back():
                mm(ps[:, b1, :], cb[:, C_ROT:C_ROT + 128], qb[i][:, :], True, True, [CB, QB[i]], [PS[b1]])
                tt("dve", q32[i][:, :], q32[i][:, :], cosT[:, :], ALU.mult, [Q32[i], COS], [Q32[i]])
                tt("dve", t2[i][:, :], ps[:, b1, :], sinT[:, :], ALU.mult, [PS[b1], SIN], [T2[i]])
                tt("dve", dst, q32[i][:, :], t2[i][:, :], ALU.add, [Q32[i], T2[i]], [dstbuf])
            rp_pend[0] = back

        Pt = [M.at(o + i * 1024, [128, 2, 256], BF16) for i in range(2)]; o += 2048
        PT = [Buf("pt%d" % i) for i in range(2)]
        dn = [M.at(o + i * 512, [128, 128], F32) for i in range(2)]; o += 1024
        DN = [Buf("dn%d" % i) for i in range(2)]
        assert o <= 212992, o
        msw = cb[:, C_MSWA:C_MSWA + 512].rearrange("p (h u) -> p h u", h=2)
        o64 = cb[:, C_O64:C_O64 + 64]
        for half in range(2):
            for t in range(NT):
                sl = slice(t * 512, (t + 1) * 512)
                cosT, sinT, COS, SIN = cosTs[t % 2], sinTs[t % 2], COSs[t % 2], SINs[t % 2]
                if half == 0:
                    dma("sp", posi[:, :], pos_h[:, sl], writes=[POSI])
                    cp("dve", ang[:, :], posi[:, :], [POSI], [ANG])
                    ts("dve", ang[:, :], ang[:, :], cf[:, 0:1], None, ALU.mult, ALU.bypass, [ANG, CF], [ANG])
                    def sin_of(base_ap, BASE, dst, DST):
                        ts("dve", tmpa[:, :], base_ap, INV2PI, MAGIC, ALU.mult, ALU.add, [BASE], [TMPA])
                        ts("dve", tmpa[:, :], tmpa[:, :], MAGIC, None, ALU.subtract, ALU.bypass, [TMPA], [TMPA])
                        stt(tmpb[:, :], tmpa[:, :], -CW1, base_ap, ALU.mult, ALU.add, [TMPA, BASE], [TMPB])
                        stt(tmpb[:, :], tmpa[:, :], -CW2, tmpb[:, :], ALU.mult, ALU.add, [TMPA, TMPB], [TMPB])
                        ts("dve", tmpb[:, :], tmpb[:, :], 3.141592, -3.141592, ALU.min, ALU.max, [TMPB], [TMPB])
                        act(dst, tmpb[:, :], AF.Sin, [TMPB], [DST])
                    sin_of(ang[:, :], ANG, sinT[:, :], SIN)
                    ts("dve", ang[:, :], ang[:, :], 0.5 * PI, None, ALU.add, ALU.bypass, [ANG], [ANG])
                    sin_of(ang[:, :], ANG, cosT[:, :], COS)
                    dma("sp", rt_s[0, :, sl], sinT[:, :], reads=[SIN], key=("rts", 0, t % 2))
                    dma("sp", rt_s[1, :, sl], cosT[:, :], reads=[COS], key=("rts", 1, t % 2))
                else:
                    dma("sp", sinT[:, :], rt_s[0, :, sl], writes=[SIN], key=("rtl", 0, t % 2))
                    dma("sp", cosT[:, :], rt_s[1, :, sl], writes=[COS], key=("rtl", 1, t % 2))
                if half == 0:
                    rope_proj(slice(512, 640), KTA[:, sl], KA[t], t, cosT, sinT, COS, SIN)
                for c in (2 * half, 2 * half + 1):
                    rope_proj(slice(c * 128, (c + 1) * 128), QTA[:, c % 2, sl], QA[c][t], t, cosT, sinT, COS, SIN)
                for j in range(4 if half == 0 else 0):
                    blk = t * 4 + j
                    for k in range(8):
                        mm(ps[:, 4, j * 128:(j + 1) * 128], xT[:, k, blk * 128:(blk + 1) * 128], wA[:, k, 640:768],
                           k == 0, k == 7, [XT[t], WA], [PS[4]])
                if half == 0:
                    cp("act", VA[:, t * 4:(t + 1) * 4, :], ps[:, 4, :].rearrange("p (j d) -> p j d", j=4), [PS[4]], [VAb[t]])

            rope_flush()
            P.barrier()
            asteps = [(c, kb) for c in (2 * half, 2 * half + 1) for kb in range(NB)]

            def a_front(n):
                c, kb = asteps[n]
                N = 256 if kb < NB - 1 else 128
                i = n % 2
                b0 = 2 * i
                t0 = kb * 128
                tq = [QA[c][(t0) // 512]] + ([QA[c][(t0 + 128) // 512]] if N == 256 else [])
                for h in range(2):
                    r = slice(64 * h, 64 * h + 64)
                    mm(ps[:, b0 + h, 0:N], KTA[r, t0:t0 + 128], QTA[r, c % 2, t0:t0 + N], True, True,
                       [KA[kb // 4]] + tq, [PS[b0 + h]])
                act(Pt[i][:, :, 0:N], ps[:, b0:b0 + 2, 0:N], AF.Exp, [PS[b0], PS[b0 + 1]], [PT[i]], scale=0.125)
                tt("dve", Pt[i][:, :, 0:N], Pt[i][:, :, 0:N], msw[:, :, 0:N], ALU.mult, [PT[i], CB], [PT[i]])

            def a_back(n):
                c, kb = asteps[n]
                N = 256 if kb < NB - 1 else 128
                i = n % 2
                t0 = kb * 128
                ob, db = 4 + kb % 2, 6 + kb % 2
                ob1, db1 = 4 + (kb + 1) % 2, 6 + (kb + 1) % 2
                for h in range(2):
                    r = slice(64 * h, 64 * h + 64)
                    vsl = VA[:, kb, r]
                    mm(ps[r, ob, 0:128], vsl, Pt[i][:, h, 0:128], kb == 0, True, [VAb[kb // 4], PT[i]], [PS[ob]])
                    mm(ps[r, db, 0:128], o64, Pt[i][:, h, 0:128], kb == 0, True, [CB, PT[i]], [PS[db]])
                j = kb % 2
                act(dn[j][:, :], ps[:, db, 0:128], AF.Ln, [PS[db], SK], [DN[j]], bias=sk[:, c:c + 1])
                act(dn[j][:, :], dn[j][:, :], AF.Exp, [DN[j]], [DN[j]], scale=-1.0)
                tt("dve", yAT[:, c, t0:t0 + 128], ps[:, ob, 0:128], dn[j][:, :], ALU.mult,
                   [PS[ob], DN[j]], [YA[kb // 4]])
                if N == 256:
                    for h in range(2):
                        r = slice(64 * h, 64 * h + 64)
                        vsl = VA[:, kb, r]
                        mm(ps[r, ob1, 0:128], vsl, Pt[i][:, h, 128:256], True, False, [VAb[kb // 4], PT[i]], [PS[ob1]])
                        mm(ps[r, db1, 0:128], o64, Pt[i][:, h, 128:256], True, False, [CB, PT[i]], [PS[db1]])

            a_front(0)
            for n in range(len(asteps)):
                if n + 1 < len(asteps):
                    a_front(n + 1)
                a_back(n)
            P.barrier()

        o = LOC
        QTBs = [M.at(o + i * 2 * T, [128, T], BF16) for i in range(2)]; o += 4 * T
        KTBs = [M.at(o + i * 2 * T, [128, T], BF16) for i in range(2)]; o += 4 * T
        VBs = [M.at(o + i * 2 * T, [128, NB, 128], BF16) for i in range(2)]; o += 4 * T
        QBbs = [[Buf("qB%d_%d" % (i, t)) for t in range(NT)] for i in range(2)]
        KBbs = [[Buf("kB%d_%d" % (i, t)) for t in range(NT)] for i in range(2)]
        VBbs = [[Buf("vB%d_%d" % (i, t)) for t in range(NT)] for i in range(2)]
        wB = [M.at(o + i * 6144, [128, 8, 384], BF16) for i in range(2)]; o += 12288
        WB = [Buf("wB%d" % i) for i in range(2)]
        NE = 1
        Et = [M.at(o + i * 4096, [128, 2, 512], F32) for i in range(NE)]; o += 4096 * NE
        ET = [Buf("E%d" % i) for i in range(NE)]
        NL = 2
        Lt = [M.at(o + i * 2048, [128, 2, 512], BF16) for i in range(NL)]; o += 2048 * NL
        LT = [Buf("L%d" % i) for i in range(NL)]
        Ac = [M.at(o + i * 2048, [128, 2, 512], BF16) for i in range(2)]; o += 4096
        AC = [Buf("Ac%d" % i) for i in range(2)]
        At = [M.at(o + i * 2048, [128, 2, 512], BF16) for i in range(2)]; o += 4096
        AT = [Buf("At%d" % i) for i in range(2)]
        assert o <= 212992, o
        msb2 = cb[:, C_MSB:C_MSB + 256].rearrange("p (h u) -> p h u", h=2)
        ntri = cb[:, C_TRI:C_TRI + 128]
        none_ = cb[:, C_ONE:C_ONE + 128]
        PJ = 7

        def wB_load(hp):
            for k in range(0, 8, 4):
                dma("pool", wB[hp % 2][:, k:k + 4, :], wB_h[hp, :, k:k + 4, :], writes=[WB[hp % 2]])

        def proj_ops(hp):
            w, W, si = wB[hp % 2], WB[hp % 2], hp % 2
            ops_ = []
            for t in range(NT):
                sl = slice(t * 512, (t + 1) * 512)
                for which, dst, dbuf in ((0, QTBs[si], QBbs[si]), (1, KTBs[si], KBbs[si])):
                    for k in range(8):
                        ops_.append(lambda k=k, which=which, sl=sl, t=t: mm(
                            ps[:, PJ, :], w[:, k, which * 128:(which + 1) * 128], xT[:, k, sl], k == 0, k == 7,
                            [W, XT[t]], [PS[PJ]]))
                    ops_.append(lambda dst=dst, dbuf=dbuf, sl=sl, t=t: cp("dve", dst[:, sl], ps[:, PJ, :], [PS[PJ]], [dbuf[t]]))
                for j in range(4):
                    blk = t * 4 + j
                    for k in range(8):
                        ops_.append(lambda k=k, j=j, blk=blk, t=t: mm(
                            ps[:, PJ, j * 128:(j + 1) * 128], xT[:, k, blk * 128:(blk + 1) * 128], w[:, k, 256:384],
                            k == 0, k == 7, [XT[t], W], [PS[PJ]]))
                ops_.append(lambda t=t: cp("dve", VBs[si][:, t * 4:(t + 1) * 4, :],
                                          ps[:, PJ, :].rearrange("p (j d) -> p j d", j=4), [PS[PJ]], [VBbs[si][t]]))
            return ops_

        wB_load(0)
        for f_ in proj_ops(0):
            f_()
        for hp in range(4):
            QTB, KTB, VB = QTBs[hp % 2], KTBs[hp % 2], VBs[hp % 2]
            QBb, KBb, VBb = QBbs[hp % 2], KBbs[hp % 2], VBbs[hp % 2]
            nxt = []
            if hp < 3:
                wB_load(hp + 1)
                nxt = proj_ops(hp + 1)

            steps = []
            for qt in range(NT):
                acur = None
                kbs = list(range(4 * qt + 3, -1, -1))
                for si, kb in enumerate(kbs):
                    off = max(0, kb - 4 * qt) * 128
                    st_ = dict(qt=qt, kb=kb, off=off, N=512 - off, t0=qt * 512 + off, diag=kb >= 4 * qt,
                               first=si == 0, last=kb == 0, ob=6, acur=acur)
                    if kb != 0:
                        st_["anew"] = 0 if si == 0 else 1 - acur
                        acur = st_["anew"]
                    steps.append(st_)
            for n_, st_ in enumerate(steps):
                st_["zp"] = 2 * (n_ % 3)
                st_["e"] = n_ % NE
                st_["l"] = n_ % NL
                st_["a"] = n_ % 2

            def s_z(S):
                off, zp, kb, t0, N = S["off"], S["zp"], S["kb"], S["t0"], S["N"]
                for h in range(2):
                    r = slice(64 * h, 64 * h + 64)
                    mm(ps[:, zp + h, off:512], KTB[r, kb * 128:(kb + 1) * 128], QTB[r, t0:t0 + N], True, False,
                       [KBb[kb // 4], QBb[S["qt"]]], [PS[zp + h]])

            def s_el(S):
                off, zp, e_i, l_i = S["off"], S["zp"], S["e"], S["l"]
                Zb = [PS[zp], PS[zp + 1]]
                act(Et[e_i][:, :, off:512], ps[:, zp:zp + 2, off:512], AF.Exp, Zb, [ET[e_i]], scale=0.125)
                act(Lt[l_i][:, :, off:512], Et[e_i][:, :, off:512], AF.Ln, [ET[e_i]], [LT[l_i]], bias=1.0)
                if S["diag"]:
                    tt("dve", Lt[l_i][:, :, off:off + 128], Lt[l_i][:, :, off:off + 128], msb2, ALU.mult,
                       [LT[l_i], CB], [LT[l_i]])
                if not S["last"]:
                    anew, acur = S["anew"], S["acur"]
                    if S["first"]:
                        ms(Ac[anew][:, :, :], 0.0, [AC[anew]])
                        cp("dve", Ac[anew][:, :, off:512], Lt[l_i][:, :, off:512], [LT[l_i]], [AC[anew]])
                    else:
                        if off > 0:
                            ms(Ac[anew][:, :, :], 0.0, [AC[anew]])
                        tt("dve", Ac[anew][:, :, off:512], Ac[acur][:, :, off:512], Lt[l_i][:, :, off:512], ALU.add,
                           [AC[acur], LT[l_i]], [AC[anew]])

            def s_tri(S):
                off, zp, l_i = S["off"], S["zp"], S["l"]
                for h in range(2):
                    mm(ps[:, zp + h, off:512], ntri, Lt[l_i][:, h, off:512], False, S["first"],
                       [CB, LT[l_i]], [PS[zp + h]])
                    if not S["first"]:
                        mm(ps[:, zp + h, off:512], none_, Ac[S["acur"]][:, h, off:512], False, True,
                           [CB, AC[S["acur"]]], [PS[zp + h]])

            def s_a(S):
                off, zp, a_i = S["off"], S["zp"], S["a"]
                Zb = [PS[zp], PS[zp + 1]]
                act(At[a_i][:, :, off:512], ps[:, zp:zp + 2, off:512], AF.Exp, Zb, [AT[a_i]], scale=0.125)
                if S["diag"]:
                    tt("dve", At[a_i][:, :, off:off + 128], At[a_i][:, :, off:off + 128], msb2, ALU.mult,
                       [AT[a_i], CB], [AT[a_i]])

            def s_av(S):
                off, a_i, ob, kb = S["off"], S["a"], S["ob"], S["kb"]
                for h in range(2):
                    r = slice(64 * h, 64 * h + 64)
                    mm(ps[r, ob, off:512], VB[:, kb, r], At[a_i][:, h, off:512], S["first"], S["last"],
                       [VBb[kb // 4], AT[a_i]], [PS[ob]])
                if S["last"]:
                    qt = S["qt"]
                    cp("dve", yBT[:, hp, qt * 512:(qt + 1) * 512], ps[:, ob, :], [PS[ob]], [YB[qt]])

            ns = len(steps)
            for tau in range(ns + 2):
                rounds_left = max(1, ns - 4 - tau)
                take = -(-len(nxt) // rounds_left) if tau < ns - 4 else len(nxt)
                for f_ in nxt[:take]:
                    f_()
                nxt = nxt[take:]
                if tau < ns:
                    s_z(steps[tau])
                if 1 <= tau <= ns:
                    s_tri(steps[tau - 1])
                if 2 <= tau:
                    s_av(steps[tau - 2])
                if tau < ns:
                    s_el(steps[tau])
                if 1 <= tau <= ns:
                    s_a(steps[tau - 1])
        P.barrier()

        if debug:
            o = LOC
            dtmp = M.at(o, [128, 4, 512], F32)
            DT = Buf("dtmp")
            for t in range(NT):
                sl = slice(t * 512, (t + 1) * 512)
                cp("dve", dtmp[:, :, :], yAT[:, :, sl], [YA[t]], [DT])
                dma("sp", dbgA[:, :, sl], dtmp[:, :, :], reads=[DT], key=("dbg", 0))
                cp("dve", dtmp[:, :, :], yBT[:, :, sl], [YB[t]], [DT])
                dma("sp", dbgB[:, :, sl], dtmp[:, :, :], reads=[DT], key=("dbg", 0))
            P.barrier()

        o = LOC
        wa = M.at(o, [128, 4, 1024], BF16); o += 8192
        wb = M.at(o, [128, 4, 1024], BF16); o += 8192
        wo = M.at(o, [128, 8, 1024], BF16); o += 16384
        WAa, WBb, WO = Buf("wa"), Buf("wb"), Buf("wo")
        g1 = M.at(o, [128, D], F32); o += 4096
        b1 = M.at(o, [128, D], F32); o += 4096
        LN1 = Buf("ln1")
        bg = M.at(o, [128, 16], F32); o += 64
        BG = Buf("bg")
        epsb = M.at(o, [128, 1], F32); o += 32
        mhalf = M.at(o, [128, 1], F32); o += 32
        CC = Buf("cc")
        wgc = [M.at(o + i * 4096, [128, 8, 256], BF16) for i in range(2)]; o += 8192
        WG = [Buf("wg%d" % i) for i in range(2)]
        gt = [M.at(o, [128, 2, 512], F32)] * 2; o += 4096
        GT = [Buf("gt")] * 2
        hT = M.at(o, [128, 8, 512], BF16); o += 8192
        HT = [Buf("hT%d" % c) for c in range(8)]
        xin = [M.at(o + i * 4096, [128, D], F32) for i in range(2)]; o += 8192
        XIN = [Buf("xin%d" % i) for i in range(2)]
        x1b = [M.at(o + i * 2048, [128, D], BF16) for i in range(4)]; o += 8192
        X1B = [Buf("x1b%d" % i) for i in range(4)]
        st = [M.at(o + i * 64, [128, 2, 6], F32) for i in range(4)]; o += 256
        mv = [M.at(o + i * 32, [128, 2], F32) for i in range(4)]; o += 128
        rs = [M.at(o + i * 32, [128, 1], F32) for i in range(4)]; o += 128
        nbt = [M.at(o + i * 32, [128, 1], F32) for i in range(4)]; o += 128
        NBT = [Buf("nb%d" % i) for i in range(4)]
        STt = [Buf("st%d" % i) for i in range(4)]
        MV = [Buf("mv%d" % i) for i in range(4)]
        RS = [Buf("rs%d" % i) for i in range(4)]
        assert o <= 212992, o

        for k in range(0, 4, 2):
            dma("pool", wa[:, k:k + 2, :], wa_h[:, k:k + 2, :], writes=[WAa])
            dma("pool", wb[:, k:k + 2, :], wb_h[:, k:k + 2, :], writes=[WBb])
        for k in range(0, 8, 2):
            dma("pool", wo[:, k:k + 2, :], wo_h[:, k:k + 2, :], writes=[WO])
        dma("sp", g1[:, :], ln_h[0], writes=[LN1])
        dma("sp", b1[:, :], ln_h[1], writes=[LN1])
        dma("sp", bg[:, :], bg_h[:, :], writes=[BG])
        ms(epsb[:, :], EPS, [CC])
        ms(mhalf[:, :], -0.5, [CC])

        def layer_norm(yv, Y, gam, bet, LNB, out_ap, OUT, st_, mv_, rs_, ST_, MV_, RS_, eps_, mh_, CC_, nb_, NB_):
            for hh in range(2):
                P.op("dve", lambda e, o_=st_[:, hh, :], i_=yv[:, hh * 512:(hh + 1) * 512]: e.bn_stats(out=o_, in_=i_),
                     reads=[Y], writes=[ST_], multi=True)
            P.op("dve", lambda e, o_=mv_[:, :], i_=st_[:, :, :]: e.bn_aggr(out=o_, in_=i_), reads=[ST_], writes=[MV_])
            ts("pool", rs_[:, :], mv_[:, 1:2], eps_[:, 0:1], None, ALU.add, ALU.bypass, [MV_, CC_], [RS_])
            tt("pool", rs_[:, :], rs_[:, :], mh_[:, :], ALU.pow, [RS_, CC_], [RS_])
            stt(nb_[:, :], mv_[:, 0:1], -1.0, rs_[:, :], ALU.mult, ALU.mult, [MV_, RS_], [NB_])
            act(yv, yv, AF.Identity, [Y, RS_, NB_], [Y], bias=nb_[:, 0:1], scale=rs_[:, 0:1])
            tt("dve", yv, yv, gam, ALU.mult, [Y, LNB], [Y])
            tt("dve", out_ap, yv, bet, ALU.add, [Y, LNB], [OUT])

        def c1_R(t, j):
            rb = 4 + 2 * (j % 2)
            for hf in range(2):
                for c in range(8):
                    mm(ps[:, rb + hf, :], hT[:, c, j * 128:(j + 1) * 128], wo[:, c, hf * 512:(hf + 1) * 512],
                       c == 0, c == 7, [HT[c], WO], [PS[rb + hf]])

        def c1_R(t, j):
            rb = 4 + 2 * (j % 2)
            for hf in range(2):
                for c in range(8):
                    mm(ps[:, rb + hf, :], hT[:, c, j * 128:(j + 1) * 128], wo[:, c, hf * 512:(hf + 1) * 512],
                       c == 0, c == 7, [HT[c], WO], [PS[rb + hf]])

        def c1_tail(t):
            def xload(j):
                blk = t * 4 + j
                dma("sp", xin[j % 2][:, :], x_h[blk * 128:(blk + 1) * 128, :], writes=[XIN[j % 2]], key=("xin", j % 2))

            def evac(j):
                rb = 4 + 2 * (j % 2)
                stt(xin[j % 2][:, :], xin[j % 2][:, :], ALPHA, ps[:, rb:rb + 2, :].rearrange("p a n -> p (a n)"),
                    ALU.mult, ALU.add, [XIN[j % 2], PS[rb], PS[rb + 1]], [XIN[j % 2]])

            xload(0)
            xload(1)
            c1_R(t, 0)
            c1_R(t, 1)
            evac(0)
            evac(1)
            c1_R(t, 2)
            c1_R(t, 3)
            L = []

            def ln_stages(js):
                its = [dict(yv=xin[j % 2][:, :], Y=XIN[j % 2], st=st[j], mv=mv[j], rs=rs[j], nb=nbt[j],
                            ST=STt[j], MV=MV[j], RS=RS[j], NB=NBT[j], j=j, blk=t * 4 + j) for j in js]
                for it_ in its:
                    for hh in range(2):
                        L.append(lambda it_=it_, hh=hh: P.op(
                            "dve", lambda e, o_=it_["st"][:, hh, :], i_=it_["yv"][:, hh * 512:(hh + 1) * 512]:
                            e.bn_stats(out=o_, in_=i_), reads=[it_["Y"]], writes=[it_["ST"]], multi=True))
                for it_ in its:
                    L.append(lambda it_=it_: P.op("dve", lambda e, o_=it_["mv"][:, :], i_=it_["st"][:, :, :]:
                                                  e.bn_aggr(out=o_, in_=i_), reads=[it_["ST"]], writes=[it_["MV"]]))
                for it_ in its:
                    L.append(lambda it_=it_: ts("pool", it_["rs"][:, :], it_["mv"][:, 1:2], epsb[:, 0:1], None,
                                                ALU.add, ALU.bypass, [it_["MV"], CC], [it_["RS"]]))
                for it_ in its:
                    L.append(lambda it_=it_: tt("pool", it_["rs"][:, :], it_["rs"][:, :], mhalf[:, :], ALU.pow,
                                                [it_["RS"], CC], [it_["RS"]]))
                for it_ in its:
                    L.append(lambda it_=it_: stt(it_["nb"][:, :], it_["mv"][:, 0:1], -1.0, it_["rs"][:, :],
                                                 ALU.mult, ALU.mult, [it_["MV"], it_["RS"]], [it_["NB"]]))
                for it_ in its:
                    L.append(lambda it_=it_: act(it_["yv"], it_["yv"], AF.Identity, [it_["Y"], it_["RS"], it_["NB"]],
                                                 [it_["Y"]], bias=it_["nb"][:, 0:1], scale=it_["rs"][:, 0:1]))
                for it_ in its:
                    L.append(lambda it_=it_: tt("dve", it_["yv"], it_["yv"], g1[:, :], ALU.mult, [it_["Y"], LN1], [it_["Y"]]))
                for it_ in its:
                    L.append(lambda it_=it_: tt("dve", it_["yv"], it_["yv"], b1[:, :], ALU.add, [it_["Y"], LN1], [it_["Y"]]))
                for it_ in its:
                    j, blk = it_["j"], it_["blk"]
                    L.append(lambda j=j, blk=blk: dma("sp", x1_s[blk * 128:(blk + 1) * 128, :], xin[j % 2][:, :],
                                                       reads=[XIN[j % 2]], key=("x1s", j % 2)))
                    L.append(lambda j=j: cp("act", x1b[j][:, :], xin[j % 2][:, :], [XIN[j % 2]], [X1B[j]]))

            def t_stages(j):
                blk = t * 4 + j
                tb = 4 + j
                tpb = ps[:, tb, :].bitcast(BF16)
                for c in range(8):
                    L.append(lambda c=c: P.op(
                        "pe", lambda e, o_=tpb[:, c * 128:(c + 1) * 128], i_=x1b[j][:, c * 128:(c + 1) * 128],
                        id_=cb[:, C_ID:C_ID + 128]: e.transpose(out=o_, in_=i_, identity=id_),
                        reads=[X1B[j], CB], writes=[PS[tb]]))
                L.append(lambda: cp("act", xT[:, :, blk * 128:(blk + 1) * 128], tpb.rearrange("p (c n) -> p c n", c=8),
                                    [PS[tb]], [XT[t]]))

            ln_stages((0, 1))
            L.append(lambda: xload(2))
            L.append(lambda: xload(3))
            L.append(lambda: evac(2))
            L.append(lambda: evac(3))
            t_stages(0)
            t_stages(1)
            ln_stages((2, 3))
            t_stages(2)
            t_stages(3)
            return L

        gi = 0
        bi = 0
        pend1 = []

        def wg_load(n):
            if n < NT * 8:
                dma("pool", wgc[n % 2][:, :, :], wg_h[n % 8, :, :, :], writes=[WG[n % 2]], key=("wg", n % 2))
        wg_load(0)
        for t in range(NT):
            sl = slice(t * 512, (t + 1) * 512)
            for c in range(8):
                wi = gi % 2
                gi2 = gi % 2
                gi += 1
                gb = 0
                for ab in range(2):
                    for k in range(8):
                        mm(ps[:, gb + ab, :], wgc[wi][:, k, ab * 128:(ab + 1) * 128], xT[:, k, sl], k == 0, k == 7,
                           [WG[wi], XT[t]], [PS[gb + ab]])
                    act(gt[gi2][:, ab, :], ps[:, gb + ab, :], AF.Sigmoid, [PS[gb + ab], BG], [GT[gi2]],
                        bias=bg[:, ab * 8 + c:ab * 8 + c + 1])
                wg_load(gi)
                for k in range(4):
                    mm(ps[:, 2, :], wa[:, k, c * 128:(c + 1) * 128], yAT[:, k, sl], k == 0, k == 3, [WAa, YA[t]], [PS[2]])
                for k in range(4):
                    mm(ps[:, 3, :], wb[:, k, c * 128:(c + 1) * 128], yBT[:, k, sl], k == 0, k == 3, [WBb, YB[t]], [PS[3]])
                tt("dve", gt[gi2][:, 0, :], gt[gi2][:, 0, :], ps[:, 2, :], ALU.mult, [GT[gi2], PS[2]], [GT[gi2]])
                tt("dve", gt[gi2][:, 1, :], gt[gi2][:, 1, :], ps[:, 3, :], ALU.mult, [GT[gi2], PS[3]], [GT[gi2]])
                take = -(-len(pend1) // (8 - c))
                for f_ in pend1[:take]:
                    f_()
                del pend1[:take]
                tt("dve", hT[:, c, :], gt[gi2][:, 0, :], gt[gi2][:, 1, :], ALU.add, [GT[gi2]], [HT[c]])
            assert not pend1
            pend1.extend(c1_tail(t))
        for f_ in pend1:
            f_()
        del pend1[:]
        P.barrier()

        o = LOC2
        TQ = min(1024, T)
        NQt = T // TQ
        HPQ = TQ // 512
        wdn = M.at(o, [128, NJ, 1024], BF16); o += NJ * 2048
        WDN = Buf("wdn")
        hid = M.at(o, [128, NJ, TQ], BF16); o += NJ * TQ * 2
        HID = [[Buf("hid%d_%d" % (j, hh)) for hh in range(HPQ)] for j in range(NJ)]
        g2 = M.at(o, [128, D], F32); o += 4096
        b2 = M.at(o, [128, D], F32); o += 4096
        LN2 = Buf("ln2")
        cpm = M.at(o, [128, 2 * NJ, 4], F32); o += 2 * NJ * 16
        CPM = Buf("cpm")
        halo = M.at(o, [128, 2 * NJ, 2], F32); o += 2 * NJ * 8
        HALO = [Buf("halo%d" % jj) for jj in range(2 * NJ)]
        o = (o + 31) // 32 * 32
        wupc = [M.at(o + i * 4096, [128, 8, 256], BF16) for i in range(3)]; o += 12288
        WUP = [Buf("wup%d" % i) for i in range(3)]
        U = [[M.at(o + (i * 2 + g) * 2080, [128, 514], F32) for g in range(2)] for i in range(2)]; o += 4 * 2080
        UB = [[Buf("U%d_%d" % (i, g)) for g in range(2)] for i in range(2)]
        Aa = [[M.at(o + (i * 2 + g) * 2048, [128, 512], F32) for g in range(2)] for i in range(3)]; o += 6 * 2048
        AB = [[Buf("A%d_%d" % (i, g)) for g in range(2)] for i in range(3)]
        xin = [M.at(o + i * 4096, [128, D], F32) for i in range(2)]; o += 8192
        XIN = [Buf("xin2_%d" % i) for i in range(2)]
        st = [M.at(o + i * 64, [128, 2, 6], F32) for i in range(2)]; o += 128
        mv = [M.at(o + i * 32, [128, 2], F32) for i in range(2)]; o += 64
        rs = [M.at(o + i * 32, [128, 1], F32) for i in range(2)]; o += 64
        nbt = [M.at(o + i * 32, [128, 1], F32) for i in range(2)]; o += 64
        NBT = [Buf("nb2%d" % i) for i in range(2)]
        epsb = M.at(o, [128, 1], F32); o += 32
        mhalf = M.at(o, [128, 1], F32); o += 32
        STt = [Buf("st2%d" % i) for i in range(2)]
        MV = [Buf("mv2%d" % i) for i in range(2)]
        RS = [Buf("rs2%d" % i) for i in range(2)]
        CC = Buf("cc2")
        assert o <= 212992, o

        for j in range(0, NJ, 2):
            dma("pool", wdn[:, j:j + 2, :], wdn_h[:, j:j + 2, :], writes=[WDN])
        dma("sp", g2[:, :], ln_h[2], writes=[LN2])
        dma("sp", b2[:, :], ln_h[3], writes=[LN2])
        dma("sp", cpm[:, :, :], cp_h[:, :, :], writes=[CPM])
        ms(epsb[:, :], EPS, [CC])
        ms(mhalf[:, :], -0.5, [CC])
        ms(halo[:, :, :], 0.0, HALO)

        ui = 0
        wi_ = 0
        bi = 0
        wseq = [(q, j) for q in range(NQt) for j in range(NJ)]

        def wup_load(n):
            if n < len(wseq):
                dma("pool", wupc[n % 3][:, :, :], wup_h[wseq[n][1], :, :, :], writes=[WUP[n % 3]], key=("wup", n % 3))
        wup_load(0)
        wup_load(1)
        def ln_front(it_):
            for hh in range(2):
                P.op("dve", lambda e, o_=it_["st"][:, hh, :], i_=it_["yv"][:, hh * 512:(hh + 1) * 512]:
                     e.bn_stats(out=o_, in_=i_), reads=[it_["Y"]], writes=[it_["ST"]], multi=True)
            P.op("dve", lambda e, o_=it_["mv"][:, :], i_=it_["st"][:, :, :]: e.bn_aggr(out=o_, in_=i_),
                 reads=[it_["ST"]], writes=[it_["MV"]])
            ts("pool", it_["rs"][:, :], it_["mv"][:, 1:2], epsb[:, 0:1], None, ALU.add, ALU.bypass,
               [it_["MV"], CC], [it_["RS"]])
            tt("pool", it_["rs"][:, :], it_["rs"][:, :], mhalf[:, :], ALU.pow, [it_["RS"], CC], [it_["RS"]])
            stt(it_["nb"][:, :], it_["mv"][:, 0:1], -1.0, it_["rs"][:, :], ALU.mult, ALU.mult,
                [it_["MV"], it_["RS"]], [it_["NB"]])
            act(it_["yv"], it_["yv"], AF.Identity, [it_["Y"], it_["RS"], it_["NB"]], [it_["Y"]],
                bias=it_["nb"][:, 0:1], scale=it_["rs"][:, 0:1])

        def ln_back(it_, gam, bet, LNB):
            tt("dve", it_["yv"], it_["yv"], gam, ALU.mult, [it_["Y"], LNB], [it_["Y"]])
            tt("dve", it_["yv"], it_["yv"], bet, ALU.add, [it_["Y"], LNB], [it_["Y"]])

        pend = [None]

        def flush_pend():
            if pend[0] is not None:
                o_, a_, b_, rd, wr = pend[0]
                act(a_, a_, AF.Silu, [rd[0]], [rd[0]])
                tt("dve", o_, a_, b_, ALU.mult, rd, wr)
                pend[0] = None

        def down_front(q, jb):
            blk = q * (TQ // 128) + jb
            i = blk % 2
            rb = 4 + 2 * i
            dma("sp", xin[i][:, :], x1_s[blk * 128:(blk + 1) * 128, :], writes=[XIN[i]], key=("xin", i))
            for hf in range(2):
                for j in range(NJ):
                    mm(ps[:, rb + hf, :], hid[:, j, jb * 128:(jb + 1) * 128], wdn[:, j, hf * 512:(hf + 1) * 512],
                       j == 0, j == NJ - 1, [HID[j][jb // 4], WDN], [PS[rb + hf]])
            stt(xin[i][:, :], xin[i][:, :], ALPHA, ps[:, rb:rb + 2, :].rearrange("p a n -> p (a n)"),
                ALU.mult, ALU.add, [XIN[i], PS[rb], PS[rb + 1]], [XIN[i]])
            it_ = dict(yv=xin[i][:, :], Y=XIN[i], st=st[i], mv=mv[i], rs=rs[i], nb=nbt[i],
                       ST=STt[i], MV=MV[i], RS=RS[i], NB=NBT[i], blk=blk, i=i)
            ln_front(it_)
            return it_

        def down_back(it_):
            ln_back(it_, g2[:, :], b2[:, :], LN2)
            blk, i = it_["blk"], it_["i"]
            dma("sp", out_h[blk * 128:(blk + 1) * 128, :], xin[i][:, :], reads=[XIN[i]], key=("out", i))

        for q in range(NQt):
            for j in range(NJ):
                wi = wi_ % 3
                wup_load(wi_ + 2)
                wi_ += 1
                for hh in range(HPQ):
                    tI = q * HPQ + hh
                    sl = slice(tI * 512, (tI + 1) * 512)
                    i = ui % 2
                    ia = ui % 3
                    ui += 1
                    for g in range(2):
                        jj = g * NJ + j
                        b = 2 * i + g
                        for k in range(8):
                            mm(ps[:, b, :], wupc[wi][:, k, g * 128:(g + 1) * 128], xT[:, k, sl], k == 0, k == 7,
                               [WUP[wi], XT[tI]], [PS[b]])
                        cp("pool", U[i][g][:, 0:2], halo[:, jj, :], [HALO[jj]], [UB[i][g]])
                        P.op("act", lambda e, o_=U[i][g][:, 2:514], i_=ps[:, b, :]: e.activation(out=o_, in_=i_, func=AF.Copy),
                             reads=[PS[b]], writes=[UB[i][g]], multi=True)
                        act(Aa[ia][g][:, :], ps[:, b, :], AF.Identity, [PS[b], CPM], [AB[ia][g]],
                            bias=cpm[:, jj, 3:4], scale=cpm[:, jj, 2:3])
                        cp("pool", halo[:, jj, :], U[i][g][:, 512:514], [UB[i][g]], [HALO[jj]])
                        stt(Aa[ia][g][:, :], U[i][g][:, 1:513], cpm[:, jj, 1:2], Aa[ia][g][:, :], ALU.mult, ALU.add,
                            [UB[i][g], CPM, AB[ia][g]], [AB[ia][g]])
                        stt(Aa[ia][g][:, :], U[i][g][:, 0:512], cpm[:, jj, 0:1], Aa[ia][g][:, :], ALU.mult, ALU.add,
                            [UB[i][g], CPM, AB[ia][g]], [AB[ia][g]])
                    flush_pend()
                    pend[0] = (hid[:, j, hh * 512:(hh + 1) * 512], Aa[ia][0][:, :], Aa[ia][1][:, :],
                               [AB[ia][0], AB[ia][1]], [HID[j][hh]])
            flush_pend()
            prev = None
            for jb in range(TQ // 128):
                cur = down_front(q, jb)
                if prev is not None:
                    down_back(prev)
                prev = cur
            down_back(prev)
        P.barrier()
        P.emit(nc, stack)
    return nc


def host_prep(b, x, positions, w_in, b_gate, sinks, w_branch_a, w_branch_b, w_out,
              ln1_g, ln1_b, w_up, conv_w, conv_b, w_down, ln2_g, ln2_b, shared):
    m = dict(shared)
    m["xT"] = np.ascontiguousarray(x[b].T)
    m["x"] = np.ascontiguousarray(x[b])
    m["pos"] = np.ascontiguousarray(np.broadcast_to(positions[b][None, :].astype(np.int32), (128, positions.shape[1])))
    return m


def host_shared(w_in, b_gate, sinks, w_branch_a, w_branch_b, w_out,
                ln1_g, ln1_b, w_up, conv_w, conv_b, w_down, ln2_g, ln2_b):
    f = np.float32
    w_in = w_in[0]
    cbm = np.zeros((128, NCB), f)
    cbm[:, C_ID:C_ID + 128] = np.eye(128, dtype=f)
    rot = np.zeros((128, 128), f)
    for m_ in range(128):
        base, ml = (m_ // 64) * 64, m_ % 64
        if ml < 32:
            rot[base + ml + 32, m_] = -1.0
        else:
            rot[base + ml - 32, m_] = 1.0
    cbm[:, C_ROT:C_ROT + 128] = rot
    jj, ss = np.meshgrid(np.arange(128), np.arange(128), indexing="ij")
    cbm[:, C_TRI:C_TRI + 128] = np.where(jj >= ss, -8.0, 0.0)
    cbm[:, C_ONE:C_ONE + 128] = -8.0
    s_, t_ = np.meshgrid(np.arange(128), np.arange(128), indexing="ij")
    msb = (s_ < t_).astype(f)
    cbm[:, C_MSB:C_MSB + 128] = msb
    cbm[:, C_MSB + 128:C_MSB + 256] = msb
    mswa = np.concatenate([(s_ <= t_).astype(f), (s_ > t_).astype(f)], axis=1)
    cbm[:, C_MSWA:C_MSWA + 256] = mswa
    cbm[:, C_MSWA + 256:C_MSWA + 512] = mswa
    cbm[:, C_O64:C_O64 + 64] = 1.0
    cfm = np.zeros((128, 8), f)
    inv = (f(1.0) / np.power(f(10000.0), np.arange(0, 64, 2, dtype=f) / f(64.0))).astype(f)
    cfm[:, 0] = inv[np.arange(128) % 32]
    def kp(w):
        K, N = w.shape
        return np.ascontiguousarray(w.reshape(K // 128, 128, N).transpose(1, 0, 2))
    qa_cols = np.concatenate([np.r_[c * 64:(c + 1) * 64, (4 + c) * 64:(5 + c) * 64] for c in range(4)])
    wA = np.concatenate([w_in[:, 0:512][:, qa_cols], w_in[:, 512:640], w_in[:, 640:768]], axis=1)
    wB = np.stack([np.concatenate([w_in[:, 768 + hp * 128:768 + (hp + 1) * 128],
                                   w_in[:, 1280 + hp * 128:1280 + (hp + 1) * 128],
                                   w_in[:, 1792 + hp * 128:1792 + (hp + 1) * 128]], axis=1) for hp in range(4)])
    wgA, wgB = w_in[:, 2304:3328], w_in[:, 3328:4352]
    wg = np.stack([np.concatenate([wgA[:, c * 128:(c + 1) * 128], wgB[:, c * 128:(c + 1) * 128]], axis=1)
                   for c in range(8)])
    frow = np.array([[(c if p < 64 else 4 + c) * 64 + p % 64 for c in range(4)] for p in range(128)])
    wa = np.ascontiguousarray(w_branch_a[0][frow, :])
    wup = w_up[0]
    wupr = np.stack([np.concatenate([wup[:, j * 128:(j + 1) * 128], wup[:, DFF + j * 128:DFF + (j + 1) * 128]], axis=1)
                     for j in range(NJ)])
    cpm = np.zeros((128, 2 * NJ, 4), f)
    cw, cbias = conv_w[0], conv_b[0]
    for jj_ in range(2 * NJ):
        ch = jj_ * 128 + np.arange(128)
        cpm[:, jj_, 0:3] = cw[:, ch].T
        cpm[:, jj_, 3] = cbias[ch]
    bgm = np.zeros((128, 16), f)
    for c in range(8):
        bgm[:, c] = b_gate[0][c * 128:(c + 1) * 128]
        bgm[:, 8 + c] = b_gate[0][1024 + c * 128:1024 + (c + 1) * 128]
    skm = np.zeros((128, 4), f)
    for c in range(4):
        skm[0:64, c] = sinks[0][c]
        skm[64:128, c] = sinks[0][4 + c]
    lnm = np.stack([np.broadcast_to(v[0][None, :], (128, D)) for v in (ln1_g, ln1_b, ln2_g, ln2_b)]).astype(f)
    return dict(
        cb=cbm, cf=cfm,
        wA=kp(wA), wB=np.stack([kp(wB[hp]) for hp in range(4)]),
        wg=np.stack([kp(wg[c]) for c in range(8)]),
        wa=wa, wb=kp(w_branch_b[0]), wo=kp(w_out[0]),
        wup=np.stack([kp(wupr[j]) for j in range(NJ)]), wdn=kp(w_down[0]),
        cp=cpm, bg=bgm, sk=skm, ln=np.ascontiguousarray(lnm),
    )


_NC_CACHE = {}


def kernel(x, positions, w_in, b_gate, sinks, w_branch_a, w_branch_b, w_out,
           ln1_g, ln1_b, w_up, conv_w, conv_b, w_down, ln2_g, ln2_b):
    args = [np.asarray(a) for a in (x, positions, w_in, b_gate, sinks, w_branch_a, w_branch_b, w_out,
                                    ln1_g, ln1_b, w_up, conv_w, conv_b, w_down, ln2_g, ln2_b)]
    x, positions = args[0], args[1]
    B, T, _ = x.shape
    shared = host_shared(*args[2:])
    in_maps = [host_prep(b, x, positions, *args[2:], shared) for b in range(B)]
    nc = build(T)
    res = run_bass_kernel_spmd(nc, in_maps, core_ids=list(range(B)))
    return np.stack([np.asarray(r["out"]) for r in res.results]).astype(np.float32)
```

```python
import numpy as np
import concourse.bass as bass
import concourse.mybir as mybir
from concourse.bass_utils import run_bass_kernel_spmd

F32 = mybir.dt.float32
BF16 = mybir.dt.bfloat16
I32 = mybir.dt.int32
AF = mybir.ActivationFunctionType
ALU = mybir.AluOpType

D = 1024
DFF = 2816
NJ = DFF // 128
HD = 64
ALPHA = float(2.0 ** 0.25)
EPS = 1e-5
PI = float(np.pi)
INV2PI = float(np.float32(1.0 / (2.0 * np.pi)))
MAGIC = 12582912.0
CW1 = 6.28125
CW2 = float(2.0 * np.pi - 6.28125)
SBUF_BASE = 16384

C_ID, C_ROT, C_TRI, C_ONE, C_MSB, C_MSWA, C_O64, NCB = 0, 128, 256, 384, 512, 768, 1280, 1344


class Buf:
    __slots__ = ("name", "w", "r", "psum")

    def __init__(self, name, psum=False):
        self.name = name
        self.w = []
        self.r = []
        self.psum = psum


class Prog:
    ENGS = ("pe", "act", "dve", "pool", "sp")

    def __init__(self):
        self.ops = []
        self.bar_start = 0

    def op(self, eng, fn, reads=(), writes=(), dma=None, multi=False):
        i = len(self.ops)
        deps = set()
        for b in reads:
            deps.update(b.w)
            if b.psum:
                deps.update(r for r in b.r if self.ops[r]["eng"] != eng)
        for b in writes:
            if not multi:
                deps.update(b.w)
            deps.update(b.r)
        for b in reads:
            b.r.append(i)
        for b in writes:
            if multi and not b.r:
                b.w = b.w + [i]
            else:
                b.w = [i]
            b.r = []
        deps.discard(i)
        last = {}
        keep = set()
        for d in deps:
            od = self.ops[d]
            if od["dma"] is not None:
                keep.add(d)
            elif last.get(od["eng"], -1) < d:
                last[od["eng"]] = d
        keep.update(last.values())
        self.ops.append(dict(eng=eng, fn=fn, deps=keep, dma=dma))
        return i

    def barrier(self):
        last = {}
        deps = set()
        for idx in range(self.bar_start, len(self.ops)):
            o = self.ops[idx]
            if o["fn"] is None:
                continue
            if o["dma"] is not None:
                deps.add(idx)
            else:
                last[o["eng"]] = idx
        deps.update(last.values())
        for e in self.ENGS:
            self.ops.append(dict(eng=e, fn=None, deps=set(deps), dma=None))
        self.bar_start = len(self.ops)

    def emit(self, nc, stack):
        ops = self.ops
        need = set()
        for o in ops:
            for d in o["deps"]:
                od = ops[d]
                if o["eng"] == "pe" and od["eng"] == "pe" and od["dma"] is None:
                    continue
                need.add(d)
        esem = {e: stack.enter_context(nc.semaphore("s_" + e)) for e in self.ENGS}
        dsem = {}
        tick = {e: 0 for e in self.ENGS}
        dcnt = {}
        sig = {}
        for i, o in enumerate(ops):
            if i not in need or o["fn"] is None:
                continue
            if o["dma"] is not None:
                k = o["dma"]
                if k not in dsem:
                    dsem[k] = stack.enter_context(nc.semaphore("d_%d" % len(dsem)))
                    dcnt[k] = 0
                dcnt[k] += 16
                sig[i] = (dsem[k], dcnt[k], 16)
            else:
                tick[o["eng"]] += 1
                sig[i] = (esem[o["eng"]], tick[o["eng"]], 1)
        per = {e: [] for e in self.ENGS}
        for i, o in enumerate(ops):
            per[o["eng"]].append(i)
        self.stats = dict(ticks=dict(tick), nsem=len(dsem) + len(esem), nops=len(ops),
                          dmax=max(dcnt.values()) if dcnt else 0)
        block = stack.enter_context(nc.Block())

        def run(eng, ename):
            waited = {}
            for i in per[ename]:
                o = ops[i]
                for d in sorted(o["deps"]):
                    if d not in sig:
                        continue
                    od = ops[d]
                    if ename == "pe" and od["eng"] == "pe" and od["dma"] is None:
                        continue
                    sem, val, _ = sig[d]
                    key = id(sem)
                    if waited.get(key, 0) >= val:
                        continue
                    eng.wait_ge(sem, val)
                    waited[key] = val
                if o["fn"] is not None:
                    ins = o["fn"](eng)
                    if i in sig:
                        ins.then_inc(sig[i][0], sig[i][2])

        @block.tensor
        def _(e):
            run(e, "pe")

        @block.scalar
        def _(e):
            run(e, "act")

        @block.vector
        def _(e):
            run(e, "dve")

        @block.gpsimd
        def _(e):
            run(e, "pool")

        @block.sync
        def _(e):
            run(e, "sp")


class Mem:
    def __init__(self, nc):
        self.nc = nc
        self.n = 0

    def at(self, off, shape, dt):
        self.n += 1
        return self.nc.alloc_sbuf_tensor_at("t%d" % self.n, list(shape), dt, offset=SBUF_BASE + off).ap()


def build(T=4096, debug=False):
    NB = T // 128
    NT = T // 512
    nc = bass.Bass("TRN2", target_bir_lowering=False)
    P = Prog()
    M = Mem(nc)

    def din(name, shape, dt=F32):
        return nc.dram_tensor(name, list(shape), dt, kind="ExternalInput").ap()

    xT_h = din("xT", [D, T])
    x_h = din("x", [T, D])
    pos_h = din("pos", [128, T], I32)
    cb_h = din("cb", [128, NCB])
    cf_h = din("cf", [128, 8])
    wA_h = din("wA", [128, 8, 768])
    wB_h = din("wB", [4, 128, 8, 384])
    wg_h = din("wg", [8, 128, 8, 256])
    wa_h = din("wa", [128, 4, 1024])
    wb_h = din("wb", [128, 4, 1024])
    wo_h = din("wo", [128, 8, 1024])
    wup_h = din("wup", [NJ, 128, 8, 256])
    wdn_h = din("wdn", [128, NJ, 1024])
    cp_h = din("cp", [128, 2 * NJ, 4])
    bg_h = din("bg", [128, 16])
    sk_h = din("sk", [128, 4])
    ln_h = din("ln", [4, 128, D])
    out_h = nc.dram_tensor("out", [T, D], F32, kind="ExternalOutput").ap()
    x1_s = nc.dram_tensor("x1s", [T, D], F32, kind="Internal").ap()
    rt_s = nc.dram_tensor("rts", [2, 128, T], F32, kind="Internal").ap()
    if debug:
        dbgA = nc.dram_tensor("dbgA", [128, 4, T], F32, kind="ExternalOutput").ap()
        dbgB = nc.dram_tensor("dbgB", [128, 4, T], F32, kind="ExternalOutput").ap()

    import contextlib
    stack = contextlib.ExitStack()
    with stack:
        ps = stack.enter_context(nc.psum_tensor([128, 8, 512], F32))
        PS = [Buf("ps%d" % b, psum=True) for b in range(8)]

        o = 0
        cb = M.at(o, [128, NCB], BF16); o += 2 * NCB
        CB = Buf("cb")
        cf = M.at(o, [128, 8], F32); o += 32
        CF = Buf("cf")
        xT = M.at(o, [128, 8, T], BF16); o += 16 * T
        XT = [Buf("xT%d" % t) for t in range(NT)]
        LOC2 = o
        yAT = M.at(o, [128, 4, T], BF16); o += 8 * T
        YA = [Buf("yA%d" % t) for t in range(NT)]
        yBT = M.at(o, [128, 4, T], BF16); o += 8 * T
        YB = [Buf("yB%d" % t) for t in range(NT)]
        LOC = o
        assert LOC % 32 == 0

        def dma(eng, out, in_, reads=(), writes=(), key=None, multi=True):
            if key is None:
                key = ("w", writes[0].name) if writes else ("r", reads[0].name)
            return P.op(eng, lambda e, out=out, in_=in_: e.dma_start(out=out, in_=in_),
                        reads=reads, writes=writes, dma=key, multi=multi)

        def ms(ap, val, writes):
            return P.op("pool", lambda e, ap=ap, val=val: e.memset(ap, val), writes=writes)

        def mm(out, lhsT, rhs, start, stop, reads, writes):
            return P.op("pe", lambda e, out=out, lhsT=lhsT, rhs=rhs, start=start, stop=stop:
                        e.matmul(out, lhsT=lhsT, rhs=rhs, start=start, stop=stop, skip_group_check=True),
                        reads=reads, writes=writes)

        def act(out, in_, func, reads, writes, bias=0.0, scale=1.0):
            return P.op("act", lambda e, out=out, in_=in_, func=func, bias=bias, scale=scale:
                        e.activation(out=out, in_=in_, func=func, bias=bias, scale=scale),
                        reads=reads, writes=writes)

        def tt(eng, out, in0, in1, op, reads, writes):
            return P.op(eng, lambda e, out=out, in0=in0, in1=in1, op=op:
                        e.tensor_tensor(out=out, in0=in0, in1=in1, op=op), reads=reads, writes=writes)

        def ts(eng, out, in0, s1, s2, op0, op1, reads, writes):
            return P.op(eng, lambda e, out=out, in0=in0, s1=s1, s2=s2, op0=op0, op1=op1:
                        e.tensor_scalar(out=out, in0=in0, scalar1=s1, scalar2=s2, op0=op0, op1=op1),
                        reads=reads, writes=writes)

        def stt(out, in0, scalar, in1, op0, op1, reads, writes):
            return P.op("dve", lambda e, out=out, in0=in0, scalar=scalar, in1=in1, op0=op0, op1=op1:
                        e.scalar_tensor_tensor(out=out, in0=in0, scalar=scalar, in1=in1, op0=op0, op1=op1),
                        reads=reads, writes=writes)

        def cp(eng, out, in_, reads, writes):
            if eng == "act":
                return act(out, in_, AF.Copy, reads, writes)
            return P.op(eng, lambda e, out=out, in_=in_: e.tensor_copy(out=out, in_=in_),
                        reads=reads, writes=writes)

        dma("pool", cb[:, :], cb_h[:, :], writes=[CB])
        dma("sp", cf[:, :], cf_h[:, :], writes=[CF])
        xv = xT_h.rearrange("(c p) t -> p c t", p=128)

        def load_xT(t):
            dma("pool", xT[:, :, t * 512:(t + 1) * 512], xv[:, :, t * 512:(t + 1) * 512], writes=[XT[t]])
        load_xT(0)

        o = LOC
        QTA = M.at(o, [128, 2, T], BF16); o += 4 * T
        QA = [[Buf("qa%d_%d" % (c, t)) for t in range(NT)] for c in range(4)]
        KTA = M.at(o, [128, T], BF16); o += 2 * T
        KA = [Buf("ka%d" % t) for t in range(NT)]
        VA = M.at(o, [128, NB, 128], BF16); o += 2 * T
        VAb = [Buf("va%d" % t) for t in range(NT)]
        sk = M.at(o, [128, 4], F32); o += 32
        SK = Buf("sk")
        npi = M.at(o, [128, 1], F32); o += 32
        NPI = Buf("npi")
        A2 = o
        wA = M.at(o, [128, 8, 768], BF16); o += 8 * 768 * 2
        WA = Buf("wA")
        ang = M.at(o, [128, 512], F32); o += 2048
        tmpa = M.at(o, [128, 512], F32); o += 2048
        tmpb = M.at(o, [128, 512], F32); o += 2048
        posi = tmpb.bitcast(I32)
        TMPB = Buf("tmpb")
        cosTs = [M.at(o + i * 2048, [128, 512], F32) for i in range(2)]; o += 4096
        sinTs = [M.at(o + i * 2048, [128, 512], F32) for i in range(2)]; o += 4096
        COSs = [Buf("cos%d" % i) for i in range(2)]
        SINs = [Buf("sin%d" % i) for i in range(2)]
        ANG, TMPA = Buf("ang"), Buf("tmpa")
        POSI = TMPB
        q32 = [M.at(o + i * 2048, [128, 512], F32) for i in range(2)]; o += 4096
        qb = [M.at(o + i * 1024, [128, 512], BF16) for i in range(2)]; o += 2048
        t2 = [M.at(o + i * 2048, [128, 512], F32) for i in range(2)]; o += 4096
        Q32 = [Buf("q32_%d" % i) for i in range(2)]
        QB = [Buf("qb_%d" % i) for i in range(2)]
        T2 = [Buf("t2_%d" % i) for i in range(2)]
        assert o <= 212992, o

        for k in range(0, 8, 2):
            dma("pool", wA[:, k:k + 2, :], wA_h[:, k:k + 2, :], writes=[WA])
        for t in range(1, NT):
            load_xT(t)
        dma("sp", sk[:, :], sk_h[:, :], writes=[SK])
        ms(npi[:, :], -PI, [NPI])
        act(sk[:, :], sk[:, :], AF.Exp, [SK], [SK])

        ctr = [0]
        rp_pend = [None]
        RTB = [[Buf("rt%d_%d" % (k, t)) for t in range(NT)] for k in range(2)]

        def rope_flush():
            if rp_pend[0] is not None:
                f_ = rp_pend[0]
                rp_pend[0] = None
                f_()

        def rope_proj(wcols, dst, dstbuf, t, cosT, sinT, COS, SIN):
            i = ctr[0] % 2
            ctr[0] += 1
            b0, b1 = 2 * i, 2 * i + 1
            for k in range(8):
                mm(ps[:, b0, :], wA[:, k, wcols], xT[:, k, t * 512:(t + 1) * 512], k == 0, k == 7,
                   [WA, XT[t]], [PS[b0]])
            act(q32[i][:, :], ps[:, b0, :], AF.Copy, [PS[b0]], [Q32[i]])
            cp("act", qb[i][:, :], ps[:, b0, :], [PS[b0]], [QB[i]])
            rope_flush()

            def back():
                mm(ps[:, b1, :], cb[:, C_ROT:C_ROT + 128], qb[i][:, :], True, True, [CB, QB[i]], [PS[b1]])
                tt("dve", q32[i][:, :], q32[i][:, :], cosT[:, :], ALU.mult, [Q32[i], COS], [Q32[i]])
                tt("dve", t2[i][:, :], ps[:, b1, :], sinT[:, :], ALU.mult, [PS[b1], SIN], [T2[i]])
                tt("dve", dst, q32[i][:, :], t2[i][:, :], ALU.add, [Q32[i], T2[i]], [dstbuf])
            rp_pend[0] = back

        Pt = [M.at(o + i * 1024, [128, 2, 256], BF16) for i in range(2)]; o += 2048
        PT = [Buf("pt%d" % i) for i in range(2)]
        dn = [M.at(o + i * 512, [128, 128], F32) for i in range(2)]; o += 1024
        DN = [Buf("dn%d" % i) for i in range(2)]
        assert o <= 212992, o
        msw = cb[:, C_MSWA:C_MSWA + 512].rearrange("p (h u) -> p h u", h=2)
        o64 = cb[:, C_O64:C_O64 + 64]
        for half in range(2):
            for t in range(NT):
                sl = slice(t * 512, (t + 1) * 512)
                cosT, sinT, COS, SIN = cosTs[t % 2], sinTs[t % 2], COSs[t % 2], SINs[t % 2]
                if half == 0:
                    dma("sp", posi[:, :], pos_h[:, sl], writes=[POSI])
                    cp("dve", ang[:, :], posi[:, :], [POSI], [ANG])
                    ts("dve", ang[:, :], ang[:, :], cf[:, 0:1], None, ALU.mult, ALU.bypass, [ANG, CF], [ANG])
                    def sin_of(base_ap, BASE, dst, DST):
                        ts("dve", tmpa[:, :], base_ap, INV2PI, MAGIC, ALU.mult, ALU.add, [BASE], [TMPA])
                        ts("dve", tmpa[:, :], tmpa[:, :], MAGIC, None, ALU.subtract, ALU.bypass, [TMPA], [TMPA])
                        stt(tmpb[:, :], tmpa[:, :], -CW1, base_ap, ALU.mult, ALU.add, [TMPA, BASE], [TMPB])
                        stt(tmpb[:, :], tmpa[:, :], -CW2, tmpb[:, :], ALU.mult, ALU.add, [TMPA, TMPB], [TMPB])
                        ts("dve", tmpb[:, :], tmpb[:, :], 3.141592, -3.141592, ALU.min, ALU.max, [TMPB], [TMPB])
                        act(dst, tmpb[:, :], AF.Sin, [TMPB], [DST])
                    sin_of(ang[:, :], ANG, sinT[:, :], SIN)
                    ts("dve", ang[:, :], ang[:, :], 0.5 * PI, None, ALU.add, ALU.bypass, [ANG], [ANG])
                    sin_of(ang[:, :], ANG, cosT[:, :], COS)
                    dma("sp", rt_s[0, :, sl], sinT[:, :], reads=[SIN], writes=[RTB[0][t]], key=("rts", 0, t % 2))
                    dma("sp", rt_s[1, :, sl], cosT[:, :], reads=[COS], writes=[RTB[1][t]], key=("rts", 1, t % 2))
                else:
                    dma("sp", sinT[:, :], rt_s[0, :, sl], reads=[RTB[0][t]], writes=[SIN], key=("rtl", 0, t % 2))
                    dma("sp", cosT[:, :], rt_s[1, :, sl], reads=[RTB[1][t]], writes=[COS], key=("rtl", 1, t % 2))
                if half == 0:
                    rope_proj(slice(512, 640), KTA[:, sl], KA[t], t, cosT, sinT, COS, SIN)
                for c in (2 * half, 2 * half + 1):
                    rope_proj(slice(c * 128, (c + 1) * 128), QTA[:, c % 2, sl], QA[c % 2][t], t, cosT, sinT, COS, SIN)
                for j in range(4 if half == 0 else 0):
                    blk = t * 4 + j
                    for k in range(8):
                        mm(ps[:, 4, j * 128:(j + 1) * 128], xT[:, k, blk * 128:(blk + 1) * 128], wA[:, k, 640:768],
                           k == 0, k == 7, [XT[t], WA], [PS[4]])
                if half == 0:
                    cp("act", VA[:, t * 4:(t + 1) * 4, :], ps[:, 4, :].rearrange("p (j d) -> p j d", j=4), [PS[4]], [VAb[t]])

            rope_flush()
            asteps = [(c, kb) for c in (2 * half, 2 * half + 1) for kb in range(NB)]

            def a_front(n):
                c, kb = asteps[n]
                N = 256 if kb < NB - 1 else 128
                i = n % 2
                b0 = 2 * i
                t0 = kb * 128
                tq = [QA[c % 2][(t0) // 512]] + ([QA[c % 2][(t0 + 128) // 512]] if N == 256 else [])
                for h in range(2):
                    r = slice(64 * h, 64 * h + 64)
                    mm(ps[:, b0 + h, 0:N], KTA[r, t0:t0 + 128], QTA[r, c % 2, t0:t0 + N], True, True,
                       [KA[kb // 4]] + tq, [PS[b0 + h]])
                act(Pt[i][:, :, 0:N], ps[:, b0:b0 + 2, 0:N], AF.Exp, [PS[b0], PS[b0 + 1]], [PT[i]], scale=0.125)
                tt("dve", Pt[i][:, :, 0:N], Pt[i][:, :, 0:N], msw[:, :, 0:N], ALU.mult, [PT[i], CB], [PT[i]])

            def a_back(n):
                c, kb = asteps[n]
                N = 256 if kb < NB - 1 else 128
                i = n % 2
                t0 = kb * 128
                ob, db = 4 + kb % 2, 6 + kb % 2
                ob1, db1 = 4 + (kb + 1) % 2, 6 + (kb + 1) % 2
                for h in range(2):
                    r = slice(64 * h, 64 * h + 64)
                    vsl = VA[:, kb, r]
                    mm(ps[r, ob, 0:128], vsl, Pt[i][:, h, 0:128], kb == 0, True, [VAb[kb // 4], PT[i]], [PS[ob]])
                    mm(ps[r, db, 0:128], o64, Pt[i][:, h, 0:128], kb == 0, True, [CB, PT[i]], [PS[db]])
                j = kb % 2
                act(dn[j][:, :], ps[:, db, 0:128], AF.Ln, [PS[db], SK], [DN[j]], bias=sk[:, c:c + 1])
                act(dn[j][:, :], dn[j][:, :], AF.Exp, [DN[j]], [DN[j]], scale=-1.0)
                tt("dve", yAT[:, c, t0:t0 + 128], ps[:, ob, 0:128], dn[j][:, :], ALU.mult,
                   [PS[ob], DN[j]], [YA[kb // 4]])
                if N == 256:
                    for h in range(2):
                        r = slice(64 * h, 64 * h + 64)
                        vsl = VA[:, kb, r]
                        mm(ps[r, ob1, 0:128], vsl, Pt[i][:, h, 128:256], True, False, [VAb[kb // 4], PT[i]], [PS[ob1]])
                        mm(ps[r, db1, 0:128], o64, Pt[i][:, h, 128:256], True, False, [CB, PT[i]], [PS[db1]])

            a_front(0)
            for n in range(len(asteps)):
                if n + 1 < len(asteps):
                    a_front(n + 1)
                a_back(n)
        P.barrier()

        o = LOC
        QTBs = [M.at(o + i * 2 * T, [128, T], BF16) for i in range(2)]; o += 4 * T
        KTBs = [M.at(o + i * 2 * T, [128, T], BF16) for i in range(2)]; o += 4 * T
        VBs = [M.at(o + i * 2 * T, [128, NB, 128], BF16) for i in range(2)]; o += 4 * T
        QBbs = [[Buf("qB%d_%d" % (i, t)) for t in range(NT)] for i in range(2)]
        KBbs = [[Buf("kB%d_%d" % (i, t)) for t in range(NT)] for i in range(2)]
        VBbs = [[Buf("vB%d_%d" % (i, t)) for t in range(NT)] for i in range(2)]
        wB = [M.at(o + i * 6144, [128, 8, 384], BF16) for i in range(2)]; o += 12288
        WB = [Buf("wB%d" % i) for i in range(2)]
        NE = 1
        Et = [M.at(o + i * 4096, [128, 2, 512], F32) for i in range(NE)]; o += 4096 * NE
        ET = [Buf("E%d" % i) for i in range(NE)]
        NL = 2
        Lt = [M.at(o + i * 2048, [128, 2, 512], BF16) for i in range(NL)]; o += 2048 * NL
        LT = [Buf("L%d" % i) for i in range(NL)]
        Ac = [M.at(o + i * 2048, [128, 2, 512], BF16) for i in range(2)]; o += 4096
        AC = [Buf("Ac%d" % i) for i in range(2)]
        At = [M.at(o + i * 2048, [128, 2, 512], BF16) for i in range(2)]; o += 4096
        AT = [Buf("At%d" % i) for i in range(2)]
        assert o <= 212992, o
        msb2 = cb[:, C_MSB:C_MSB + 256].rearrange("p (h u) -> p h u", h=2)
        ntri = cb[:, C_TRI:C_TRI + 128]
        none_ = cb[:, C_ONE:C_ONE + 128]
        PJ = 7

        def wB_load(hp):
            for k in range(0, 8, 4):
                dma("pool", wB[hp % 2][:, k:k + 4, :], wB_h[hp, :, k:k + 4, :], writes=[WB[hp % 2]])

        def proj_ops(hp):
            w, W, si = wB[hp % 2], WB[hp % 2], hp % 2
            ops_ = []
            for t in range(NT):
                sl = slice(t * 512, (t + 1) * 512)
                for which, dst, dbuf in ((0, QTBs[si], QBbs[si]), (1, KTBs[si], KBbs[si])):
                    for k in range(8):
                        ops_.append(lambda k=k, which=which, sl=sl, t=t: mm(
                            ps[:, PJ, :], w[:, k, which * 128:(which + 1) * 128], xT[:, k, sl], k == 0, k == 7,
                            [W, XT[t]], [PS[PJ]]))
                    ops_.append(lambda dst=dst, dbuf=dbuf, sl=sl, t=t: cp("dve", dst[:, sl], ps[:, PJ, :], [PS[PJ]], [dbuf[t]]))
                for j in range(4):
                    blk = t * 4 + j
                    for k in range(8):
                        ops_.append(lambda k=k, j=j, blk=blk, t=t: mm(
                            ps[:, PJ, j * 128:(j + 1) * 128], xT[:, k, blk * 128:(blk + 1) * 128], w[:, k, 256:384],
                            k == 0, k == 7, [XT[t], W], [PS[PJ]]))
                ops_.append(lambda t=t: cp("dve", VBs[si][:, t * 4:(t + 1) * 4, :],
                                          ps[:, PJ, :].rearrange("p (j d) -> p j d", j=4), [PS[PJ]], [VBbs[si][t]]))
            return ops_

        wB_load(0)
        for f_ in proj_ops(0):
            f_()
        for hp in range(4):
            QTB, KTB, VB = QTBs[hp % 2], KTBs[hp % 2], VBs[hp % 2]
            QBb, KBb, VBb = QBbs[hp % 2], KBbs[hp % 2], VBbs[hp % 2]
            nxt = []
            if hp < 3:
                wB_load(hp + 1)
                nxt = proj_ops(hp + 1)

            steps = []
            for qt in range(NT):
                acur = None
                kbs = list(range(4 * qt + 3, -1, -1))
                for si, kb in enumerate(kbs):
                    off = max(0, kb - 4 * qt) * 128
                    st_ = dict(qt=qt, kb=kb, off=off, N=512 - off, t0=qt * 512 + off, diag=kb >= 4 * qt,
                               first=si == 0, last=kb == 0, ob=6, acur=acur)
                    if kb != 0:
                        st_["anew"] = 0 if si == 0 else 1 - acur
                        acur = st_["anew"]
                    steps.append(st_)
            for n_, st_ in enumerate(steps):
                st_["zp"] = 2 * (n_ % 3)
                st_["e"] = n_ % NE
                st_["l"] = n_ % NL
                st_["a"] = n_ % 2

            def s_z(S):
                off, zp, kb, t0, N = S["off"], S["zp"], S["kb"], S["t0"], S["N"]
                for h in range(2):
                    r = slice(64 * h, 64 * h + 64)
                    mm(ps[:, zp + h, off:512], KTB[r, kb * 128:(kb + 1) * 128], QTB[r, t0:t0 + N], True, False,
                       [KBb[kb // 4], QBb[S["qt"]]], [PS[zp + h]])

            def s_el(S):
                off, zp, e_i, l_i = S["off"], S["zp"], S["e"], S["l"]
                Zb = [PS[zp], PS[zp + 1]]
                act(Et[e_i][:, :, off:512], ps[:, zp:zp + 2, off:512], AF.Exp, Zb, [ET[e_i]], scale=0.125)
                act(Lt[l_i][:, :, off:512], Et[e_i][:, :, off:512], AF.Ln, [ET[e_i]], [LT[l_i]], bias=1.0)
                if S["diag"]:
                    tt("dve", Lt[l_i][:, :, off:off + 128], Lt[l_i][:, :, off:off + 128], msb2, ALU.mult,
                       [LT[l_i], CB], [LT[l_i]])
                if not S["last"]:
                    anew, acur = S["anew"], S["acur"]
                    if S["first"]:
                        ms(Ac[anew][:, :, :], 0.0, [AC[anew]])
                        cp("dve", Ac[anew][:, :, off:512], Lt[l_i][:, :, off:512], [LT[l_i]], [AC[anew]])
                    else:
                        if off > 0:
                            ms(Ac[anew][:, :, :], 0.0, [AC[anew]])
                        tt("dve", Ac[anew][:, :, off:512], Ac[acur][:, :, off:512], Lt[l_i][:, :, off:512], ALU.add,
                           [AC[acur], LT[l_i]], [AC[anew]])

            def s_tri(S):
                off, zp, l_i = S["off"], S["zp"], S["l"]
                for h in range(2):
                    mm(ps[:, zp + h, off:512], ntri, Lt[l_i][:, h, off:512], False, S["first"],
                       [CB, LT[l_i]], [PS[zp + h]])
                    if not S["first"]:
                        mm(ps[:, zp + h, off:512], none_, Ac[S["acur"]][:, h, off:512], False, True,
                           [CB, AC[S["acur"]]], [PS[zp + h]])

            def s_a(S):
                off, zp, a_i = S["off"], S["zp"], S["a"]
                Zb = [PS[zp], PS[zp + 1]]
                act(At[a_i][:, :, off:512], ps[:, zp:zp + 2, off:512], AF.Exp, Zb, [AT[a_i]], scale=0.125)
                if S["diag"]:
                    tt("dve", At[a_i][:, :, off:off + 128], At[a_i][:, :, off:off + 128], msb2, ALU.mult,
                       [AT[a_i], CB], [AT[a_i]])

            def s_av(S):
                off, a_i, ob, kb = S["off"], S["a"], S["ob"], S["kb"]
                for h in range(2):
                    r = slice(64 * h, 64 * h + 64)
                    mm(ps[r, ob, off:512], VB[:, kb, r], At[a_i][:, h, off:512], S["first"], S["last"],
                       [VBb[kb // 4], AT[a_i]], [PS[ob]])
                if S["last"]:
                    qt = S["qt"]
                    cp("dve", yBT[:, hp, qt * 512:(qt + 1) * 512], ps[:, ob, :], [PS[ob]], [YB[qt]])

            ns = len(steps)
            for tau in range(ns + 2):
                rounds_left = max(1, ns - 4 - tau)
                take = -(-len(nxt) // rounds_left) if tau < ns - 4 else len(nxt)
                for f_ in nxt[:take]:
                    f_()
                nxt = nxt[take:]
                if tau < ns:
                    s_z(steps[tau])
                if 1 <= tau <= ns:
                    s_tri(steps[tau - 1])
                if 2 <= tau:
                    s_av(steps[tau - 2])
                if tau < ns:
                    s_el(steps[tau])
                if 1 <= tau <= ns:
                    s_a(steps[tau - 1])
        P.barrier()

        if debug:
            o = LOC
            dtmp = M.at(o, [128, 4, 512], F32)
            DT = Buf("dtmp")
            for t in range(NT):
                sl = slice(t * 512, (t + 1) * 512)
                cp("dve", dtmp[:, :, :], yAT[:, :, sl], [YA[t]], [DT])
                dma("sp", dbgA[:, :, sl], dtmp[:, :, :], reads=[DT], key=("dbg", 0))
                cp("dve", dtmp[:, :, :], yBT[:, :, sl], [YB[t]], [DT])
                dma("sp", dbgB[:, :, sl], dtmp[:, :, :], reads=[DT], key=("dbg", 0))
            P.barrier()

        o = LOC
        wa = M.at(o, [128, 4, 1024], BF16); o += 8192
        wb = M.at(o, [128, 4, 1024], BF16); o += 8192
        wo = M.at(o, [128, 8, 1024], BF16); o += 16384
        WAa, WBb, WO = Buf("wa"), Buf("wb"), Buf("wo")
        g1 = M.at(o, [128, D], F32); o += 4096
        b1 = M.at(o, [128, D], F32); o += 4096
        LN1 = Buf("ln1")
        bg = M.at(o, [128, 16], F32); o += 64
        BG = Buf("bg")
        epsb = M.at(o, [128, 1], F32); o += 32
        mhalf = M.at(o, [128, 1], F32); o += 32
        CC = Buf("cc")
        wgc = [M.at(o + i * 4096, [128, 8, 256], BF16) for i in range(2)]; o += 8192
        WG = [Buf("wg%d" % i) for i in range(2)]
        gt = [M.at(o, [128, 2, 512], F32)] * 2; o += 4096
        GT = [Buf("gt")] * 2
        hT = M.at(o, [128, 8, 512], BF16); o += 8192
        HT = [Buf("hT%d" % c) for c in range(8)]
        xin = [M.at(o + i * 4096, [128, D], F32) for i in range(2)]; o += 8192
        XIN = [Buf("xin%d" % i) for i in range(2)]
        x1b = [M.at(o + i * 2048, [128, D], BF16) for i in range(4)]; o += 8192
        X1B = [Buf("x1b%d" % i) for i in range(4)]
        st = [M.at(o + i * 64, [128, 2, 6], F32) for i in range(4)]; o += 256
        mv = [M.at(o + i * 32, [128, 2], F32) for i in range(4)]; o += 128
        rs = [M.at(o + i * 32, [128, 1], F32) for i in range(4)]; o += 128
        nbt = [M.at(o + i * 32, [128, 1], F32) for i in range(4)]; o += 128
        NBT = [Buf("nb%d" % i) for i in range(4)]
        STt = [Buf("st%d" % i) for i in range(4)]
        MV = [Buf("mv%d" % i) for i in range(4)]
        RS = [Buf("rs%d" % i) for i in range(4)]
        assert o <= 212992, o

        for k in range(0, 4, 2):
            dma("pool", wa[:, k:k + 2, :], wa_h[:, k:k + 2, :], writes=[WAa])
            dma("pool", wb[:, k:k + 2, :], wb_h[:, k:k + 2, :], writes=[WBb])
        for k in range(0, 8, 2):
            dma("pool", wo[:, k:k + 2, :], wo_h[:, k:k + 2, :], writes=[WO])
        dma("sp", g1[:, :], ln_h[0], writes=[LN1])
        dma("sp", b1[:, :], ln_h[1], writes=[LN1])
        dma("sp", bg[:, :], bg_h[:, :], writes=[BG])
        ms(epsb[:, :], EPS, [CC])
        ms(mhalf[:, :], -0.5, [CC])

        def layer_norm(yv, Y, gam, bet, LNB, out_ap, OUT, st_, mv_, rs_, ST_, MV_, RS_, eps_, mh_, CC_, nb_, NB_):
            for hh in range(2):
                P.op("dve", lambda e, o_=st_[:, hh, :], i_=yv[:, hh * 512:(hh + 1) * 512]: e.bn_stats(out=o_, in_=i_),
                     reads=[Y], writes=[ST_], multi=True)
            P.op("dve", lambda e, o_=mv_[:, :], i_=st_[:, :, :]: e.bn_aggr(out=o_, in_=i_), reads=[ST_], writes=[MV_])
            ts("pool", rs_[:, :], mv_[:, 1:2], eps_[:, 0:1], None, ALU.add, ALU.bypass, [MV_, CC_], [RS_])
            tt("pool", rs_[:, :], rs_[:, :], mh_[:, :], ALU.pow, [RS_, CC_], [RS_])
            stt(nb_[:, :], mv_[:, 0:1], -1.0, rs_[:, :], ALU.mult, ALU.mult, [MV_, RS_], [NB_])
            act(yv, yv, AF.Identity, [Y, RS_, NB_], [Y], bias=nb_[:, 0:1], scale=rs_[:, 0:1])
            tt("dve", yv, yv, gam, ALU.mult, [Y, LNB], [Y])
            tt("dve", out_ap, yv, bet, ALU.add, [Y, LNB], [OUT])

        def c1_R(t, j):
            rb = 4 + 2 * (j % 2)
            for hf in range(2):
                for c in range(8):
                    mm(ps[:, rb + hf, :], hT[:, c, j * 128:(j + 1) * 128], wo[:, c, hf * 512:(hf + 1) * 512],
                       c == 0, c == 7, [HT[c], WO], [PS[rb + hf]])

        def c1_R(t, j):
            rb = 4 + 2 * (j % 2)
            for hf in range(2):
                for c in range(8):
                    mm(ps[:, rb + hf, :], hT[:, c, j * 128:(j + 1) * 128], wo[:, c, hf * 512:(hf + 1) * 512],
                       c == 0, c == 7, [HT[c], WO], [PS[rb + hf]])

        def c1_tail(t):
            def xload(j):
                blk = t * 4 + j
                dma("sp", xin[j % 2][:, :], x_h[blk * 128:(blk + 1) * 128, :], writes=[XIN[j % 2]], key=("xin", j % 2))

            def evac(j):
                rb = 4 + 2 * (j % 2)
                stt(xin[j % 2][:, :], xin[j % 2][:, :], ALPHA, ps[:, rb:rb + 2, :].rearrange("p a n -> p (a n)"),
                    ALU.mult, ALU.add, [XIN[j % 2], PS[rb], PS[rb + 1]], [XIN[j % 2]])

            xload(0)
            xload(1)
            c1_R(t, 0)
            c1_R(t, 1)
            evac(0)
            evac(1)
            c1_R(t, 2)
            c1_R(t, 3)
            L = []

            def ln_stages(js):
                its = [dict(yv=xin[j % 2][:, :], Y=XIN[j % 2], st=st[j], mv=mv[j], rs=rs[j], nb=nbt[j],
                            ST=STt[j], MV=MV[j], RS=RS[j], NB=NBT[j], j=j, blk=t * 4 + j) for j in js]
                for it_ in its:
                    for hh in range(2):
                        L.append(lambda it_=it_, hh=hh: P.op(
                            "dve", lambda e, o_=it_["st"][:, hh, :], i_=it_["yv"][:, hh * 512:(hh + 1) * 512]:
                            e.bn_stats(out=o_, in_=i_), reads=[it_["Y"]], writes=[it_["ST"]], multi=True))
                for it_ in its:
                    L.append(lambda it_=it_: P.op("dve", lambda e, o_=it_["mv"][:, :], i_=it_["st"][:, :, :]:
                                                  e.bn_aggr(out=o_, in_=i_), reads=[it_["ST"]], writes=[it_["MV"]]))
                for it_ in its:
                    L.append(lambda it_=it_: ts("pool", it_["rs"][:, :], it_["mv"][:, 1:2], epsb[:, 0:1], None,
                                                ALU.add, ALU.bypass, [it_["MV"], CC], [it_["RS"]]))
                for it_ in its:
                    L.append(lambda it_=it_: tt("pool", it_["rs"][:, :], it_["rs"][:, :], mhalf[:, :], ALU.pow,
                                                [it_["RS"], CC], [it_["RS"]]))
                for it_ in its:
                    L.append(lambda it_=it_: stt(it_["nb"][:, :], it_["mv"][:, 0:1], -1.0, it_["rs"][:, :],
                                                 ALU.mult, ALU.mult, [it_["MV"], it_["RS"]], [it_["NB"]]))
                for it_ in its:
                    L.append(lambda it_=it_: act(it_["yv"], it_["yv"], AF.Identity, [it_["Y"], it_["RS"], it_["NB"]],
                                                 [it_["Y"]], bias=it_["nb"][:, 0:1], scale=it_["rs"][:, 0:1]))
                for it_ in its:
                    L.append(lambda it_=it_: tt("dve", it_["yv"], it_["yv"], g1[:, :], ALU.mult, [it_["Y"], LN1], [it_["Y"]]))
                for it_ in its:
                    L.append(lambda it_=it_: tt("dve", it_["yv"], it_["yv"], b1[:, :], ALU.add, [it_["Y"], LN1], [it_["Y"]]))
                for it_ in its:
                    j, blk = it_["j"], it_["blk"]
                    L.append(lambda j=j, blk=blk: dma("sp", x1_s[blk * 128:(blk + 1) * 128, :], xin[j % 2][:, :],
                                                       reads=[XIN[j % 2]], key=("x1s", j % 2)))
                    L.append(lambda j=j: cp("act", x1b[j][:, :], xin[j % 2][:, :], [XIN[j % 2]], [X1B[j]]))

            def t_stages(j):
                blk = t * 4 + j
                tb = 4 + j
                tpb = ps[:, tb, :].bitcast(BF16)
                for c in range(8):
                    L.append(lambda c=c: P.op(
                        "pe", lambda e, o_=tpb[:, c * 128:(c + 1) * 128], i_=x1b[j][:, c * 128:(c + 1) * 128],
                        id_=cb[:, C_ID:C_ID + 128]: e.transpose(out=o_, in_=i_, identity=id_),
                        reads=[X1B[j], CB], writes=[PS[tb]]))
                L.append(lambda: cp("act", xT[:, :, blk * 128:(blk + 1) * 128], tpb.rearrange("p (c n) -> p c n", c=8),
                                    [PS[tb]], [XT[t]]))

            ln_stages((0, 1))
            L.append(lambda: xload(2))
            L.append(lambda: xload(3))
            L.append(lambda: evac(2))
            L.append(lambda: evac(3))
            t_stages(0)
            t_stages(1)
            ln_stages((2, 3))
            t_stages(2)
            t_stages(3)
            return L

        gi = 0
        bi = 0
        pend1 = []

        def wg_load(n):
            if n < NT * 8:
                dma("pool", wgc[n % 2][:, :, :], wg_h[n % 8, :, :, :], writes=[WG[n % 2]], key=("wg", n % 2))
        wg_load(0)
        for t in range(NT):
            sl = slice(t * 512, (t + 1) * 512)
            for c in range(8):
                wi = gi % 2
                gi2 = gi % 2
                gi += 1
                gb = 0
                for ab in range(2):
                    for k in range(8):
                        mm(ps[:, gb + ab, :], wgc[wi][:, k, ab * 128:(ab + 1) * 128], xT[:, k, sl], k == 0, k == 7,
                           [WG[wi], XT[t]], [PS[gb + ab]])
                    act(gt[gi2][:, ab, :], ps[:, gb + ab, :], AF.Sigmoid, [PS[gb + ab], BG], [GT[gi2]],
                        bias=bg[:, ab * 8 + c:ab * 8 + c + 1])
                wg_load(gi)
                for k in range(4):
                    mm(ps[:, 2, :], wa[:, k, c * 128:(c + 1) * 128], yAT[:, k, sl], k == 0, k == 3, [WAa, YA[t]], [PS[2]])
                for k in range(4):
                    mm(ps[:, 3, :], wb[:, k, c * 128:(c + 1) * 128], yBT[:, k, sl], k == 0, k == 3, [WBb, YB[t]], [PS[3]])
                tt("dve", gt[gi2][:, 0, :], gt[gi2][:, 0, :], ps[:, 2, :], ALU.mult, [GT[gi2], PS[2]], [GT[gi2]])
                tt("dve", gt[gi2][:, 1, :], gt[gi2][:, 1, :], ps[:, 3, :], ALU.mult, [GT[gi2], PS[3]], [GT[gi2]])
                take = -(-len(pend1) // (8 - c))
                for f_ in pend1[:take]:
                    f_()
                del pend1[:take]
                tt("dve", hT[:, c, :], gt[gi2][:, 0, :], gt[gi2][:, 1, :], ALU.add, [GT[gi2]], [HT[c]])
            assert not pend1
            pend1.extend(c1_tail(t))
        for f_ in pend1:
            f_()
        del pend1[:]
        P.barrier()

        o = LOC2
        TQ = min(1024, T)
        NQt = T // TQ
        HPQ = TQ // 512
        wdn = M.at(o, [128, NJ, 1024], BF16); o += NJ * 2048
        WDN = Buf("wdn")
        hid = M.at(o, [128, NJ, TQ], BF16); o += NJ * TQ * 2
        HID = [[Buf("hid%d_%d" % (j, hh)) for hh in range(HPQ)] for j in range(NJ)]
        g2 = M.at(o, [128, D], F32); o += 4096
        b2 = M.at(o, [128, D], F32); o += 4096
        LN2 = Buf("ln2")
        cpm = M.at(o, [128, 2 * NJ, 4], F32); o += 2 * NJ * 16
        CPM = Buf("cpm")
        halo = M.at(o, [128, 2 * NJ, 2], F32); o += 2 * NJ * 8
        HALO = [Buf("halo%d" % jj) for jj in range(2 * NJ)]
        o = (o + 31) // 32 * 32
        wupc = [M.at(o + i * 4096, [128, 8, 256], BF16) for i in range(3)]; o += 12288
        WUP = [Buf("wup%d" % i) for i in range(3)]
        U = [[M.at(o + (i * 2 + g) * 2080, [128, 514], F32) for g in range(2)] for i in range(2)]; o += 4 * 2080
        UB = [[Buf("U%d_%d" % (i, g)) for g in range(2)] for i in range(2)]
        Aa = [[M.at(o + (i * 2 + g) * 2048, [128, 512], F32) for g in range(2)] for i in range(3)]; o += 6 * 2048
        AB = [[Buf("A%d_%d" % (i, g)) for g in range(2)] for i in range(3)]
        xin = [M.at(o + i * 4096, [128, D], F32) for i in range(2)]; o += 8192
        XIN = [Buf("xin2_%d" % i) for i in range(2)]
        st = [M.at(o + i * 64, [128, 2, 6], F32) for i in range(2)]; o += 128
        mv = [M.at(o + i * 32, [128, 2], F32) for i in range(2)]; o += 64
        rs = [M.at(o + i * 32, [128, 1], F32) for i in range(2)]; o += 64
        nbt = [M.at(o + i * 32, [128, 1], F32) for i in range(2)]; o += 64
        NBT = [Buf("nb2%d" % i) for i in range(2)]
        epsb = M.at(o, [128, 1], F32); o += 32
        mhalf = M.at(o, [128, 1], F32); o += 32
        STt = [Buf("st2%d" % i) for i in range(2)]
        MV = [Buf("mv2%d" % i) for i in range(2)]
        RS = [Buf("rs2%d" % i) for i in range(2)]
        CC = Buf("cc2")
        assert o <= 212992, o

        for j in range(0, NJ, 2):
            dma("pool", wdn[:, j:j + 2, :], wdn_h[:, j:j + 2, :], writes=[WDN])
        dma("sp", g2[:, :], ln_h[2], writes=[LN2])
        dma("sp", b2[:, :], ln_h[3], writes=[LN2])
        dma("sp", cpm[:, :, :], cp_h[:, :, :], writes=[CPM])
        ms(epsb[:, :], EPS, [CC])
        ms(mhalf[:, :], -0.5, [CC])
        ms(halo[:, :, :], 0.0, HALO)

        ui = 0
        wi_ = 0
        bi = 0
        wseq = [(q, j) for q in range(NQt) for j in range(NJ)]

        def wup_load(n):
            if n < len(wseq):
                dma("pool", wupc[n % 3][:, :, :], wup_h[wseq[n][1], :, :, :], writes=[WUP[n % 3]], key=("wup", n % 3))
        wup_load(0)
        wup_load(1)
        def ln_front(it_):
            for hh in range(2):
                P.op("dve", lambda e, o_=it_["st"][:, hh, :], i_=it_["yv"][:, hh * 512:(hh + 1) * 512]:
                     e.bn_stats(out=o_, in_=i_), reads=[it_["Y"]], writes=[it_["ST"]], multi=True)
            P.op("dve", lambda e, o_=it_["mv"][:, :], i_=it_["st"][:, :, :]: e.bn_aggr(out=o_, in_=i_),
                 reads=[it_["ST"]], writes=[it_["MV"]])
            ts("pool", it_["rs"][:, :], it_["mv"][:, 1:2], epsb[:, 0:1], None, ALU.add, ALU.bypass,
               [it_["MV"], CC], [it_["RS"]])
            tt("pool", it_["rs"][:, :], it_["rs"][:, :], mhalf[:, :], ALU.pow, [it_["RS"], CC], [it_["RS"]])
            stt(it_["nb"][:, :], it_["mv"][:, 0:1], -1.0, it_["rs"][:, :], ALU.mult, ALU.mult,
                [it_["MV"], it_["RS"]], [it_["NB"]])
            act(it_["yv"], it_["yv"], AF.Identity, [it_["Y"], it_["RS"], it_["NB"]], [it_["Y"]],
                bias=it_["nb"][:, 0:1], scale=it_["rs"][:, 0:1])

        def ln_back(it_, gam, bet, LNB):
            tt("dve", it_["yv"], it_["yv"], gam, ALU.mult, [it_["Y"], LNB], [it_["Y"]])
            tt("dve", it_["yv"], it_["yv"], bet, ALU.add, [it_["Y"], LNB], [it_["Y"]])

        pend = [None]

        def flush_pend():
            if pend[0] is not None:
                o_, a_, b_, rd, wr = pend[0]
                act(a_, a_, AF.Silu, [rd[0]], [rd[0]])
                tt("dve", o_, a_, b_, ALU.mult, rd, wr)
                pend[0] = None

        def down_front(q, jb):
            blk = q * (TQ // 128) + jb
            i = blk % 2
            rb = 4 + 2 * i
            dma("sp", xin[i][:, :], x1_s[blk * 128:(blk + 1) * 128, :], writes=[XIN[i]], key=("xin", i))
            for hf in range(2):
                for j in range(NJ):
                    mm(ps[:, rb + hf, :], hid[:, j, jb * 128:(jb + 1) * 128], wdn[:, j, hf * 512:(hf + 1) * 512],
                       j == 0, j == NJ - 1, [HID[j][jb // 4], WDN], [PS[rb + hf]])
            stt(xin[i][:, :], xin[i][:, :], ALPHA, ps[:, rb:rb + 2, :].rearrange("p a n -> p (a n)"),
                ALU.mult, ALU.add, [XIN[i], PS[rb], PS[rb + 1]], [XIN[i]])
            it_ = dict(yv=xin[i][:, :], Y=XIN[i], st=st[i], mv=mv[i], rs=rs[i], nb=nbt[i],
                       ST=STt[i], MV=MV[i], RS=RS[i], NB=NBT[i], blk=blk, i=i)
            ln_front(it_)
            return it_

        def down_back(it_):
            ln_back(it_, g2[:, :], b2[:, :], LN2)
            blk, i = it_["blk"], it_["i"]
            dma("sp", out_h[blk * 128:(blk + 1) * 128, :], xin[i][:, :], reads=[XIN[i]], key=("out", i))

        for q in range(NQt):
            for j in range(NJ):
                wi = wi_ % 3
                wup_load(wi_ + 2)
                wi_ += 1
                for hh in range(HPQ):
                    tI = q * HPQ + hh
                    sl = slice(tI * 512, (tI + 1) * 512)
                    i = ui % 2
                    ia = ui % 3
                    ui += 1
                    for g in range(2):
                        jj = g * NJ + j
                        b = 2 * i + g
                        for k in range(8):
                            mm(ps[:, b, :], wupc[wi][:, k, g * 128:(g + 1) * 128], xT[:, k, sl], k == 0, k == 7,
                               [WUP[wi], XT[tI]], [PS[b]])
                        cp("pool", U[i][g][:, 0:2], halo[:, jj, :], [HALO[jj]], [UB[i][g]])
                        P.op("act", lambda e, o_=U[i][g][:, 2:514], i_=ps[:, b, :]: e.activation(out=o_, in_=i_, func=AF.Copy),
                             reads=[PS[b]], writes=[UB[i][g]], multi=True)
                        act(Aa[ia][g][:, :], ps[:, b, :], AF.Identity, [PS[b], CPM], [AB[ia][g]],
                            bias=cpm[:, jj, 3:4], scale=cpm[:, jj, 2:3])
                        cp("pool", halo[:, jj, :], U[i][g][:, 512:514], [UB[i][g]], [HALO[jj]])
                        stt(Aa[ia][g][:, :], U[i][g][:, 1:513], cpm[:, jj, 1:2], Aa[ia][g][:, :], ALU.mult, ALU.add,
                            [UB[i][g], CPM, AB[ia][g]], [AB[ia][g]])
                        stt(Aa[ia][g][:, :], U[i][g][:, 0:512], cpm[:, jj, 0:1], Aa[ia][g][:, :], ALU.mult, ALU.add,
                            [UB[i][g], CPM, AB[ia][g]], [AB[ia][g]])
                    flush_pend()
                    pend[0] = (hid[:, j, hh * 512:(hh + 1) * 512], Aa[ia][0][:, :], Aa[ia][1][:, :],
                               [AB[ia][0], AB[ia][1]], [HID[j][hh]])
            flush_pend()
            prev = None
            for jb in range(TQ // 128):
                cur = down_front(q, jb)
                if prev is not None:
                    down_back(prev)
                prev = cur
            down_back(prev)
        P.barrier()
        P.emit(nc, stack)
    return nc


def host_prep(b, x, positions, w_in, b_gate, sinks, w_branch_a, w_branch_b, w_out,
              ln1_g, ln1_b, w_up, conv_w, conv_b, w_down, ln2_g, ln2_b, shared):
    m = dict(shared)
    m["xT"] = np.ascontiguousarray(x[b].T)
    m["x"] = np.ascontiguousarray(x[b])
    m["pos"] = np.ascontiguousarray(np.broadcast_to(positions[b][None, :].astype(np.int32), (128, positions.shape[1])))
    return m


def host_shared(w_in, b_gate, sinks, w_branch_a, w_branch_b, w_out,
                ln1_g, ln1_b, w_up, conv_w, conv_b, w_down, ln2_g, ln2_b):
    f = np.float32
    w_in = w_in[0]
    cbm = np.zeros((128, NCB), f)
    cbm[:, C_ID:C_ID + 128] = np.eye(128, dtype=f)
    rot = np.zeros((128, 128), f)
    for m_ in range(128):
        base, ml = (m_ // 64) * 64, m_ % 64
        if ml < 32:
            rot[base + ml + 32, m_] = -1.0
        else:
            rot[base + ml - 32, m_] = 1.0
    cbm[:, C_ROT:C_ROT + 128] = rot
    jj, ss = np.meshgrid(np.arange(128), np.arange(128), indexing="ij")
    cbm[:, C_TRI:C_TRI + 128] = np.where(jj >= ss, -8.0, 0.0)
    cbm[:, C_ONE:C_ONE + 128] = -8.0
    s_, t_ = np.meshgrid(np.arange(128), np.arange(128), indexing="ij")
    msb = (s_ < t_).astype(f)
    cbm[:, C_MSB:C_MSB + 128] = msb
    cbm[:, C_MSB + 128:C_MSB + 256] = msb
    mswa = np.concatenate([(s_ <= t_).astype(f), (s_ > t_).astype(f)], axis=1)
    cbm[:, C_MSWA:C_MSWA + 256] = mswa
    cbm[:, C_MSWA + 256:C_MSWA + 512] = mswa
    cbm[:, C_O64:C_O64 + 64] = 1.0
    cfm = np.zeros((128, 8), f)
    inv = (f(1.0) / np.power(f(10000.0), np.arange(0, 64, 2, dtype=f) / f(64.0))).astype(f)
    cfm[:, 0] = inv[np.arange(128) % 32]
    def kp(w):
        K, N = w.shape
        return np.ascontiguousarray(w.reshape(K // 128, 128, N).transpose(1, 0, 2))
    qa_cols = np.concatenate([np.r_[c * 64:(c + 1) * 64, (4 + c) * 64:(5 + c) * 64] for c in range(4)])
    wA = np.concatenate([w_in[:, 0:512][:, qa_cols], w_in[:, 512:640], w_in[:, 640:768]], axis=1)
    wB = np.stack([np.concatenate([w_in[:, 768 + hp * 128:768 + (hp + 1) * 128],
                                   w_in[:, 1280 + hp * 128:1280 + (hp + 1) * 128],
                                   w_in[:, 1792 + hp * 128:1792 + (hp + 1) * 128]], axis=1) for hp in range(4)])
    wgA, wgB = w_in[:, 2304:3328], w_in[:, 3328:4352]
    wg = np.stack([np.concatenate([wgA[:, c * 128:(c + 1) * 128], wgB[:, c * 128:(c + 1) * 128]], axis=1)
                   for c in range(8)])
    frow = np.array([[(c if p < 64 else 4 + c) * 64 + p % 64 for c in range(4)] for p in range(128)])
    wa = np.ascontiguousarray(w_branch_a[0][frow, :])
    wup = w_up[0]
    wupr = np.stack([np.concatenate([wup[:, j * 128:(j + 1) * 128], wup[:, DFF + j * 128:DFF + (j + 1) * 128]], axis=1)
                     for j in range(NJ)])
    cpm = np.zeros((128, 2 * NJ, 4), f)
    cw, cbias = conv_w[0], conv_b[0]
    for jj_ in range(2 * NJ):
        ch = jj_ * 128 + np.arange(128)
        cpm[:, jj_, 0:3] = cw[:, ch].T
        cpm[:, jj_, 3] = cbias[ch]
    bgm = np.zeros((128, 16), f)
    for c in range(8):
        bgm[:, c] = b_gate[0][c * 128:(c + 1) * 128]
        bgm[:, 8 + c] = b_gate[0][1024 + c * 128:1024 + (c + 1) * 128]
    skm = np.zeros((128, 4), f)
    for c in range(4):
        skm[0:64, c] = sinks[0][c]
        skm[64:128, c] = sinks[0][4 + c]
    lnm = np.stack([np.broadcast_to(v[0][None, :], (128, D)) for v in (ln1_g, ln1_b, ln2_g, ln2_b)]).astype(f)
    return dict(
        cb=cbm, cf=cfm,
        wA=kp(wA), wB=np.stack([kp(wB[hp]) for hp in range(4)]),
        wg=np.stack([kp(wg[c]) for c in range(8)]),
        wa=wa, wb=kp(w_branch_b[0]), wo=kp(w_out[0]),
        wup=np.stack([kp(wupr[j]) for j in range(NJ)]), wdn=kp(w_down[0]),
        cp=cpm, bg=bgm, sk=skm, ln=np.ascontiguousarray(lnm),
    )


_NC_CACHE = {}


def kernel(x, positions, w_in, b_gate, sinks, w_branch_a, w_branch_b, w_out,
           ln1_g, ln1_b, w_up, conv_w, conv_b, w_down, ln2_g, ln2_b):
    args = [np.asarray(a) for a in (x, positions, w_in, b_gate, sinks, w_branch_a, w_branch_b, w_out,
                                    ln1_g, ln1_b, w_up, conv_w, conv_b, w_down, ln2_g, ln2_b)]
    x, positions = args[0], args[1]
    B, T, _ = x.shape
    shared = host_shared(*args[2:])
    in_maps = [host_prep(b, x, positions, *args[2:], shared) for b in range(B)]
    nc = build(T)
    res = run_bass_kernel_spmd(nc, in_maps, core_ids=list(range(B)))
    return np.stack([np.asarray(r["out"]) for r in res.results]).astype(np.float32)
```
